# Optimizing a Trainium2 kernel written in Bass

```python
import math
import jax, jax.numpy as jnp
from jax import lax
import numpy as np

D_MODEL = 1024
BATCH = 8
SEQ = 2048
DEPTH = 1

N_HEADS = 8
QK_NOPE_DIM = 128
QK_ROPE_DIM = 64
V_HEAD_DIM = 128
Q_LORA_RANK = 384
KV_LORA_RANK = 256
ROPE_THETA = 10000.0
Q_BLOCK = 128
POOL_WINDOWS = (2, 4, 8, 16)
POOL_GROUP = 128
POOL_WIDTH = POOL_GROUP * len(POOL_WINDOWS)
D_FF = 2816
MACARON_WEIGHT = 0.5
N_BRANCHES = 2
NORM_EPS = 1e-6
IN_OFFSETS = [
    Q_LORA_RANK,
    Q_LORA_RANK + KV_LORA_RANK,
    Q_LORA_RANK + KV_LORA_RANK + QK_ROPE_DIM,
    Q_LORA_RANK + KV_LORA_RANK + QK_ROPE_DIM + POOL_WIDTH,
]
IN_WIDTH = Q_LORA_RANK + KV_LORA_RANK + QK_ROPE_DIM + POOL_WIDTH + N_BRANCHES * D_MODEL

kernel_name = "hybrid_mla_pool_macaron_block"


def _rmsnorm(x, g):
    xf = x.astype(jnp.float32)
    y = xf * lax.rsqrt(jnp.mean(xf * xf, axis=-1, keepdims=True) + NORM_EPS)
    return (y * g.astype(jnp.float32)).astype(x.dtype)


def _swiglu(x, w_gate, w_up, w_down):
    return (jax.nn.silu(x @ w_gate) * (x @ w_up)) @ w_down


def _rope(x, cos, sin):
    half = x.shape[-1] // 2
    x1, x2 = x[..., :half], x[..., half:]
    return jnp.concatenate([x1 * cos - x2 * sin, x2 * cos + x1 * sin], axis=-1)


def _mla(c_q, c_kv, k_r, positions, q_a_norm_g, w_uq, kv_a_norm_g, w_uk, w_uv):
    b, s, _ = c_q.shape
    q = (_rmsnorm(c_q, q_a_norm_g) @ w_uq).reshape(b, s, N_HEADS, QK_NOPE_DIM + QK_ROPE_DIM)
    q_nope, q_rope = q[..., :QK_NOPE_DIM], q[..., QK_NOPE_DIM:]
    c_kv = _rmsnorm(c_kv, kv_a_norm_g)
    k_nope = (c_kv @ w_uk).reshape(b, s, N_HEADS, QK_NOPE_DIM)
    v = (c_kv @ w_uv).reshape(b, s, N_HEADS, V_HEAD_DIM)
    inv_freq = ROPE_THETA ** (-jnp.arange(0, QK_ROPE_DIM, 2, dtype=jnp.float32) / QK_ROPE_DIM)
    ang = positions.astype(jnp.float32)[..., None] * inv_freq
    cos = jnp.cos(ang).astype(c_q.dtype)
    sin = jnp.sin(ang).astype(c_q.dtype)
    k_rope = _rope(k_r, cos, sin)
    q_rope = _rope(q_rope, cos[:, :, None, :], sin[:, :, None, :])
    scale = 1.0 / math.sqrt(QK_NOPE_DIM + QK_ROPE_DIM)
    nb = s // Q_BLOCK

    def to_blocks(t):
        return t.reshape(b, nb, Q_BLOCK, *t.shape[2:]).swapaxes(0, 1)

    def attend(qs):
        qn, qr = qs
        logits = (jnp.einsum('bqhd,bkhd->bhqk', qn, k_nope)
                  + jnp.einsum('bqhd,bkd->bhqk', qr, k_rope))
        p = jax.nn.softmax(logits.astype(jnp.float32) * scale, axis=-1).astype(v.dtype)
        return jnp.einsum('bhqk,bkhd->bqhd', p, v)

    o = lax.map(attend, (to_blocks(q_nope), to_blocks(q_rope)))
    return o.swapaxes(0, 1).reshape(b, s, N_HEADS * V_HEAD_DIM)


def _centred_mean(xg, window):
    s = xg.shape[1]
    xf = xg.astype(jnp.float32)
    csum = jnp.concatenate([jnp.zeros_like(xf[:, :1]), jnp.cumsum(xf, axis=1)], axis=1)
    left = window // 2
    right = window - 1 - left
    t = jnp.arange(s)
    lo = jnp.clip(t - left, 0, s)
    hi = jnp.clip(t + right + 1, 0, s)
    total = jnp.take(csum, hi, axis=1) - jnp.take(csum, lo, axis=1)
    count = (hi - lo).astype(jnp.float32)
    return (total / count[None, :, None]).astype(xg.dtype)


def _pool_mixer(xp, pool_w, pool_scale):
    groups = []
    for gi, w in enumerate(POOL_WINDOWS):
        xg = xp[..., gi * POOL_GROUP:(gi + 1) * POOL_GROUP]
        groups.append(_centred_mean(xg, w) - xg)
    d = jnp.stack(groups, axis=2)
    y = jnp.einsum('bsgc,gcd->bsgd', d, pool_w)
    return y.reshape(xp.shape[0], xp.shape[1], POOL_WIDTH) * pool_scale


def setup_inputs(seed: int = 0) -> dict:
    key = jax.random.key(seed)
    ks = list(jax.random.split(key, 32))

    def dense(k, shape, fan_in):
        return jax.random.normal(k, shape, jnp.float32) * (fan_in ** -0.5)

    def gain(k, shape):
        return 1.0 + 0.05 * jax.random.normal(k, shape, jnp.float32)

    L = DEPTH
    return {
        "x": jax.random.normal(ks[0], (BATCH, SEQ, D_MODEL), jnp.float32),
        "positions": jnp.broadcast_to(jnp.arange(SEQ, dtype=jnp.int32), (BATCH, SEQ)),
        "ffn1_pre_g": gain(ks[1], (L, D_MODEL)),
        "ffn1_w_gate": dense(ks[2], (L, D_MODEL, D_FF), D_MODEL),
        "ffn1_w_up": dense(ks[3], (L, D_MODEL, D_FF), D_MODEL),
        "ffn1_w_down": dense(ks[4], (L, D_FF, D_MODEL), D_FF),
        "ffn1_post_g": gain(ks[5], (L, D_MODEL)),
        "mix_pre_g": gain(ks[6], (L, D_MODEL)),
        "w_in": dense(ks[7], (L, D_MODEL, IN_WIDTH), D_MODEL),
        "q_a_norm_g": gain(ks[8], (L, Q_LORA_RANK)),
        "w_uq": dense(ks[9], (L, Q_LORA_RANK, N_HEADS * (QK_NOPE_DIM + QK_ROPE_DIM)), Q_LORA_RANK),
        "kv_a_norm_g": gain(ks[10], (L, KV_LORA_RANK)),
        "w_uk": dense(ks[11], (L, KV_LORA_RANK, N_HEADS * QK_NOPE_DIM), KV_LORA_RANK),
        "w_uv": dense(ks[12], (L, KV_LORA_RANK, N_HEADS * V_HEAD_DIM), KV_LORA_RANK),
        "w_o_attn": dense(ks[13], (L, N_HEADS * V_HEAD_DIM, D_MODEL), N_HEADS * V_HEAD_DIM),
        "pool_w": dense(ks[14], (L, len(POOL_WINDOWS), POOL_GROUP, POOL_GROUP), POOL_GROUP),
        "pool_scale": gain(ks[15], (L, POOL_WIDTH)),
        "w_o_pool": dense(ks[16], (L, POOL_WIDTH, D_MODEL), POOL_WIDTH),
        "w_out": dense(ks[17], (L, D_MODEL, D_MODEL), D_MODEL),
        "mix_post_g": gain(ks[18], (L, D_MODEL)),
        "ffn2_pre_g": gain(ks[19], (L, D_MODEL)),
        "ffn2_w_gate": dense(ks[20], (L, D_MODEL, D_FF), D_MODEL),
        "ffn2_w_up": dense(ks[21], (L, D_MODEL, D_FF), D_MODEL),
        "ffn2_w_down": dense(ks[22], (L, D_FF, D_MODEL), D_FF),
        "ffn2_post_g": gain(ks[23], (L, D_MODEL)),
        "final_g": gain(ks[24], (L, D_MODEL)),
    }


def reference(x, positions, ffn1_pre_g, ffn1_w_gate, ffn1_w_up, ffn1_w_down, ffn1_post_g,
              mix_pre_g, w_in, q_a_norm_g, w_uq, kv_a_norm_g, w_uk, w_uv, w_o_attn,
              pool_w, pool_scale, w_o_pool, w_out, mix_post_g,
              ffn2_pre_g, ffn2_w_gate, ffn2_w_up, ffn2_w_down, ffn2_post_g, final_g):
    for l in range(DEPTH):
        f1 = _swiglu(_rmsnorm(x, ffn1_pre_g[l]), ffn1_w_gate[l], ffn1_w_up[l], ffn1_w_down[l])
        x = x + MACARON_WEIGHT * _rmsnorm(f1, ffn1_post_g[l])

        u = _rmsnorm(x, mix_pre_g[l])
        z = u @ w_in[l]
        c_q, c_kv, k_r, x_pool, gate_logits = jnp.split(z, IN_OFFSETS, axis=-1)
        y_attn = _mla(c_q, c_kv, k_r, positions, q_a_norm_g[l], w_uq[l],
                      kv_a_norm_g[l], w_uk[l], w_uv[l]) @ w_o_attn[l]
        y_pool = _pool_mixer(x_pool, pool_w[l], pool_scale[l]) @ w_o_pool[l]
        g_attn, g_pool = jnp.split(jax.nn.sigmoid(gate_logits), N_BRANCHES, axis=-1)
        mixed = (g_attn * y_attn + g_pool * y_pool) @ w_out[l]
        x = x + _rmsnorm(mixed, mix_post_g[l])

        f2 = _swiglu(_rmsnorm(x, ffn2_pre_g[l]), ffn2_w_gate[l], ffn2_w_up[l], ffn2_w_down[l])
        x = x + MACARON_WEIGHT * _rmsnorm(f2, ffn2_post_g[l])

        x = _rmsnorm(x, final_g[l])
    return x
```

```python
import numpy as np
import concourse.bass as bass
import concourse.mybir as mybir
from concourse.bass_utils import run_bass_kernel_spmd

F32 = mybir.dt.float32
BF16 = mybir.dt.bfloat16
I32 = mybir.dt.int32
U8 = mybir.dt.uint8
AF = mybir.ActivationFunctionType
ALU = mybir.AluOpType

ENGS = ("pe", "act", "dve", "pool", "sp")
_ENG_ATTR = {"pe": "tensor", "act": "scalar", "dve": "vector", "pool": "gpsimd", "sp": "sync"}

T = 2048
D = 1024
FF = 2816
NFF = 22
H = 8
EPS = 1e-6
NB = 4
TB = 512
IN_W = 3264
G_ROWS = 65
GO = dict(f1pre=0, f1post=8, mpre=16, gq=24, gkv=27, psc=29, mpost=33, f2pre=41, f2post=49, fin=57)
C_ID = 0
C_FC = 128
C_FS = 129
C_HPI = 130
C_EDGE = 132
C_W = C_EDGE + 64
WINS = (2, 4, 8, 16)


class Buf:
    __slots__ = ("name", "w", "r", "excl")

    def __init__(self, name, excl=False):
        self.name = name
        self.w = None
        self.r = {}
        self.excl = excl


class Op:
    __slots__ = ("stream", "idx", "sig", "count")

    def __init__(self, stream, idx):
        self.stream = stream
        self.idx = idx
        self.sig = False
        self.count = None


class Prog:
    def __init__(self, nc, eng_sems, dma_sems):
        self.nc = nc
        self.semobj = dict(zip(ENGS, eng_sems))
        self.nidx = {e: 0 for e in ENGS}
        self.ops = {e: [] for e in ENGS}
        self.known = {e: {} for e in ENGS}
        self.free_dma_sems = list(dma_sems)
        self.chain_last = {}
        self.last = {}

    def _need(self, eng, deps, op, skip_same):
        if op is None:
            return
        if skip_same and op.stream == eng:
            return
        if self.known[eng].get(op.stream, 0) >= op.idx:
            return
        cur = deps.get(op.stream)
        if cur is None or cur.idx < op.idx:
            deps[op.stream] = op

    def _collect(self, eng, reads, writes):
        deps = {}
        writes = list(writes)
        for b in reads:
            if b.excl:
                writes.append(b)
                continue
            self._need(eng, deps, b.w, False)
        for b in writes:
            self._need(eng, deps, b.w, True)
            for o in b.r.values():
                self._need(eng, deps, o, True)
        for s, o in deps.items():
            self.known[eng][s] = o.idx
            o.sig = True
        return list(deps.values())

    def _mark(self, op, reads, writes):
        for b in reads:
            if b.excl:
                b.w = op
                b.r = {}
            else:
                b.r[op.stream] = op
        for b in writes:
            b.w = op
            b.r = {}
        self.last[op.stream] = op

    def op(self, eng, fn, reads=(), writes=()):
        deps = self._collect(eng, reads, writes)
        self.nidx[eng] += 1
        o = Op(eng, self.nidx[eng])
        self._mark(o, reads, writes)
        self.ops[eng].append((deps, fn, o))
        return o

    def dma(self, eng, chain, fn, reads=(), writes=()):
        if chain not in self.semobj:
            self.semobj[chain] = self.free_dma_sems.pop()
            self.nidx[chain] = 0
        deps = self._collect(eng, reads, writes)
        last = self.chain_last.get(chain)
        if last is not None and self.known[eng].get(chain, 0) < last.idx:
            deps = [d for d in deps if d.stream != chain] + [last]
            self.known[eng][chain] = last.idx
        self.nidx[chain] += 1
        o = Op(chain, self.nidx[chain])
        o.sig = True
        self.chain_last[chain] = o
        self._mark(o, reads, writes)
        self.ops[eng].append((deps, fn, o))
        return o

    def barrier(self, engines=ENGS):
        lasts = list(self.last.values())
        for e in engines:
            deps = {}
            for o in lasts:
                self._need(e, deps, o, True)
            for s, o in deps.items():
                self.known[e][s] = o.idx
                o.sig = True
            if deps:
                self.ops[e].append((list(deps.values()), None, None))

    def emit(self, block):
        for e in ENGS:
            c = 0
            for deps, fn, o in self.ops[e]:
                if o is None:
                    continue
                if o.stream == e:
                    if o.sig:
                        c += 1
                        o.count = c
                else:
                    o.count = 16 * o.idx
        for e in ENGS:
            ops = self.ops[e]
            if not ops:
                continue

            def body(engine, ops=ops, e=e):
                for deps, fn, o in ops:
                    for d in deps:
                        engine.wait_ge(self.semobj[d.stream], d.count)
                    if fn is None:
                        continue
                    ins = fn(engine)
                    if o.stream != e:
                        ins.then_inc(self.semobj[o.stream], 16)
                    elif o.sig:
                        ins.then_inc(self.semobj[e], 1)

            getattr(block, _ENG_ATTR[e])(body)


class Arena:
    def __init__(self, tensor, nbytes):
        self.t = tensor
        self.free = [(0, nbytes)]
        self.used = {}

    def alloc(self, name, shape, dt, top=False):
        esz = {F32: 4, BF16: 2, I32: 4}[dt]
        n = esz
        for s in shape[1:]:
            n *= s
        nb = (n + 63) // 64 * 64
        order = list(enumerate(self.free))
        if top:
            order = order[::-1]
        for i, (o, sz) in order:
            if sz >= nb:
                if sz == nb:
                    self.free.pop(i)
                elif top:
                    self.free[i] = (o, sz - nb)
                    o = o + sz - nb
                else:
                    self.free[i] = (o + nb, sz - nb)
                self.used[name] = (o, nb)
                v = self.t[0:shape[0], o:o + n].bitcast(dt)
                if len(shape) == 3:
                    v = v.rearrange("p (a b) -> p a b", a=shape[1])
                return v
        raise RuntimeError(f"arena full for {name} {shape} free={self.free}")

    def release(self, *names):
        for name in names:
            o, nb = self.used.pop(name)
            self.free.append((o, nb))
        self.free.sort()
        m = []
        for o, sz in self.free:
            if m and m[-1][0] + m[-1][1] == o:
                m[-1] = (m[-1][0], m[-1][1] + sz)
            else:
                m.append((o, sz))
        self.free = m


class Rot:
    def __init__(self, ids):
        self.ids = list(ids)
        self.i = 0

    def next(self):
        v = self.ids[self.i % len(self.ids)]
        self.i += 1
        return v


from contextlib import ExitStack
import math

SCALE = 1.0 / math.sqrt(192.0)
TWO_PI = 2.0 * math.pi


def build(stage=3):
    nc = bass.Bass("TRN2", target_bir_lowering=False)

    def din(name, shape, dt=F32):
        return nc.dram_tensor(name, list(shape), dt, kind="ExternalInput").ap()

    x_d = din("x", [T, D])
    pos_d = din("pos", [128, T], I32)
    cst_d = din("cst", [128, C_W])
    g_d = din("gstack", [G_ROWS, 128])
    W = {}
    for f in ("ffn1", "ffn2"):
        W[f + "_g"] = din(f + "_w_gate", [D, FF])
        W[f + "_u"] = din(f + "_w_up", [D, FF])
        W[f + "_d"] = din(f + "_w_down", [FF, D])
    w_in_d = din("w_in", [D, IN_W])
    w_uq_d = din("w_uq", [384, 1536])
    w_uk_d = din("w_uk", [256, 1024])
    w_uv_d = din("w_uv", [256, 1024])
    w_oa_d = din("w_o_attn", [1024, 1024])
    pw_d = din("pool_w", [4, 128, 128])
    w_op_d = din("w_o_pool", [512, 1024])
    w_out_d = din("w_out", [1024, 1024])
    y_d = nc.dram_tensor("y", [T, D], F32, kind="ExternalOutput").ap()

    es = ExitStack()
    ARENA_BYTES = 206 * 1024
    arena_t = es.enter_context(nc.sbuf_tensor("arena", [128, ARENA_BYTES], U8))
    A = Arena(arena_t, ARENA_BYTES)
    banks = [es.enter_context(nc.psum_tensor(f"bk{i}", [128, 512], F32)) for i in range(8)]
    BK = [Buf(f"bk{i}", excl=True) for i in range(8)]
    eng_sems = [es.enter_context(nc.semaphore(f"s_{e}")) for e in ENGS]
    dma_sems = [es.enter_context(nc.semaphore(f"d{i}")) for i in range(48)]
    P = Prog(nc, eng_sems, dma_sems)
    rot = Rot(range(8))

    def kc_rows(ap2d):
        return ap2d.rearrange("(c p) n -> p c n", p=128)

    def tbs(tb):
        return slice(tb * TB, (tb + 1) * TB)

    def mm(bank, c0, n, pairs, reads):
        out = banks[bank][:, c0:c0 + n]

        def fn(e, pairs=pairs, out=out):
            k = len(pairs)
            for i, (l, r) in enumerate(pairs):
                ins = e.matmul(out, l, r, start=(i == 0), stop=(i == k - 1))
            return ins
        return P.op("pe", fn, reads=reads, writes=[BK[bank]])

    def act(out, in_, func, reads, writes, **kw):
        return P.op("act", lambda e: e.activation(out=out, in_=in_, func=func, **kw), reads=reads, writes=writes)

    def tt(out, in0, in1, op, reads, writes, eng="dve"):
        return P.op(eng, lambda e: e.tensor_tensor(out=out, in0=in0, in1=in1, op=op), reads=reads, writes=writes)

    def stt(out, in0, scalar, in1, op0, op1, reads, writes, eng="dve"):
        return P.op(eng, lambda e: e.scalar_tensor_tensor(out=out, in0=in0, scalar=scalar, in1=in1, op0=op0, op1=op1),
                    reads=reads, writes=writes)

    def ts(out, in0, s1, s2, op0, op1, reads, writes, eng="dve"):
        if s2 is None:
            return P.op(eng, lambda e: e.tensor_scalar(out=out, in0=in0, scalar1=s1, scalar2=None, op0=op0),
                        reads=reads, writes=writes)
        return P.op(eng, lambda e: e.tensor_scalar(out=out, in0=in0, scalar1=s1, scalar2=s2, op0=op0, op1=op1),
                    reads=reads, writes=writes)

    def cp(out, in_, reads, writes, eng="dve"):
        return P.op(eng, lambda e: e.tensor_copy(out=out, in_=in_), reads=reads, writes=writes)

    def dma(q, chain, out, in_, reads=(), writes=()):
        return P.dma(q, chain, lambda e: e.dma_start(out=out, in_=in_), reads=reads, writes=writes)

    xT = A.alloc("xT", [128, 8, T], F32)
    XB = [[Buf(f"x{c}_{tb}") for tb in range(NB)] for c in range(8)]
    cst = A.alloc("cst", [128, C_W], F32); cstB = Buf("cst")
    gT = A.alloc("gT", [128, G_ROWS], F32); gTB = Buf("gT")
    gh = A.alloc("gh", [128, G_ROWS], F32); ghB = Buf("gh")
    ones = A.alloc("ones", [128, 128], BF16); onesB = Buf("ones")
    epsb = A.alloc("epsb", [128, 1], F32); epsB = Buf("eps")
    sq = A.alloc("sq", [128, 8, TB], BF16); sqB = Buf("sq")
    rs = [A.alloc(f"rs{i}", [128, TB], F32) for i in range(2)]; rsB = [Buf(f"rs{i}") for i in range(2)]
    sg = [A.alloc(f"sg{i}", [128, TB], F32) for i in range(2)]; sgB = [Buf(f"sg{i}") for i in range(2)]
    ctr = {"rs": 0, "sg": 0}
    ident = cst[:, C_ID:C_ID + 128]

    def norm_rs(src3, nch, Dn, reads):
        act(sq[:, 0:nch, :], src3, AF.Square, reads, [sqB])
        b = rot.next()
        mm(b, 0, TB, [(ones, sq[:, c, :]) for c in range(nch)], [onesB, sqB])
        ri = ctr["rs"] % 2
        ctr["rs"] += 1
        act(rs[ri], banks[b][:, :], AF.Sqrt, [BK[b], epsB], [rsB[ri]], bias=epsb[:, 0:1], scale=1.0 / Dn)
        P.op("dve", lambda e: e.reciprocal(out=rs[ri], in_=rs[ri]), reads=[rsB[ri]], writes=[rsB[ri]])
        return ri

    def next_sg():
        i = ctr["sg"] % 2
        ctr["sg"] += 1
        return i

    dma("sp", "c_cst", cst, cst_d[:, :], writes=[cstB])
    gsb = A.alloc("gsb", [128, 128], F32, top=True); gsbB = Buf("gsb")
    dma("sp", "c_g", gsb[0:G_ROWS, :], g_d[:, :], writes=[gsbB])
    P.op("dve", lambda e: e.memset(ones, 1.0), writes=[onesB])
    P.op("dve", lambda e: e.memset(epsb, EPS), writes=[epsB])
    P.op("pe", lambda e: e.transpose(banks[0][:, 0:G_ROWS], gsb[0:G_ROWS, :], cst[0:G_ROWS, 0:G_ROWS]),
         reads=[gsbB, cstB], writes=[BK[0]])
    act(gT, banks[0][:, 0:G_ROWS], AF.Copy, [BK[0]], [gTB])
    P.op("act", lambda e: e.mul(out=gh, in_=gT, mul=0.5), reads=[gTB], writes=[ghB])

    xin = [A.alloc(f"xin{i}", [128, D], F32, top=True) for i in range(3)]
    xinB = [Buf(f"xin{i}") for i in range(3)]
    for i in range(16):
        xi = i % 3
        dma("sp", f"c_xin{xi}", xin[xi], x_d[i * 128:(i + 1) * 128, :], writes=[xinB[xi]])
        for half in range(2):
            b = rot.next()

            def trf(e, xi=xi, half=half, b=b):
                for j in range(4):
                    c = half * 4 + j
                    ins = e.transpose(banks[b][:, j * 128:(j + 1) * 128], xin[xi][:, c * 128:(c + 1) * 128], ident)
                return ins
            P.op("pe", trf, reads=[xinB[xi], cstB], writes=[BK[b]])
            act(xT[:, half * 4:half * 4 + 4, i * 128:(i + 1) * 128],
                banks[b][:, :].rearrange("p (c t) -> p c t", c=4), AF.Copy,
                [BK[b]], [XB[half * 4 + j][i // 4] for j in range(4)])
    A.release("gsb", "xin0", "xin1", "xin2")

    def ffn(f, go_pre, go_post):
        hT = A.alloc("hT", [128, 8, 1024], BF16)
        HB = [[Buf(f"h{c}_{t}") for t in range(2)] for c in range(8)]
        aT = A.alloc("aT", [128, NFF, 1024], BF16)
        AB = [[Buf(f"a{j}_{t}") for t in range(2)] for j in range(NFF)]
        f1T = A.alloc("f1T", [128, 8, 1024], F32, top=True)
        FB = [[Buf(f"f{c}_{t}") for t in range(2)] for c in range(8)]
        wg = [A.alloc(f"wg{i}", [128, 8, 256], BF16) for i in range(2)]; wgB = [Buf(f"wg{i}") for i in range(2)]
        wu = [A.alloc(f"wu{i}", [128, 8, 256], BF16) for i in range(2)]; wuB = [Buf(f"wu{i}") for i in range(2)]
        wdn = [A.alloc(f"wdn{i}", [128, NFF, 128], BF16) for i in range(2)]; wdB = [Buf(f"wdn{i}") for i in range(2)]
        Wg, Wu, Wd = W[f + "_g"], W[f + "_u"], W[f + "_d"]
        cnt = {"g": 0, "d": 0}
        bgq = []

        def Nn(hh):
            for tl in range(2):
                tb = 2 * hh + tl
                ri = norm_rs(xT[:, :, tbs(tb)], 8, D, [XB[c][tb] for c in range(8)])
                for c in range(8):
                    stt(hT[:, c, tbs(tl)], xT[:, c, tbs(tb)], gT[:, go_pre + c:go_pre + c + 1], rs[ri],
                        ALU.mult, ALU.mult, [XB[c][tb], gTB, rsB[ri]], [HB[c][tl]])

        def GU(hh):
            for grp in range(11):
                bi = cnt["g"] % 2
                cnt["g"] += 1
                dma("pool", f"c_wg{bi}", wg[bi], kc_rows(Wg[:, grp * 256:(grp + 1) * 256]), writes=[wgB[bi]])
                dma("pool", f"c_wu{bi}", wu[bi], kc_rows(Wu[:, grp * 256:(grp + 1) * 256]), writes=[wuB[bi]])
                for j2 in range(2):
                    j = grp * 2 + j2
                    for tl in range(2):
                        bg = rot.next()
                        bu = rot.next()
                        hr = [HB[kc][tl] for kc in range(8)]
                        mm(bg, 0, TB, [(wg[bi][:, kc, j2 * 128:(j2 + 1) * 128], hT[:, kc, tbs(tl)]) for kc in range(8)],
                           [wgB[bi]] + hr)
                        mm(bu, 0, TB, [(wu[bi][:, kc, j2 * 128:(j2 + 1) * 128], hT[:, kc, tbs(tl)]) for kc in range(8)],
                           [wuB[bi]] + hr)
                        si = next_sg()
                        act(sg[si], banks[bg][:, :], AF.Silu, [BK[bg]], [sgB[si]])
                        tt(aT[:, j, tbs(tl)], sg[si], banks[bu][:, :], ALU.mult, [sgB[si], BK[bu]], [AB[j][tl]])
                        if bgq:
                            bgq.pop(0)()
            while bgq:
                bgq.pop(0)()

        def Dd(hh):
            for dc in range(8):
                bi = cnt["d"] % 2
                cnt["d"] += 1
                dma("pool", f"c_wd{bi}", wdn[bi], kc_rows(Wd[:, dc * 128:(dc + 1) * 128]), writes=[wdB[bi]])
                for tl in range(2):
                    b = rot.next()
                    mm(b, 0, TB, [(wdn[bi][:, j, :], aT[:, j, tbs(tl)]) for j in range(NFF)],
                       [wdB[bi]] + [AB[j][tl] for j in range(NFF)])
                    act(f1T[:, dc, tbs(tl)], banks[b][:, :], AF.Copy, [BK[b]], [FB[dc][tl]])

        def Pp(hh):
            th = []
            for tl in range(2):
                tb = 2 * hh + tl
                st = {}

                def t_norm(tl=tl, st=st):
                    st["ri"] = norm_rs(f1T[:, :, tbs(tl)], 8, D, [FB[c][tl] for c in range(8)])
                th.append(t_norm)
                for c in range(8):
                    def t_c(tl=tl, tb=tb, c=c, st=st):
                        ri = st["ri"]
                        stt(f1T[:, c, tbs(tl)], f1T[:, c, tbs(tl)], gh[:, go_post + c:go_post + c + 1], rs[ri],
                            ALU.mult, ALU.mult, [FB[c][tl], ghB, rsB[ri]], [FB[c][tl]])
                        tt(xT[:, c, tbs(tb)], f1T[:, c, tbs(tl)], xT[:, c, tbs(tb)], ALU.add,
                           [FB[c][tl], XB[c][tb]], [XB[c][tb]], eng="pool")
                    th.append(t_c)
            return th

        Nn(0)
        GU(0)
        Nn(1)
        Dd(0)
        bgq.extend(Pp(0))
        GU(1)
        Dd(1)
        for t in Pp(1):
            t()
        P.barrier()
        A.release("hT", "aT", "f1T", "wg0", "wg1", "wu0", "wu1", "wdn0", "wdn1")

    if stage >= 1:
        ffn("ffn1", GO["f1pre"], GO["f1post"])

    def make_uT(uT, UB, tblist=range(NB), off=0):
        for tb in tblist:
            ri = norm_rs(xT[:, :, tbs(tb)], 8, D, [XB[c][tb] for c in range(8)])
            for c in range(8):
                stt(uT[:, c, tbs(tb - off)], xT[:, c, tbs(tb)], gT[:, GO["mpre"] + c:GO["mpre"] + c + 1], rs[ri],
                    ALU.mult, ALU.mult, [XB[c][tb], gTB, rsB[ri]], [UB[c][tb - off]])

    def mixer():
        cosT = A.alloc("cosT", [128, T], F32, top=True); cosB = Buf("cos")
        sinT = A.alloc("sinT", [128, T], F32, top=True); sinB = Buf("sin")
        cqn = A.alloc("cqn", [128, 3, T], BF16, top=True)
        CQB = [[Buf(f"cq{c}_{t}") for t in range(NB)] for c in range(3)]
        ckvn = A.alloc("ckvn", [128, 2, T], BF16, top=True)
        CKB = [[Buf(f"ck{c}_{t}") for t in range(NB)] for c in range(2)]
        krT = A.alloc("krT", [128, T], BF16, top=True)
        KRB = [Buf(f"kr{t}") for t in range(NB)]
        posi = A.alloc("posi", [128, T], I32); posiB = Buf("posi")
        kf = posi.bitcast(F32)
        posf = A.alloc("posf", [128, T], F32); posfB = Buf("posf")
        ang = A.alloc("ang", [128, T], F32); angB = Buf("ang")
        uT = A.alloc("uT", [128, 8, T], BF16)
        UB = [[Buf(f"u{c}_{t}") for t in range(NB)] for c in range(8)]
        win = A.alloc("win", [128, 8, 768], BF16); winB = Buf("win")
        wkA = A.alloc("wkA", [128, 8, 128], BF16); wkAB = Buf("wkA")
        wkB = A.alloc("wkB", [128, 8, 128], BF16); wkBB = Buf("wkB")
        cqr = A.alloc("cqr", [128, 3, TB], F32); cqrB = [Buf(f"cqr{c}") for c in range(3)]
        ckr = A.alloc("ckr", [128, 2, TB], F32); ckrB = [Buf(f"ckr{c}") for c in range(2)]
        dma("pool", "c_win", win, kc_rows(w_in_d[:, 0:768]), writes=[winB])
        dma("sp", "c_pos", posi, pos_d[:, :], writes=[posiB])
        PL = "pool"
        cp(posf, posi, [posiB], [posfB], eng=PL)
        angs = []
        ts(ang, posf, cst[:, C_FC:C_FC + 1], math.pi / 2, ALU.mult, ALU.add, [posfB, cstB], [angB], eng=PL)
        ts(posf, posf, cst[:, C_FS:C_FS + 1], None, ALU.mult, None, [posfB, cstB], [posfB], eng=PL)
        for ti in range(2):
            a = ang if ti == 0 else posf
            aB = angB if ti == 0 else posfB
            ts(posi, a, 1.0 / TWO_PI, None, ALU.mult, None, [aB], [posiB], eng=PL)
            cp(kf, posi, [posiB], [posiB], eng=PL)
            ts(kf, kf, -TWO_PI, None, ALU.mult, None, [posiB], [posiB], eng=PL)
            tt(a, a, kf, ALU.add, [posiB, aB], [aB], eng=PL)
            ts(kf, a, math.pi, -TWO_PI, ALU.is_gt, ALU.mult, [aB], [posiB], eng=PL)
            tt(a, a, kf, ALU.add, [posiB, aB], [aB], eng=PL)
            ts(kf, a, -math.pi, TWO_PI, ALU.is_lt, ALU.mult, [aB], [posiB], eng=PL)
            tt(a, a, kf, ALU.add, [posiB, aB], [aB], eng=PL)
            ts(a, a, 3.14159, -3.14159, ALU.min, ALU.max, [aB], [aB], eng=PL)
            angs.append((a, aB))
        make_uT(uT, UB)
        for d0 in (0, 64):
            cp(wkA[:, :, d0:d0 + 64], win[:, :, 640:704], [winB], [wkAB])
            cp(wkB[:, :, d0:d0 + 32], win[:, :, 672:704], [winB], [wkBB])
            cp(wkB[:, :, d0 + 32:d0 + 64], win[:, :, 640:672], [winB], [wkBB])
        for tb in range(NB):
            ur = [UB[kc][tb] for kc in range(8)]
            for c in range(3):
                b = rot.next()
                mm(b, 0, TB, [(win[:, kc, c * 128:(c + 1) * 128], uT[:, kc, tbs(tb)]) for kc in range(8)], [winB] + ur)
                act(cqr[:, c, :], banks[b][:, :], AF.Copy, [BK[b]], [cqrB[c]])
            for c in range(2):
                b = rot.next()
                mm(b, 0, TB, [(win[:, kc, 384 + c * 128:384 + (c + 1) * 128], uT[:, kc, tbs(tb)]) for kc in range(8)],
                   [winB] + ur)
                act(ckr[:, c, :], banks[b][:, :], AF.Copy, [BK[b]], [ckrB[c]])
            ri = norm_rs(cqr[:, :, :], 3, 384, cqrB)
            for c in range(3):
                stt(cqn[:, c, tbs(tb)], cqr[:, c, :], gT[:, GO["gq"] + c:GO["gq"] + c + 1], rs[ri],
                    ALU.mult, ALU.mult, [cqrB[c], gTB, rsB[ri]], [CQB[c][tb]])
            ri = norm_rs(ckr[:, :, :], 2, 256, ckrB)
            for c in range(2):
                stt(ckvn[:, c, tbs(tb)], ckr[:, c, :], gT[:, GO["gkv"] + c:GO["gkv"] + c + 1], rs[ri],
                    ALU.mult, ALU.mult, [ckrB[c], gTB, rsB[ri]], [CKB[c][tb]])
        act(cosT, angs[0][0], AF.Sin, [angs[0][1]], [cosB])
        act(sinT, angs[1][0], AF.Sin, [angs[1][1]], [sinB])
        for tb in range(NB):
            ur = [UB[kc][tb] for kc in range(8)]
            ba = rot.next()
            mm(ba, 0, TB, [(wkA[:, kc, :], uT[:, kc, tbs(tb)]) for kc in range(8)], [wkAB] + ur)
            bb = rot.next()
            mm(bb, 0, TB, [(wkB[:, kc, :], uT[:, kc, tbs(tb)]) for kc in range(8)], [wkBB] + ur)
            s0 = next_sg()
            tt(sg[s0], banks[ba][:, :], cosT[:, tbs(tb)], ALU.mult, [BK[ba], cosB], [sgB[s0]])
            s1 = next_sg()
            tt(sg[s1], banks[bb][:, :], sinT[:, tbs(tb)], ALU.mult, [BK[bb], sinB], [sgB[s1]])
            tt(krT[:, tbs(tb)], sg[s0], sg[s1], ALU.add, [sgB[s0], sgB[s1]], [KRB[tb]])
        P.barrier()
        A.release("uT", "posi", "posf", "ang", "win", "wkA", "wkB", "cqr", "ckr")

        wq = A.alloc("wq", [128, 3, 1536], BF16); wqB_ = Buf("wq")
        dma("pool", "c_wq", wq, kc_rows(w_uq_d[:, :]), writes=[wqB_])
        wqa = A.alloc("wqa", [128, 3, 512], BF16); wqaB = Buf("wqa")
        wqb = A.alloc("wqb", [128, 3, 512], BF16); wqbB = Buf("wqb")
        qrT = A.alloc("qrT", [128, 4, T], BF16, top=True)
        QRB = [[Buf(f"qr{j}_{t}") for t in range(NB)] for j in range(4)]
        for kc in range(3):
            src = wq[:, kc, :].rearrange("p (h d) -> p h d", h=8)
            cp(wqa[:, kc, :].rearrange("p (h d) -> p h d", h=8), src[:, :, 128:192], [wqB_], [wqaB])
            dstb = wqb[:, kc, :].rearrange("p (h d) -> p h d", h=8)
            cp(dstb[:, :, 0:32], src[:, :, 160:192], [wqB_], [wqbB])
            cp(dstb[:, :, 32:64], src[:, :, 128:160], [wqB_], [wqbB])
        for j in range(4):
            for tb in range(NB):
                cr = [CQB[kc][tb] for kc in range(3)]
                ba = rot.next()
                mm(ba, 0, TB, [(wqa[:, kc, j * 128:(j + 1) * 128], cqn[:, kc, tbs(tb)]) for kc in range(3)], [wqaB] + cr)
                bb = rot.next()
                mm(bb, 0, TB, [(wqb[:, kc, j * 128:(j + 1) * 128], cqn[:, kc, tbs(tb)]) for kc in range(3)], [wqbB] + cr)
                s0 = next_sg()
                tt(sg[s0], banks[ba][:, :], cosT[:, tbs(tb)], ALU.mult, [BK[ba], cosB], [sgB[s0]])
                s1 = next_sg()
                tt(sg[s1], banks[bb][:, :], sinT[:, tbs(tb)], ALU.mult, [BK[bb], sinB], [sgB[s1]])
                tt(qrT[:, j, tbs(tb)], sg[s0], sg[s1], ALU.add, [sgB[s0], sgB[s1]], [QRB[j][tb]])
        P.barrier()
        A.release("cosT", "sinT", "wqa", "wqb")

        attnT = A.alloc("attnT", [128, 8, T], BF16, top=True)
        ATB = [[Buf(f"at{h}_{t}") for t in range(NB)] for h in range(H)]
        wk = A.alloc("wk", [128, 2, 1024], BF16); wkB_ = Buf("wk")
        wv = A.alloc("wv", [128, 2, 1024], BF16); wvB_ = Buf("wv")
        dma("pool", "c_wk", wk, kc_rows(w_uk_d[:, :]), writes=[wkB_])
        dma("pool", "c_wv", wv, kc_rows(w_uv_d[:, :]), writes=[wvB_])
        qn = [A.alloc(f"qn{i}", [128, T], BF16) for i in range(2)]
        QNB = [[Buf(f"qn{i}_{t}") for t in range(NB)] for i in range(2)]
        kn = [A.alloc(f"kn{i}", [128, T], BF16) for i in range(2)]
        KNB = [[Buf(f"kn{i}_{t}") for t in range(NB)] for i in range(2)]
        Vh = [A.alloc(f"vh{i}", [128, 16, 128], BF16) for i in range(2)]
        VB = [[Buf(f"vh{i}_{g}") for g in range(4)] for i in range(2)]
        NPT = 6
        PT = [A.alloc(f"pt{i}", [128, TB], BF16) for i in range(NPT)]
        PTB = [Buf(f"pt{i}") for i in range(NPT)]
        accs = [[sg[0], sg[1]], [rs[0], rs[1]]]
        accB = [[Buf("accA0"), Buf("accB0")], [Buf("accA1"), Buf("accB1")]]
        accS = [sq[:, 0, :], sq[:, 1, :]]
        accSB = [Buf("accS0"), Buf("accS1")]
        rcp = [A.alloc(f"rcp{i}", [128, TB], F32) for i in range(2)]
        rcpB = [Buf(f"rcp{i}") for i in range(2)]
        strot = Rot([0, 1, 2, 3])

        def gen(h):
            hb = h % 2
            for tb in range(NB):
                b = strot.next()
                mm(b, 0, TB, [(wq[:, kc, h * 192:h * 192 + 128], cqn[:, kc, tbs(tb)]) for kc in range(3)],
                   [wqB_] + [CQB[kc][tb] for kc in range(3)])
                cp(qn[hb][:, tbs(tb)], banks[b][:, :], [BK[b]], [QNB[hb][tb]])
            for tb in range(NB):
                b = strot.next()
                mm(b, 0, TB, [(wk[:, kc, h * 128:(h + 1) * 128], ckvn[:, kc, tbs(tb)]) for kc in range(2)],
                   [wkB_] + [CKB[kc][tb] for kc in range(2)])
                cp(kn[hb][:, tbs(tb)], banks[b][:, :], [BK[b]], [KNB[hb][tb]])
            for g4 in range(4):
                b = strot.next()

                def vfn(e, g4=g4, b=b, h=h):
                    for j in range(4):
                        tile = g4 * 4 + j
                        for kc in range(2):
                            ins = e.matmul(banks[b][:, j * 128:(j + 1) * 128],
                                           ckvn[:, kc, tile * 128:(tile + 1) * 128],
                                           wv[:, kc, h * 128:(h + 1) * 128], start=(kc == 0), stop=(kc == 1))
                    return ins
                P.op("pe", vfn, reads=[wvB_] + [CKB[kc][g4] for kc in range(2)], writes=[BK[b]])
                cp(Vh[hb][:, g4 * 4:(g4 + 1) * 4, :], banks[b][:, :].rearrange("p (j d) -> p j d", j=4),
                   [BK[b]], [VB[hb][g4]])

        steps = [(h, qb, kc) for h in range(H) for qb in range(NB) for kc in range(16)]
        stbank = {}
        ptc = 0

        def emit_st(s):
            h, qb, kc = steps[s]
            hb = h % 2
            po = 64 * (h % 2)
            b = strot.next()
            stbank[s] = b
            mm(b, 0, TB, [(kn[hb][:, kc * 128:(kc + 1) * 128], qn[hb][:, tbs(qb)]),
                          (krT[po:po + 64, kc * 128:(kc + 1) * 128], qrT[po:po + 64, h // 2, tbs(qb)])],
               [KNB[hb][kc // 4], QNB[hb][qb], KRB[kc // 4], QRB[h // 2][qb]])

        gen(0)
        emit_st(0)
        emit_st(1)
        for s, (h, qb, kc) in enumerate(steps):
            hb = h % 2
            grp = h * NB + qb
            g2 = grp % 2
            ob = 4 + g2
            sb_ = 6 + g2
            if s + 2 < len(steps):
                emit_st(s + 2)
            pi = s % NPT
            b = stbank.pop(s)
            act(PT[pi], banks[b][:, :], AF.Exp, [BK[b]], [PTB[pi]], scale=SCALE)
            P.op("pe", lambda e, hb=hb, kc=kc, pi=pi, ob=ob: e.matmul(
                banks[ob][:, :], Vh[hb][:, kc, :], PT[pi], start=(kc == 0), stop=(kc == 15)),
                reads=[VB[hb][kc // 4], PTB[pi]], writes=[BK[ob]])
            ae = "dve" if kc % 2 == 0 else "pool"
            acc = accs[g2][kc % 2]
            aB = accB[g2][kc % 2]
            if kc < 2:
                cp(acc, PT[pi], [PTB[pi]], [aB], eng=ae)
            else:
                tt(acc, acc, PT[pi], ALU.add, [aB, PTB[pi]], [aB], eng=ae)
            if kc == 15:
                tt(accS[g2], accs[g2][0], accs[g2][1], ALU.add, accB[g2], [accSB[g2]])
                mm(sb_, 0, TB, [(ones, accS[g2])], [onesB, accSB[g2]])
                P.op("dve", lambda e, g2=g2, sb_=sb_: e.reciprocal(out=rcp[g2], in_=banks[sb_][:, :]),
                     reads=[BK[sb_]], writes=[rcpB[g2]])
                tt(attnT[:, h, tbs(qb)], banks[ob][:, :], rcp[g2], ALU.mult, [BK[ob], rcpB[g2]], [ATB[h][qb]])
                if qb == 0 and h + 1 < H:
                    gen(h + 1)
        P.barrier()
        A.release("cqn", "ckvn", "krT", "wq", "qrT", "wk", "wv", "qn0", "qn1", "kn0", "kn1", "vh0", "vh1",
                  "pt0", "pt1", "pt2", "pt3", "pt4", "pt5", "rcp0", "rcp1")

        uT = A.alloc("uT", [128, 8, T], BF16)
        UB = [[Buf(f"u{c}_{t}") for t in range(NB)] for c in range(8)]
        ypT = A.alloc("ypT", [128, 4, T], BF16, top=True)
        YPB = [[Buf(f"yp{g}_{t}") for t in range(NB)] for g in range(4)]
        wpl = A.alloc("wpl", [128, 8, 512], BF16); wplB = Buf("wpl")
        dma("pool", "c_wpl", wpl, kc_rows(w_in_d[:, 704:1216]), writes=[wplB])
        pwt = A.alloc("pwt", [128, 4, 128], BF16); pwtB = Buf("pwt")
        dma("pool", "c_pwt", pwt, pw_d.rearrange("g c d -> c g d"), writes=[pwtB])
        XP = A.alloc("XP", [128, T + 16], F32); XPB = Buf("XP")
        sA = A.alloc("sA", [128, T + 16], F32); sAB = Buf("sA")
        sBt = A.alloc("sBt", [128, T + 16], F32); sBB = Buf("sBt")
        dT = A.alloc("dT", [128, T], BF16); dTB = Buf("dT")
        te = A.alloc("te", [128, 8], F32); teB = Buf("te")
        make_uT(uT, UB)
        P.op("dve", lambda e: e.memset(XP, 0.0), writes=[XPB])
        L = T + 16
        for g, w in enumerate(WINS):
            for tb in range(NB):
                b = rot.next()
                mm(b, 0, TB, [(wpl[:, kc, g * 128:(g + 1) * 128], uT[:, kc, tbs(tb)]) for kc in range(8)],
                   [wplB] + [UB[kc][tb] for kc in range(8)])
                cp(XP[:, 8 + tb * TB:8 + (tb + 1) * TB], banks[b][:, :], [BK[b]], [XPB])
            tt(sA[:, 0:L - 1], XP[:, 0:L - 1], XP[:, 1:L], ALU.add, [XPB], [sAB])
            tot, totB, off = sA, sAB, 7
            if w >= 4:
                tt(sBt[:, 0:L - 3], sA[:, 0:L - 3], sA[:, 2:L - 1], ALU.add, [sAB], [sBB])
                tot, totB, off = sBt, sBB, 6
            if w >= 8:
                tt(sA[:, 0:L - 7], sBt[:, 0:L - 7], sBt[:, 4:L - 3], ALU.add, [sBB], [sAB])
                tot, totB, off = sA, sAB, 4
            if w >= 16:
                tt(sBt[:, 0:L - 15], sA[:, 0:L - 15], sA[:, 8:L - 7], ALU.add, [sAB], [sBB])
                tot, totB, off = sBt, sBB, 0
            stt(dT, tot[:, off:off + T], 1.0 / w, XP[:, 8:8 + T], ALU.mult, ALU.subtract, [totB, XPB], [dTB])
            for (t0, c0) in ((0, 0), (T - 8, 8)):
                ec = C_EDGE + g * 16 + c0
                tt(te, tot[:, off + t0:off + t0 + 8], cst[:, ec:ec + 8], ALU.mult, [totB, cstB], [teB])
                tt(dT[:, t0:t0 + 8], te, XP[:, 8 + t0:16 + t0], ALU.subtract, [teB, XPB], [dTB])
            for tb in range(NB):
                b = rot.next()
                mm(b, 0, TB, [(pwt[:, g, :], dT[:, tbs(tb)])], [pwtB, dTB])
                act(ypT[:, g, tbs(tb)], banks[b][:, :], AF.Copy, [BK[b]], [YPB[g][tb]],
                    scale=gT[:, GO["psc"] + g:GO["psc"] + g + 1])
        P.barrier()
        A.release("uT", "wpl", "pwt", "XP", "sA", "sBt", "dT", "te")

        mixin = A.alloc("mixin", [128, 8, T], BF16)
        MXB = [[Buf(f"mx{c}_{t}") for t in range(NB)] for c in range(8)]
        woa = [A.alloc(f"woa{i}", [128, 8, 128], BF16) for i in range(2)]; woaB = [Buf(f"woa{i}") for i in range(2)]
        wop = [A.alloc(f"wop{i}", [128, 4, 128], BF16) for i in range(2)]; wopB = [Buf(f"wop{i}") for i in range(2)]
        wga = [A.alloc(f"wga{i}", [128, 8, 128], BF16) for i in range(2)]; wgaB = [Buf(f"wga{i}") for i in range(2)]
        wgp = [A.alloc(f"wgp{i}", [128, 8, 128], BF16) for i in range(2)]; wgpB = [Buf(f"wgp{i}") for i in range(2)]
        sgm = [A.alloc(f"sgm{i}", [128, TB], F32) for i in range(4)]; sgmB = [Buf(f"sgm{i}") for i in range(4)]
        uTh = A.alloc("uTh", [128, 8, 1024], BF16)
        UBh = [[Buf(f"uh{c}_{t}") for t in range(2)] for c in range(8)]
        sc = 0
        wc = 0
        for hh in range(2):
            make_uT(uTh, UBh, [2 * hh, 2 * hh + 1], 2 * hh)
            for dc in range(8):
                bi = wc % 2
                wc += 1
                cs = slice(dc * 128, (dc + 1) * 128)
                dma("pool", f"c_woa{bi}", woa[bi], kc_rows(w_oa_d[:, cs]), writes=[woaB[bi]])
                dma("pool", f"c_wop{bi}", wop[bi], kc_rows(w_op_d[:, cs]), writes=[wopB[bi]])
                dma("pool", f"c_wga{bi}", wga[bi], kc_rows(w_in_d[:, 1216 + dc * 128:1216 + (dc + 1) * 128]), writes=[wgaB[bi]])
                dma("pool", f"c_wgp{bi}", wgp[bi], kc_rows(w_in_d[:, 2240 + dc * 128:2240 + (dc + 1) * 128]), writes=[wgpB[bi]])
                for tl in range(2):
                    tb = 2 * hh + tl
                    ur = [UBh[kc][tl] for kc in range(8)]
                    bya = rot.next()
                    mm(bya, 0, TB, [(woa[bi][:, h, :], attnT[:, h, tbs(tb)]) for h in range(H)],
                       [woaB[bi]] + [ATB[h][tb] for h in range(H)])
                    bga = rot.next()
                    mm(bga, 0, TB, [(wga[bi][:, kc, :], uTh[:, kc, tbs(tl)]) for kc in range(8)], [wgaB[bi]] + ur)
                    byp = rot.next()
                    mm(byp, 0, TB, [(wop[bi][:, g, :], ypT[:, g, tbs(tb)]) for g in range(4)],
                       [wopB[bi]] + [YPB[g][tb] for g in range(4)])
                    bgp = rot.next()
                    mm(bgp, 0, TB, [(wgp[bi][:, kc, :], uTh[:, kc, tbs(tl)]) for kc in range(8)], [wgpB[bi]] + ur)
                    i0 = sc % 4
                    i1 = (sc + 1) % 4
                    sc += 2
                    act(sgm[i0], banks[bga][:, :], AF.Sigmoid, [BK[bga]], [sgmB[i0]])
                    act(sgm[i1], banks[bgp][:, :], AF.Sigmoid, [BK[bgp]], [sgmB[i1]])
                    tt(sgm[i0], sgm[i0], banks[bya][:, :], ALU.mult, [sgmB[i0], BK[bya]], [sgmB[i0]])
                    tt(sgm[i1], sgm[i1], banks[byp][:, :], ALU.mult, [sgmB[i1], BK[byp]], [sgmB[i1]])
                    tt(mixin[:, dc, tbs(tb)], sgm[i0], sgm[i1], ALU.add, [sgmB[i0], sgmB[i1]], [MXB[dc][tb]])
        P.barrier()
        A.release("uTh", "attnT", "ypT", "woa0", "woa1", "wop0", "wop1", "wga0", "wga1", "wgp0", "wgp1",
                  "sgm0", "sgm1", "sgm2", "sgm3")

        wo = A.alloc("wo", [128, 8, 1024], BF16); woB = Buf("wo")
        dma("pool", "c_wo", wo, kc_rows(w_out_d[:, :]), writes=[woB])
        mTs = [A.alloc(f"mT{i}", [128, 8, TB], F32) for i in range(2)]
        mTBs = [[Buf(f"mT{i}_{c}") for c in range(8)] for i in range(2)]
        for tb in range(NB):
            mT = mTs[tb % 2]
            mTB = mTBs[tb % 2]
            for dc in range(8):
                b = rot.next()
                mm(b, 0, TB, [(wo[:, kc, dc * 128:(dc + 1) * 128], mixin[:, kc, tbs(tb)]) for kc in range(8)],
                   [woB] + [MXB[kc][tb] for kc in range(8)])
                act(mT[:, dc, :], banks[b][:, :], AF.Copy, [BK[b]], [mTB[dc]])
            ri = norm_rs(mT[:, :, :], 8, D, mTB)
            for c in range(8):
                stt(mT[:, c, :], mT[:, c, :], gT[:, GO["mpost"] + c:GO["mpost"] + c + 1], rs[ri],
                    ALU.mult, ALU.mult, [mTB[c], gTB, rsB[ri]], [mTB[c]])
                tt(xT[:, c, tbs(tb)], mT[:, c, :], xT[:, c, tbs(tb)], ALU.add, [mTB[c], XB[c][tb]], [XB[c][tb]],
                   eng="pool")
        P.barrier()
        A.release("mixin", "wo", "mT0", "mT1")

    if stage >= 2:
        mixer()
    if stage >= 3:
        ffn("ffn2", GO["f2pre"], GO["f2post"])

    NYO = 6
    yo = [A.alloc(f"yo{i}", [128, D], F32) for i in range(NYO)]
    yoB = [[Buf(f"yo{i}_{hf}") for hf in range(2)] for i in range(NYO)]
    for tb in range(NB):
        if stage >= 3:
            ri = norm_rs(xT[:, :, tbs(tb)], 8, D, [XB[c][tb] for c in range(8)])
            for c in range(8):
                stt(xT[:, c, tbs(tb)], xT[:, c, tbs(tb)], gT[:, GO["fin"] + c:GO["fin"] + c + 1], rs[ri],
                    ALU.mult, ALU.mult, [XB[c][tb], gTB, rsB[ri]], [XB[c][tb]])
        for it in range(4):
            i = tb * 4 + it
            yi = i % NYO
            for half in range(2):
                b = rot.next()

                def trf(e, half=half, b=b, i=i):
                    for j in range(4):
                        c = half * 4 + j
                        ins = e.transpose(banks[b][:, j * 128:(j + 1) * 128], xT[:, c, i * 128:(i + 1) * 128], ident)
                    return ins
                P.op("pe", trf, reads=[XB[half * 4 + j][tb] for j in range(4)] + [cstB], writes=[BK[b]])
                if half == 0:
                    act(yo[yi][:, 0:512], banks[b][:, :], AF.Copy, [BK[b]], [yoB[yi][0]])
                else:
                    cp(yo[yi][:, 512:1024], banks[b][:, :], [BK[b]], [yoB[yi][1]])
            dma("sp", f"c_yo{yi}", y_d[i * 128:(i + 1) * 128, :], yo[yi], reads=yoB[yi])
    P.barrier(engines=("sp",))
    with nc.Block() as block:
        P.emit(block)
    es.close()
    return nc


_CACHE = {}


def _consts():
    c = np.zeros((128, C_W), np.float32)
    c[:, C_ID:C_ID + 128] = np.eye(128, dtype=np.float32)
    inv = (np.float32(10000.0) ** (-np.arange(0, 64, 2, dtype=np.float32) / np.float32(64))).astype(np.float32)
    for p in range(128):
        j = p % 32
        c[p, C_FC] = inv[j]
        c[p, C_FS] = -inv[j] if (p % 64) < 32 else inv[j]
    c[:, C_HPI] = np.pi / 2
    for g, w in enumerate(WINS):
        left = w // 2
        right = w - 1 - left
        for j in range(8):
            for (t, col) in ((j, j), (T - 8 + j, 8 + j)):
                lo = max(t - left, 0)
                hi = min(t + right + 1, T)
                c[:, C_EDGE + g * 16 + col] = 1.0 / (hi - lo)
    return c


def kernel(**inputs):
    stage = int(inputs.pop("_stage", 3)) if "_stage" in inputs else 3
    if stage not in _CACHE:
        _CACHE[stage] = build(stage)
    nc = _CACHE[stage]
    f = lambda k: np.ascontiguousarray(np.asarray(inputs[k], dtype=np.float32)[0])
    x = np.asarray(inputs["x"], dtype=np.float32)
    pos = np.asarray(inputs["positions"]).astype(np.int32)
    gnames = ["ffn1_pre_g", "ffn1_post_g", "mix_pre_g", "q_a_norm_g", "kv_a_norm_g", "pool_scale",
              "mix_post_g", "ffn2_pre_g", "ffn2_post_g", "final_g"]
    gstack = np.ascontiguousarray(np.concatenate([f(k).reshape(-1, 128) for k in gnames], axis=0))
    shared = {
        "cst": _consts(), "gstack": gstack,
        "ffn1_w_gate": f("ffn1_w_gate"), "ffn1_w_up": f("ffn1_w_up"), "ffn1_w_down": f("ffn1_w_down"),
        "ffn2_w_gate": f("ffn2_w_gate"), "ffn2_w_up": f("ffn2_w_up"), "ffn2_w_down": f("ffn2_w_down"),
        "w_in": f("w_in"), "w_uq": f("w_uq"), "w_uk": f("w_uk"), "w_uv": f("w_uv"), "w_o_attn": f("w_o_attn"),
        "pool_w": f("pool_w"), "w_o_pool": f("w_o_pool"), "w_out": f("w_out"),
    }
    in_maps = []
    for b in range(8):
        m = dict(shared)
        m["x"] = np.ascontiguousarray(x[b])
        m["pos"] = np.ascontiguousarray(np.broadcast_to(pos[b][None, :], (128, T)))
        in_maps.append(m)
    res = run_bass_kernel_spmd(nc, in_maps, core_ids=list(range(8)))
    return np.stack([np.asarray(r["y"], dtype=np.float32) for r in res.results], axis=0)
```

```python
import numpy as np
import concourse.bass as bass
import concourse.mybir as mybir
from concourse.bass_utils import run_bass_kernel_spmd

F32 = mybir.dt.float32
BF16 = mybir.dt.bfloat16
I32 = mybir.dt.int32
U8 = mybir.dt.uint8
AF = mybir.ActivationFunctionType
ALU = mybir.AluOpType

ENGS = ("pe", "act", "dve", "pool", "sp")
_ENG_ATTR = {"pe": "tensor", "act": "scalar", "dve": "vector", "pool": "gpsimd", "sp": "sync"}

T = 2048
D = 1024
FF = 2816
NFF = 22
H = 8
EPS = 1e-6
NB = 4
TB = 512
IN_W = 3264
G_ROWS = 65
GO = dict(f1pre=0, f1post=8, mpre=16, gq=24, gkv=27, psc=29, mpost=33, f2pre=41, f2post=49, fin=57)
C_ID = 0
C_FC = 128
C_FS = 129
C_HPI = 130
C_EDGE = 132
C_W = C_EDGE + 64
WINS = (2, 4, 8, 16)


class Buf:
    __slots__ = ("name", "w", "r", "excl")

    def __init__(self, name, excl=False):
        self.name = name
        self.w = None
        self.r = {}
        self.excl = excl


class Op:
    __slots__ = ("stream", "idx", "sig", "count")

    def __init__(self, stream, idx):
        self.stream = stream
        self.idx = idx
        self.sig = False
        self.count = None


class Prog:
    def __init__(self, nc, eng_sems, dma_sems):
        self.nc = nc
        self.semobj = dict(zip(ENGS, eng_sems))
        self.nidx = {e: 0 for e in ENGS}
        self.ops = {e: [] for e in ENGS}
        self.known = {e: {} for e in ENGS}
        self.free_dma_sems = list(dma_sems)
        self.chain_last = {}
        self.last = {}

    def _need(self, eng, deps, op, skip_same):
        if op is None:
            return
        if skip_same and op.stream == eng:
            return
        if self.known[eng].get(op.stream, 0) >= op.idx:
            return
        cur = deps.get(op.stream)
        if cur is None or cur.idx < op.idx:
            deps[op.stream] = op

    def _collect(self, eng, reads, writes):
        deps = {}
        writes = list(writes)
        for b in reads:
            if b.excl:
                writes.append(b)
                continue
            self._need(eng, deps, b.w, False)
        for b in writes:
            self._need(eng, deps, b.w, True)
            for o in b.r.values():
                self._need(eng, deps, o, True)
        for s, o in deps.items():
            self.known[eng][s] = o.idx
            o.sig = True
        return list(deps.values())

    def _mark(self, op, reads, writes):
        for b in reads:
            if b.excl:
                b.w = op
                b.r = {}
            else:
                b.r[op.stream] = op
        for b in writes:
            b.w = op
            b.r = {}
        self.last[op.stream] = op

    def op(self, eng, fn, reads=(), writes=()):
        deps = self._collect(eng, reads, writes)
        self.nidx[eng] += 1
        o = Op(eng, self.nidx[eng])
        self._mark(o, reads, writes)
        self.ops[eng].append((deps, fn, o))
        return o

    def dma(self, eng, chain, fn, reads=(), writes=()):
        if chain not in self.semobj:
            self.semobj[chain] = self.free_dma_sems.pop()
            self.nidx[chain] = 0
        deps = self._collect(eng, reads, writes)
        last = self.chain_last.get(chain)
        if last is not None and self.known[eng].get(chain, 0) < last.idx:
            deps = [d for d in deps if d.stream != chain] + [last]
            self.known[eng][chain] = last.idx
        self.nidx[chain] += 1
        o = Op(chain, self.nidx[chain])
        o.sig = True
        self.chain_last[chain] = o
        self._mark(o, reads, writes)
        self.ops[eng].append((deps, fn, o))
        return o

    def barrier(self, engines=ENGS):
        lasts = list(self.last.values())
        for e in engines:
            deps = {}
            for o in lasts:
                self._need(e, deps, o, True)
            for s, o in deps.items():
                self.known[e][s] = o.idx
                o.sig = True
            if deps:
                self.ops[e].append((list(deps.values()), None, None))

    def emit(self, block):
        for e in ENGS:
            c = 0
            for deps, fn, o in self.ops[e]:
                if o is None:
                    continue
                if o.stream == e:
                    if o.sig:
                        c += 1
                        o.count = c
                else:
                    o.count = 16 * o.idx
        for e in ENGS:
            ops = self.ops[e]
            if not ops:
                continue

            def body(engine, ops=ops, e=e):
                for deps, fn, o in ops:
                    for d in deps:
                        engine.wait_ge(self.semobj[d.stream], d.count)
                    if fn is None:
                        continue
                    ins = fn(engine)
                    if o.stream != e:
                        ins.then_inc(self.semobj[o.stream], 16)
                    elif o.sig:
                        ins.then_inc(self.semobj[e], 1)

            getattr(block, _ENG_ATTR[e])(body)


class Arena:
    def __init__(self, tensor, nbytes):
        self.t = tensor
        self.free = [(0, nbytes)]
        self.used = {}

    def alloc(self, name, shape, dt, top=False):
        esz = {F32: 4, BF16: 2, I32: 4}[dt]
        n = esz
        for s in shape[1:]:
            n *= s
        nb = (n + 63) // 64 * 64
        order = list(enumerate(self.free))
        if top:
            order = order[::-1]
        for i, (o, sz) in order:
            if sz >= nb:
                if sz == nb:
                    self.free.pop(i)
                elif top:
                    self.free[i] = (o, sz - nb)
                    o = o + sz - nb
                else:
                    self.free[i] = (o + nb, sz - nb)
                self.used[name] = (o, nb)
                v = self.t[0:shape[0], o:o + n].bitcast(dt)
                if len(shape) == 3:
                    v = v.rearrange("p (a b) -> p a b", a=shape[1])
                return v
        raise RuntimeError(f"arena full for {name} {shape} free={self.free}")

    def release(self, *names):
        for name in names:
            o, nb = self.used.pop(name)
            self.free.append((o, nb))
        self.free.sort()
        m = []
        for o, sz in self.free:
            if m and m[-1][0] + m[-1][1] == o:
                m[-1] = (m[-1][0], m[-1][1] + sz)
            else:
                m.append((o, sz))
        self.free = m


class Rot:
    def __init__(self, ids):
        self.ids = list(ids)
        self.i = 0

    def next(self):
        v = self.ids[self.i % len(self.ids)]
        self.i += 1
        return v


from contextlib import ExitStack
import math

SCALE = 1.0 / math.sqrt(192.0)
TWO_PI = 2.0 * math.pi


def build(stage=3):
    nc = bass.Bass("TRN2", target_bir_lowering=False)

    def din(name, shape, dt=F32):
        return nc.dram_tensor(name, list(shape), dt, kind="ExternalInput").ap()

    x_d = din("x", [T, D])
    pos_d = din("pos", [128, T], I32)
    cst_d = din("cst", [128, C_W])
    g_d = din("gstack", [G_ROWS, 128])
    W = {}
    for f in ("ffn1", "ffn2"):
        W[f + "_g"] = din(f + "_w_gate", [D, FF])
        W[f + "_u"] = din(f + "_w_up", [D, FF])
        W[f + "_d"] = din(f + "_w_down", [FF, D])
    w_in_d = din("w_in", [D, IN_W])
    w_uq_d = din("w_uq", [384, 1536])
    w_uk_d = din("w_uk", [256, 1024])
    w_uv_d = din("w_uv", [256, 1024])
    w_oa_d = din("w_o_attn", [1024, 1024])
    pw_d = din("pool_w", [4, 128, 128])
    w_op_d = din("w_o_pool", [512, 1024])
    w_out_d = din("w_out", [1024, 1024])
    y_d = nc.dram_tensor("y", [T, D], F32, kind="ExternalOutput").ap()

    es = ExitStack()
    ARENA_BYTES = 206 * 1024
    arena_t = es.enter_context(nc.sbuf_tensor("arena", [128, ARENA_BYTES], U8))
    A = Arena(arena_t, ARENA_BYTES)
    banks = [es.enter_context(nc.psum_tensor(f"bk{i}", [128, 512], F32)) for i in range(8)]
    BK = [Buf(f"bk{i}", excl=True) for i in range(8)]
    eng_sems = [es.enter_context(nc.semaphore(f"s_{e}")) for e in ENGS]
    dma_sems = [es.enter_context(nc.semaphore(f"d{i}")) for i in range(48)]
    P = Prog(nc, eng_sems, dma_sems)
    rot = Rot(range(8))

    def kc_rows(ap2d):
        return ap2d.rearrange("(c p) n -> p c n", p=128)

    def tbs(tb):
        return slice(tb * TB, (tb + 1) * TB)

    def mm(bank, c0, n, pairs, reads):
        out = banks[bank][:, c0:c0 + n]

        def fn(e, pairs=pairs, out=out):
            k = len(pairs)
            for i, (l, r) in enumerate(pairs):
                ins = e.matmul(out, l, r, start=(i == 0), stop=(i == k - 1))
            return ins
        return P.op("pe", fn, reads=reads, writes=[BK[bank]])

    def act(out, in_, func, reads, writes, **kw):
        return P.op("act", lambda e: e.activation(out=out, in_=in_, func=func, **kw), reads=reads, writes=writes)

    def tt(out, in0, in1, op, reads, writes, eng="dve"):
        return P.op(eng, lambda e: e.tensor_tensor(out=out, in0=in0, in1=in1, op=op), reads=reads, writes=writes)

    def stt(out, in0, scalar, in1, op0, op1, reads, writes, eng="dve"):
        return P.op(eng, lambda e: e.scalar_tensor_tensor(out=out, in0=in0, scalar=scalar, in1=in1, op0=op0, op1=op1),
                    reads=reads, writes=writes)

    def ts(out, in0, s1, s2, op0, op1, reads, writes, eng="dve"):
        if s2 is None:
            return P.op(eng, lambda e: e.tensor_scalar(out=out, in0=in0, scalar1=s1, scalar2=None, op0=op0),
                        reads=reads, writes=writes)
        return P.op(eng, lambda e: e.tensor_scalar(out=out, in0=in0, scalar1=s1, scalar2=s2, op0=op0, op1=op1),
                    reads=reads, writes=writes)

    def cp(out, in_, reads, writes, eng="dve"):
        return P.op(eng, lambda e: e.tensor_copy(out=out, in_=in_), reads=reads, writes=writes)

    def dma(q, chain, out, in_, reads=(), writes=()):
        return P.dma(q, chain, lambda e: e.dma_start(out=out, in_=in_), reads=reads, writes=writes)

    xT = A.alloc("xT", [128, 8, T], F32)
    XB = [[Buf(f"x{c}_{tb}") for tb in range(NB)] for c in range(8)]
    cst = A.alloc("cst", [128, C_W], F32); cstB = Buf("cst")
    gT = A.alloc("gT", [128, G_ROWS], F32); gTB = Buf("gT")
    gh = A.alloc("gh", [128, G_ROWS], F32); ghB = Buf("gh")
    ones = A.alloc("ones", [128, 128], BF16); onesB = Buf("ones")
    epsb = A.alloc("epsb", [128, 1], F32); epsB = Buf("eps")
    sq = A.alloc("sq", [128, 8, TB], BF16); sqB = Buf("sq")
    rs = [A.alloc(f"rs{i}", [128, TB], F32) for i in range(2)]; rsB = [Buf(f"rs{i}") for i in range(2)]
    sg = [A.alloc(f"sg{i}", [128, TB], F32) for i in range(2)]; sgB = [Buf(f"sg{i}") for i in range(2)]
    ctr = {"rs": 0, "sg": 0}
    ident = cst[:, C_ID:C_ID + 128]

    def norm_rs(src3, nch, Dn, reads):
        act(sq[:, 0:nch, :], src3, AF.Square, reads, [sqB])
        b = rot.next()
        mm(b, 0, TB, [(ones, sq[:, c, :]) for c in range(nch)], [onesB, sqB])
        ri = ctr["rs"] % 2
        ctr["rs"] += 1
        act(rs[ri], banks[b][:, :], AF.Ln, [BK[b], epsB], [rsB[ri]], bias=epsb[:, 0:1], scale=1.0 / Dn)
        act(rs[ri], rs[ri], AF.Exp, [rsB[ri]], [rsB[ri]], scale=-0.5)
        return ri

    def next_sg():
        i = ctr["sg"] % 2
        ctr["sg"] += 1
        return i

    dma("sp", "c_cst", cst, cst_d[:, :], writes=[cstB])
    gsb = A.alloc("gsb", [128, 128], F32, top=True); gsbB = Buf("gsb")
    dma("sp", "c_g", gsb[0:G_ROWS, :], g_d[:, :], writes=[gsbB])
    P.op("dve", lambda e: e.memset(ones, 1.0), writes=[onesB])
    P.op("dve", lambda e: e.memset(epsb, EPS), writes=[epsB])
    P.op("pe", lambda e: e.transpose(banks[0][:, 0:G_ROWS], gsb[0:G_ROWS, :], cst[0:G_ROWS, 0:G_ROWS]),
         reads=[gsbB, cstB], writes=[BK[0]])
    act(gT, banks[0][:, 0:G_ROWS], AF.Copy, [BK[0]], [gTB])
    P.op("act", lambda e: e.mul(out=gh, in_=gT, mul=0.5), reads=[gTB], writes=[ghB])

    xin = [A.alloc(f"xin{i}", [128, D], F32, top=True) for i in range(3)]
    xinB = [Buf(f"xin{i}") for i in range(3)]
    for i in range(16):
        xi = i % 3
        dma("sp", f"c_xin{xi}", xin[xi], x_d[i * 128:(i + 1) * 128, :], writes=[xinB[xi]])
        for half in range(2):
            b = rot.next()

            def trf(e, xi=xi, half=half, b=b):
                for j in range(4):
                    c = half * 4 + j
                    ins = e.transpose(banks[b][:, j * 128:(j + 1) * 128], xin[xi][:, c * 128:(c + 1) * 128], ident)
                return ins
            P.op("pe", trf, reads=[xinB[xi], cstB], writes=[BK[b]])
            act(xT[:, half * 4:half * 4 + 4, i * 128:(i + 1) * 128],
                banks[b][:, :].rearrange("p (c t) -> p c t", c=4), AF.Copy,
                [BK[b]], [XB[half * 4 + j][i // 4] for j in range(4)])
    A.release("gsb", "xin0", "xin1", "xin2")

    def ffn(f, go_pre, go_post):
        hT = A.alloc("hT", [128, 8, 1024], BF16)
        HB = [[Buf(f"h{c}_{t}") for t in range(2)] for c in range(8)]
        aT = A.alloc("aT", [128, NFF, 1024], BF16)
        AB = [[Buf(f"a{j}_{t}") for t in range(2)] for j in range(NFF)]
        f1T = A.alloc("f1T", [128, 8, 1024], F32, top=True)
        FB = [[Buf(f"f{c}_{t}") for t in range(2)] for c in range(8)]
        wg = [A.alloc(f"wg{i}", [128, 8, 256], BF16) for i in range(2)]; wgB = [Buf(f"wg{i}") for i in range(2)]
        wu = [A.alloc(f"wu{i}", [128, 8, 256], BF16) for i in range(2)]; wuB = [Buf(f"wu{i}") for i in range(2)]
        wdn = [A.alloc(f"wdn{i}", [128, NFF, 128], BF16) for i in range(2)]; wdB = [Buf(f"wdn{i}") for i in range(2)]
        Wg, Wu, Wd = W[f + "_g"], W[f + "_u"], W[f + "_d"]
        cnt = {"g": 0, "d": 0}
        bgq = []

        def Nn(hh):
            for tl in range(2):
                tb = 2 * hh + tl
                ri = norm_rs(xT[:, :, tbs(tb)], 8, D, [XB[c][tb] for c in range(8)])
                for c in range(8):
                    stt(hT[:, c, tbs(tl)], xT[:, c, tbs(tb)], gT[:, go_pre + c:go_pre + c + 1], rs[ri],
                        ALU.mult, ALU.mult, [XB[c][tb], gTB, rsB[ri]], [HB[c][tl]])

        def GU(hh):
            for grp in range(11):
                bi = cnt["g"] % 2
                cnt["g"] += 1
                dma("pool", f"c_wg{bi}", wg[bi], kc_rows(Wg[:, grp * 256:(grp + 1) * 256]), writes=[wgB[bi]])
                dma("pool", f"c_wu{bi}", wu[bi], kc_rows(Wu[:, grp * 256:(grp + 1) * 256]), writes=[wuB[bi]])
                for j2 in range(2):
                    j = grp * 2 + j2
                    for tl in range(2):
                        bg = rot.next()
                        bu = rot.next()
                        hr = [HB[kc][tl] for kc in range(8)]
                        mm(bg, 0, TB, [(wg[bi][:, kc, j2 * 128:(j2 + 1) * 128], hT[:, kc, tbs(tl)]) for kc in range(8)],
                           [wgB[bi]] + hr)
                        mm(bu, 0, TB, [(wu[bi][:, kc, j2 * 128:(j2 + 1) * 128], hT[:, kc, tbs(tl)]) for kc in range(8)],
                           [wuB[bi]] + hr)
                        si = next_sg()
                        act(sg[si], banks[bg][:, :], AF.Silu, [BK[bg]], [sgB[si]])
                        tt(aT[:, j, tbs(tl)], sg[si], banks[bu][:, :], ALU.mult, [sgB[si], BK[bu]], [AB[j][tl]])
                        if bgq:
                            bgq.pop(0)()
            while bgq:
                bgq.pop(0)()

        def Dd(hh):
            for dc in range(8):
                bi = cnt["d"] % 2
                cnt["d"] += 1
                dma("pool", f"c_wd{bi}", wdn[bi], kc_rows(Wd[:, dc * 128:(dc + 1) * 128]), writes=[wdB[bi]])
                for tl in range(2):
                    b = rot.next()
                    mm(b, 0, TB, [(wdn[bi][:, j, :], aT[:, j, tbs(tl)]) for j in range(NFF)],
                       [wdB[bi]] + [AB[j][tl] for j in range(NFF)])
                    act(f1T[:, dc, tbs(tl)], banks[b][:, :], AF.Copy, [BK[b]], [FB[dc][tl]])

        def Pp(hh):
            th = []
            for tl in range(2):
                tb = 2 * hh + tl
                st = {}

                def t_norm(tl=tl, st=st):
                    st["ri"] = norm_rs(f1T[:, :, tbs(tl)], 8, D, [FB[c][tl] for c in range(8)])
                th.append(t_norm)
                for c in range(8):
                    def t_c(tl=tl, tb=tb, c=c, st=st, hh=hh):
                        ri = st["ri"]
                        stt(f1T[:, c, tbs(tl)], f1T[:, c, tbs(tl)], gh[:, go_post + c:go_post + c + 1], rs[ri],
                            ALU.mult, ALU.mult, [FB[c][tl], ghB, rsB[ri]], [FB[c][tl]])
                        tt(xT[:, c, tbs(tb)], f1T[:, c, tbs(tl)], xT[:, c, tbs(tb)], ALU.add,
                           [FB[c][tl], XB[c][tb]], [XB[c][tb]], eng=("dve" if hh == 0 else "pool"))
                    th.append(t_c)
            return th

        Nn(0)
        GU(0)
        Nn(1)
        Dd(0)
        bgq.extend(Pp(0))
        GU(1)
        Dd(1)
        for t in Pp(1):
            t()
        P.barrier()
        A.release("hT", "aT", "f1T", "wg0", "wg1", "wu0", "wu1", "wdn0", "wdn1")

    if stage >= 1:
        ffn("ffn1", GO["f1pre"], GO["f1post"])

    def make_uT(uT, UB, tblist=range(NB), off=0):
        for tb in tblist:
            ri = norm_rs(xT[:, :, tbs(tb)], 8, D, [XB[c][tb] for c in range(8)])
            for c in range(8):
                stt(uT[:, c, tbs(tb - off)], xT[:, c, tbs(tb)], gT[:, GO["mpre"] + c:GO["mpre"] + c + 1], rs[ri],
                    ALU.mult, ALU.mult, [XB[c][tb], gTB, rsB[ri]], [UB[c][tb - off]])

    def mixer():
        cosT = A.alloc("cosT", [128, T], F32, top=True); cosB = Buf("cos")
        sinT = A.alloc("sinT", [128, T], F32, top=True); sinB = Buf("sin")
        cqn = A.alloc("cqn", [128, 3, T], BF16, top=True)
        CQB = [[Buf(f"cq{c}_{t}") for t in range(NB)] for c in range(3)]
        ckvn = A.alloc("ckvn", [128, 2, T], BF16, top=True)
        CKB = [[Buf(f"ck{c}_{t}") for t in range(NB)] for c in range(2)]
        krT = A.alloc("krT", [128, T], BF16, top=True)
        KRB = [Buf(f"kr{t}") for t in range(NB)]
        posi = A.alloc("posi", [128, T], I32); posiB = Buf("posi")
        kf = posi.bitcast(F32)
        posf = A.alloc("posf", [128, T], F32); posfB = Buf("posf")
        ang = A.alloc("ang", [128, T], F32); angB = Buf("ang")
        uT = A.alloc("uT", [128, 8, T], BF16)
        UB = [[Buf(f"u{c}_{t}") for t in range(NB)] for c in range(8)]
        win = A.alloc("win", [128, 8, 768], BF16); winB = Buf("win")
        wkA = A.alloc("wkA", [128, 8, 128], BF16); wkAB = Buf("wkA")
        wkB = A.alloc("wkB", [128, 8, 128], BF16); wkBB = Buf("wkB")
        cqr = A.alloc("cqr", [128, 3, TB], F32); cqrB = [Buf(f"cqr{c}") for c in range(3)]
        ckr = A.alloc("ckr", [128, 2, TB], F32); ckrB = [Buf(f"ckr{c}") for c in range(2)]
        dma("pool", "c_win", win, kc_rows(w_in_d[:, 0:768]), writes=[winB])
        dma("sp", "c_pos", posi, pos_d[:, :], writes=[posiB])
        tabq = []
        tabq.append(lambda: cp(posf, posi, [posiB], [posfB]))
        tabq.append(lambda: ts(ang, posf, cst[:, C_FC:C_FC + 1], math.pi / 2, ALU.mult, ALU.add, [posfB, cstB], [angB]))
        tabq.append(lambda: ts(posf, posf, cst[:, C_FS:C_FS + 1], None, ALU.mult, None, [posfB, cstB], [posfB]))
        angs = []
        for ti in range(2):
            a = ang if ti == 0 else posf
            aB = angB if ti == 0 else posfB
            tabq.append(lambda a=a, aB=aB: ts(posi, a, 1.0 / TWO_PI, None, ALU.mult, None, [aB], [posiB]))
            tabq.append(lambda a=a, aB=aB: cp(kf, posi, [posiB], [posiB]))
            tabq.append(lambda a=a, aB=aB: stt(a, kf, -TWO_PI, a, ALU.mult, ALU.add, [posiB, aB], [aB]))
            tabq.append(lambda a=a, aB=aB: ts(a, a, 3.14159, -3.14159, ALU.min, ALU.max, [aB], [aB]))
            angs.append((a, aB))
        make_uT(uT, UB)
        for d0 in (0, 64):
            cp(wkA[:, :, d0:d0 + 64], win[:, :, 640:704], [winB], [wkAB])
            cp(wkB[:, :, d0:d0 + 32], win[:, :, 672:704], [winB], [wkBB])
            cp(wkB[:, :, d0 + 32:d0 + 64], win[:, :, 640:672], [winB], [wkBB])
        for tb in range(NB):
            ur = [UB[kc][tb] for kc in range(8)]
            for c in range(3):
                b = rot.next()
                mm(b, 0, TB, [(win[:, kc, c * 128:(c + 1) * 128], uT[:, kc, tbs(tb)]) for kc in range(8)], [winB] + ur)
                act(cqr[:, c, :], banks[b][:, :], AF.Copy, [BK[b]], [cqrB[c]])
                if tabq:
                    tabq.pop(0)()
            for c in range(2):
                b = rot.next()
                mm(b, 0, TB, [(win[:, kc, 384 + c * 128:384 + (c + 1) * 128], uT[:, kc, tbs(tb)]) for kc in range(8)],
                   [winB] + ur)
                act(ckr[:, c, :], banks[b][:, :], AF.Copy, [BK[b]], [ckrB[c]])
                if tabq:
                    tabq.pop(0)()
            ri = norm_rs(cqr[:, :, :], 3, 384, cqrB)
            for c in range(3):
                stt(cqn[:, c, tbs(tb)], cqr[:, c, :], gT[:, GO["gq"] + c:GO["gq"] + c + 1], rs[ri],
                    ALU.mult, ALU.mult, [cqrB[c], gTB, rsB[ri]], [CQB[c][tb]])
            ri = norm_rs(ckr[:, :, :], 2, 256, ckrB)
            for c in range(2):
                stt(ckvn[:, c, tbs(tb)], ckr[:, c, :], gT[:, GO["gkv"] + c:GO["gkv"] + c + 1], rs[ri],
                    ALU.mult, ALU.mult, [ckrB[c], gTB, rsB[ri]], [CKB[c][tb]])
        while tabq:
            tabq.pop(0)()
        act(cosT, angs[0][0], AF.Sin, [angs[0][1]], [cosB])
        act(sinT, angs[1][0], AF.Sin, [angs[1][1]], [sinB])
        for tb in range(NB):
            ur = [UB[kc][tb] for kc in range(8)]
            ba = rot.next()
            mm(ba, 0, TB, [(wkA[:, kc, :], uT[:, kc, tbs(tb)]) for kc in range(8)], [wkAB] + ur)
            bb = rot.next()
            mm(bb, 0, TB, [(wkB[:, kc, :], uT[:, kc, tbs(tb)]) for kc in range(8)], [wkBB] + ur)
            s0 = next_sg()
            tt(sg[s0], banks[ba][:, :], cosT[:, tbs(tb)], ALU.mult, [BK[ba], cosB], [sgB[s0]])
            s1 = next_sg()
            tt(sg[s1], banks[bb][:, :], sinT[:, tbs(tb)], ALU.mult, [BK[bb], sinB], [sgB[s1]])
            tt(krT[:, tbs(tb)], sg[s0], sg[s1], ALU.add, [sgB[s0], sgB[s1]], [KRB[tb]])
        P.barrier()
        A.release("uT", "posi", "posf", "ang", "win", "wkA", "wkB", "cqr", "ckr")

        wq = A.alloc("wq", [128, 3, 1536], BF16); wqB_ = Buf("wq")
        dma("pool", "c_wq", wq, kc_rows(w_uq_d[:, :]), writes=[wqB_])
        wqa = A.alloc("wqa", [128, 3, 512], BF16); wqaB = Buf("wqa")
        wqb = A.alloc("wqb", [128, 3, 512], BF16); wqbB = Buf("wqb")
        qrT = A.alloc("qrT", [128, 4, T], BF16, top=True)
        QRB = [[Buf(f"qr{j}_{t}") for t in range(NB)] for j in range(4)]
        for kc in range(3):
            src = wq[:, kc, :].rearrange("p (h d) -> p h d", h=8)
            cp(wqa[:, kc, :].rearrange("p (h d) -> p h d", h=8), src[:, :, 128:192], [wqB_], [wqaB])
            dstb = wqb[:, kc, :].rearrange("p (h d) -> p h d", h=8)
            cp(dstb[:, :, 0:32], src[:, :, 160:192], [wqB_], [wqbB])
            cp(dstb[:, :, 32:64], src[:, :, 128:160], [wqB_], [wqbB])
        for j in range(4):
            for tb in range(NB):
                cr = [CQB[kc][tb] for kc in range(3)]
                ba = rot.next()
                mm(ba, 0, TB, [(wqa[:, kc, j * 128:(j + 1) * 128], cqn[:, kc, tbs(tb)]) for kc in range(3)], [wqaB] + cr)
                bb = rot.next()
                mm(bb, 0, TB, [(wqb[:, kc, j * 128:(j + 1) * 128], cqn[:, kc, tbs(tb)]) for kc in range(3)], [wqbB] + cr)
                s0 = next_sg()
                tt(sg[s0], banks[ba][:, :], cosT[:, tbs(tb)], ALU.mult, [BK[ba], cosB], [sgB[s0]])
                s1 = next_sg()
                tt(sg[s1], banks[bb][:, :], sinT[:, tbs(tb)], ALU.mult, [BK[bb], sinB], [sgB[s1]])
                tt(qrT[:, j, tbs(tb)], sg[s0], sg[s1], ALU.add, [sgB[s0], sgB[s1]], [QRB[j][tb]])
        P.barrier()
        A.release("cosT", "sinT", "wqa", "wqb")

        attnT = A.alloc("attnT", [128, 8, T], BF16, top=True)
        ATB = [[Buf(f"at{h}_{t}") for t in range(NB)] for h in range(H)]
        wk = A.alloc("wk", [128, 2, 1024], BF16); wkB_ = Buf("wk")
        wv = A.alloc("wv", [128, 2, 1024], BF16); wvB_ = Buf("wv")
        dma("pool", "c_wk", wk, kc_rows(w_uk_d[:, :]), writes=[wkB_])
        dma("pool", "c_wv", wv, kc_rows(w_uv_d[:, :]), writes=[wvB_])
        qn = [A.alloc(f"qn{i}", [128, T], BF16) for i in range(2)]
        QNB = [[Buf(f"qn{i}_{t}") for t in range(NB)] for i in range(2)]
        kn = [A.alloc(f"kn{i}", [128, T], BF16) for i in range(2)]
        KNB = [[Buf(f"kn{i}_{t}") for t in range(NB)] for i in range(2)]
        Vh = [A.alloc(f"vh{i}", [128, 16, 128], BF16) for i in range(2)]
        VB = [[Buf(f"vh{i}_{g}") for g in range(4)] for i in range(2)]
        NPT = 6
        PT = [A.alloc(f"pt{i}", [128, TB], BF16) for i in range(NPT)]
        PTB = [Buf(f"pt{i}") for i in range(NPT)]
        accs = [[sg[0], sg[1]], [rs[0], rs[1]]]
        accB = [[Buf("accA0"), Buf("accB0")], [Buf("accA1"), Buf("accB1")]]
        accS = [sq[:, 0, :], sq[:, 1, :]]
        accSB = [Buf("accS0"), Buf("accS1")]
        rcp = [A.alloc(f"rcp{i}", [128, TB], F32) for i in range(2)]
        rcpB = [Buf(f"rcp{i}") for i in range(2)]
        strot = Rot([0, 1, 2, 3])

        def gen(h):
            hb = h % 2
            for tb in range(NB):
                b = strot.next()
                mm(b, 0, TB, [(wq[:, kc, h * 192:h * 192 + 128], cqn[:, kc, tbs(tb)]) for kc in range(3)],
                   [wqB_] + [CQB[kc][tb] for kc in range(3)])
                act(qn[hb][:, tbs(tb)], banks[b][:, :], AF.Copy, [BK[b]], [QNB[hb][tb]])
            for tb in range(NB):
                b = strot.next()
                mm(b, 0, TB, [(wk[:, kc, h * 128:(h + 1) * 128], ckvn[:, kc, tbs(tb)]) for kc in range(2)],
                   [wkB_] + [CKB[kc][tb] for kc in range(2)])
                act(kn[hb][:, tbs(tb)], banks[b][:, :], AF.Copy, [BK[b]], [KNB[hb][tb]])
            for g4 in range(4):
                b = strot.next()

                def vfn(e, g4=g4, b=b, h=h):
                    for j in range(4):
                        tile = g4 * 4 + j
                        for kc in range(2):
                            ins = e.matmul(banks[b][:, j * 128:(j + 1) * 128],
                                           ckvn[:, kc, tile * 128:(tile + 1) * 128],
                                           wv[:, kc, h * 128:(h + 1) * 128], start=(kc == 0), stop=(kc == 1))
                    return ins
                P.op("pe", vfn, reads=[wvB_] + [CKB[kc][g4] for kc in range(2)], writes=[BK[b]])
                cp(Vh[hb][:, g4 * 4:(g4 + 1) * 4, :], banks[b][:, :].rearrange("p (j d) -> p j d", j=4),
                   [BK[b]], [VB[hb][g4]])

        steps = [(h, qb, kc) for h in range(H) for qb in range(NB) for kc in range(16)]
        stbank = {}
        ptc = 0

        def emit_st(s):
            h, qb, kc = steps[s]
            hb = h % 2
            po = 64 * (h % 2)
            b = strot.next()
            stbank[s] = b
            mm(b, 0, TB, [(kn[hb][:, kc * 128:(kc + 1) * 128], qn[hb][:, tbs(qb)]),
                          (krT[po:po + 64, kc * 128:(kc + 1) * 128], qrT[po:po + 64, h // 2, tbs(qb)])],
               [KNB[hb][kc // 4], QNB[hb][qb], KRB[kc // 4], QRB[h // 2][qb]])

        def finalize(grp):
            h, qb = grp // NB, grp % NB
            g2 = grp % 2
            ob = 4 + g2
            sb_ = 6 + g2
            tt(accS[g2], accs[g2][0], accs[g2][1], ALU.add, accB[g2], [accSB[g2]])
            mm(sb_, 0, TB, [(ones, accS[g2])], [onesB, accSB[g2]])
            act(rcp[g2], banks[sb_][:, :], AF.Ln, [BK[sb_]], [rcpB[g2]])
            act(rcp[g2], rcp[g2], AF.Exp, [rcpB[g2]], [rcpB[g2]], scale=-1.0)
            tt(attnT[:, h, tbs(qb)], banks[ob][:, :], rcp[g2], ALU.mult, [BK[ob], rcpB[g2]], [ATB[h][qb]])

        gen(0)
        emit_st(0)
        emit_st(1)
        for s, (h, qb, kc) in enumerate(steps):
            hb = h % 2
            grp = h * NB + qb
            g2 = grp % 2
            ob = 4 + g2
            if s + 2 < len(steps):
                emit_st(s + 2)
            pi = s % NPT
            b = stbank.pop(s)
            act(PT[pi], banks[b][:, :], AF.Exp, [BK[b]], [PTB[pi]], scale=SCALE)
            P.op("pe", lambda e, hb=hb, kc=kc, pi=pi, ob=ob: e.matmul(
                banks[ob][:, :], Vh[hb][:, kc, :], PT[pi], start=(kc == 0), stop=(kc == 15)),
                reads=[VB[hb][kc // 4], PTB[pi]], writes=[BK[ob]])
            ai = 1 if kc % 3 == 2 else 0
            ae = "pool" if ai else "dve"
            acc = accs[g2][ai]
            aB = accB[g2][ai]
            if kc == 0 or kc == 2:
                cp(acc, PT[pi], [PTB[pi]], [aB], eng=ae)
            else:
                tt(acc, acc, PT[pi], ALU.add, [aB, PTB[pi]], [aB], eng=ae)
            if kc == 3 and grp > 0:
                finalize(grp - 1)
            if kc == 15 and qb == 0 and h + 1 < H:
                gen(h + 1)
        finalize(H * NB - 1)
        P.barrier()
        A.release("cqn", "ckvn", "krT", "wq", "qrT", "wk", "wv", "qn0", "qn1", "kn0", "kn1", "vh0", "vh1",
                  "pt0", "pt1", "pt2", "pt3", "pt4", "pt5", "rcp0", "rcp1")

        uT = A.alloc("uT", [128, 8, T], BF16)
        UB = [[Buf(f"u{c}_{t}") for t in range(NB)] for c in range(8)]
        ypT = A.alloc("ypT", [128, 4, T], BF16, top=True)
        YPB = [[Buf(f"yp{g}_{t}") for t in range(NB)] for g in range(4)]
        wpl = A.alloc("wpl", [128, 8, 512], BF16); wplB = Buf("wpl")
        dma("pool", "c_wpl", wpl, kc_rows(w_in_d[:, 704:1216]), writes=[wplB])
        pwt = A.alloc("pwt", [128, 4, 128], BF16); pwtB = Buf("pwt")
        dma("pool", "c_pwt", pwt, pw_d.rearrange("g c d -> c g d"), writes=[pwtB])
        XP = A.alloc("XP", [128, T + 16], F32); XPB = Buf("XP")
        sA = A.alloc("sA", [128, T + 16], F32); sAB = Buf("sA")
        sBt = A.alloc("sBt", [128, T + 16], F32); sBB = Buf("sBt")
        dT = A.alloc("dT", [128, T], BF16); dTB = Buf("dT")
        te = A.alloc("te", [128, 8], F32); teB = Buf("te")
        make_uT(uT, UB)
        P.op("dve", lambda e: e.memset(XP, 0.0), writes=[XPB])
        L = T + 16
        for g, w in enumerate(WINS):
            for tb in range(NB):
                b = rot.next()
                mm(b, 0, TB, [(wpl[:, kc, g * 128:(g + 1) * 128], uT[:, kc, tbs(tb)]) for kc in range(8)],
                   [wplB] + [UB[kc][tb] for kc in range(8)])
                cp(XP[:, 8 + tb * TB:8 + (tb + 1) * TB], banks[b][:, :], [BK[b]], [XPB])
            tt(sA[:, 0:L - 1], XP[:, 0:L - 1], XP[:, 1:L], ALU.add, [XPB], [sAB])
            tot, totB, off = sA, sAB, 7
            if w >= 4:
                tt(sBt[:, 0:L - 3], sA[:, 0:L - 3], sA[:, 2:L - 1], ALU.add, [sAB], [sBB])
                tot, totB, off = sBt, sBB, 6
            if w >= 8:
                tt(sA[:, 0:L - 7], sBt[:, 0:L - 7], sBt[:, 4:L - 3], ALU.add, [sBB], [sAB])
                tot, totB, off = sA, sAB, 4
            if w >= 16:
                tt(sBt[:, 0:L - 15], sA[:, 0:L - 15], sA[:, 8:L - 7], ALU.add, [sAB], [sBB])
                tot, totB, off = sBt, sBB, 0
            stt(dT, tot[:, off:off + T], 1.0 / w, XP[:, 8:8 + T], ALU.mult, ALU.subtract, [totB, XPB], [dTB])
            for (t0, c0) in ((0, 0), (T - 8, 8)):
                ec = C_EDGE + g * 16 + c0
                tt(te, tot[:, off + t0:off + t0 + 8], cst[:, ec:ec + 8], ALU.mult, [totB, cstB], [teB])
                tt(dT[:, t0:t0 + 8], te, XP[:, 8 + t0:16 + t0], ALU.subtract, [teB, XPB], [dTB])
            for tb in range(NB):
                b = rot.next()
                mm(b, 0, TB, [(pwt[:, g, :], dT[:, tbs(tb)])], [pwtB, dTB])
                act(ypT[:, g, tbs(tb)], banks[b][:, :], AF.Copy, [BK[b]], [YPB[g][tb]],
                    scale=gT[:, GO["psc"] + g:GO["psc"] + g + 1])
        P.barrier()
        A.release("uT", "wpl", "pwt", "XP", "sA", "sBt", "dT", "te")

        mixin = A.alloc("mixin", [128, 8, T], BF16)
        MXB = [[Buf(f"mx{c}_{t}") for t in range(NB)] for c in range(8)]
        woa = [A.alloc(f"woa{i}", [128, 8, 128], BF16) for i in range(2)]; woaB = [Buf(f"woa{i}") for i in range(2)]
        wop = [A.alloc(f"wop{i}", [128, 4, 128], BF16) for i in range(2)]; wopB = [Buf(f"wop{i}") for i in range(2)]
        wga = [A.alloc(f"wga{i}", [128, 8, 128], BF16) for i in range(2)]; wgaB = [Buf(f"wga{i}") for i in range(2)]
        wgp = [A.alloc(f"wgp{i}", [128, 8, 128], BF16) for i in range(2)]; wgpB = [Buf(f"wgp{i}") for i in range(2)]
        sgm = [A.alloc(f"sgm{i}", [128, TB], F32) for i in range(4)]; sgmB = [Buf(f"sgm{i}") for i in range(4)]
        uTh = A.alloc("uTh", [128, 8, 1024], BF16)
        UBh = [[Buf(f"uh{c}_{t}") for t in range(2)] for c in range(8)]
        sc = 0
        wc = 0
        for hh in range(2):
            make_uT(uTh, UBh, [2 * hh, 2 * hh + 1], 2 * hh)
            for dc in range(8):
                bi = wc % 2
                wc += 1
                cs = slice(dc * 128, (dc + 1) * 128)
                dma("pool", f"c_woa{bi}", woa[bi], kc_rows(w_oa_d[:, cs]), writes=[woaB[bi]])
                dma("pool", f"c_wop{bi}", wop[bi], kc_rows(w_op_d[:, cs]), writes=[wopB[bi]])
                dma("pool", f"c_wga{bi}", wga[bi], kc_rows(w_in_d[:, 1216 + dc * 128:1216 + (dc + 1) * 128]), writes=[wgaB[bi]])
                dma("pool", f"c_wgp{bi}", wgp[bi], kc_rows(w_in_d[:, 2240 + dc * 128:2240 + (dc + 1) * 128]), writes=[wgpB[bi]])
                for tl in range(2):
                    tb = 2 * hh + tl
                    ur = [UBh[kc][tl] for kc in range(8)]
                    bya = rot.next()
                    mm(bya, 0, TB, [(woa[bi][:, h, :], attnT[:, h, tbs(tb)]) for h in range(H)],
                       [woaB[bi]] + [ATB[h][tb] for h in range(H)])
                    bga = rot.next()
                    mm(bga, 0, TB, [(wga[bi][:, kc, :], uTh[:, kc, tbs(tl)]) for kc in range(8)], [wgaB[bi]] + ur)
                    byp = rot.next()
                    mm(byp, 0, TB, [(wop[bi][:, g, :], ypT[:, g, tbs(tb)]) for g in range(4)],
                       [wopB[bi]] + [YPB[g][tb] for g in range(4)])
                    bgp = rot.next()
                    mm(bgp, 0, TB, [(wgp[bi][:, kc, :], uTh[:, kc, tbs(tl)]) for kc in range(8)], [wgpB[bi]] + ur)
                    i0 = sc % 4
                    i1 = (sc + 1) % 4
                    sc += 2
                    act(sgm[i0], banks[bga][:, :], AF.Sigmoid, [BK[bga]], [sgmB[i0]])
                    act(sgm[i1], banks[bgp][:, :], AF.Sigmoid, [BK[bgp]], [sgmB[i1]])
                    tt(sgm[i0], sgm[i0], banks[bya][:, :], ALU.mult, [sgmB[i0], BK[bya]], [sgmB[i0]])
                    tt(sgm[i1], sgm[i1], banks[byp][:, :], ALU.mult, [sgmB[i1], BK[byp]], [sgmB[i1]])
                    tt(mixin[:, dc, tbs(tb)], sgm[i0], sgm[i1], ALU.add, [sgmB[i0], sgmB[i1]], [MXB[dc][tb]])
        P.barrier()
        A.release("uTh", "attnT", "ypT", "woa0", "woa1", "wop0", "wop1", "wga0", "wga1", "wgp0", "wgp1",
                  "sgm0", "sgm1", "sgm2", "sgm3")

        wo = A.alloc("wo", [128, 8, 1024], BF16); woB = Buf("wo")
        dma("pool", "c_wo", wo, kc_rows(w_out_d[:, :]), writes=[woB])
        mTs = [A.alloc(f"mT{i}", [128, 8, TB], F32) for i in range(2)]
        mTBs = [[Buf(f"mT{i}_{c}") for c in range(8)] for i in range(2)]
        for tb in range(NB):
            mT = mTs[tb % 2]
            mTB = mTBs[tb % 2]
            for dc in range(8):
                b = rot.next()
                mm(b, 0, TB, [(wo[:, kc, dc * 128:(dc + 1) * 128], mixin[:, kc, tbs(tb)]) for kc in range(8)],
                   [woB] + [MXB[kc][tb] for kc in range(8)])
                act(mT[:, dc, :], banks[b][:, :], AF.Copy, [BK[b]], [mTB[dc]])
            ri = norm_rs(mT[:, :, :], 8, D, mTB)
            for c in range(8):
                stt(mT[:, c, :], mT[:, c, :], gT[:, GO["mpost"] + c:GO["mpost"] + c + 1], rs[ri],
                    ALU.mult, ALU.mult, [mTB[c], gTB, rsB[ri]], [mTB[c]])
                tt(xT[:, c, tbs(tb)], mT[:, c, :], xT[:, c, tbs(tb)], ALU.add, [mTB[c], XB[c][tb]], [XB[c][tb]],
                   eng="pool")
        P.barrier()
        A.release("mixin", "wo", "mT0", "mT1")

    if stage >= 2:
        mixer()
    if stage >= 3:
        ffn("ffn2", GO["f2pre"], GO["f2post"])

    NYO = 6
    yo = [A.alloc(f"yo{i}", [128, D], F32) for i in range(NYO)]
    yoB = [[Buf(f"yo{i}_{hf}") for hf in range(2)] for i in range(NYO)]
    for tb in range(NB):
        if stage >= 3:
            ri = norm_rs(xT[:, :, tbs(tb)], 8, D, [XB[c][tb] for c in range(8)])
            for c in range(8):
                stt(xT[:, c, tbs(tb)], xT[:, c, tbs(tb)], gT[:, GO["fin"] + c:GO["fin"] + c + 1], rs[ri],
                    ALU.mult, ALU.mult, [XB[c][tb], gTB, rsB[ri]], [XB[c][tb]])
        for it in range(4):
            i = tb * 4 + it
            yi = i % NYO
            for half in range(2):
                b = rot.next()

                def trf(e, half=half, b=b, i=i):
                    for j in range(4):
                        c = half * 4 + j
                        ins = e.transpose(banks[b][:, j * 128:(j + 1) * 128], xT[:, c, i * 128:(i + 1) * 128], ident)
                    return ins
                P.op("pe", trf, reads=[XB[half * 4 + j][tb] for j in range(4)] + [cstB], writes=[BK[b]])
                if half == 0:
                    act(yo[yi][:, 0:512], banks[b][:, :], AF.Copy, [BK[b]], [yoB[yi][0]])
                else:
                    cp(yo[yi][:, 512:1024], banks[b][:, :], [BK[b]], [yoB[yi][1]])
            dma("sp", f"c_yo{yi}", y_d[i * 128:(i + 1) * 128, :], yo[yi], reads=yoB[yi])
    P.barrier(engines=("sp",))
    with nc.Block() as block:
        P.emit(block)
    es.close()
    return nc


_CACHE = {}


def _consts():
    c = np.zeros((128, C_W), np.float32)
    c[:, C_ID:C_ID + 128] = np.eye(128, dtype=np.float32)
    inv = (np.float32(10000.0) ** (-np.arange(0, 64, 2, dtype=np.float32) / np.float32(64))).astype(np.float32)
    for p in range(128):
        j = p % 32
        c[p, C_FC] = inv[j]
        c[p, C_FS] = -inv[j] if (p % 64) < 32 else inv[j]
    c[:, C_HPI] = np.pi / 2
    for g, w in enumerate(WINS):
        left = w // 2
        right = w - 1 - left
        for j in range(8):
            for (t, col) in ((j, j), (T - 8 + j, 8 + j)):
                lo = max(t - left, 0)
                hi = min(t + right + 1, T)
                c[:, C_EDGE + g * 16 + col] = 1.0 / (hi - lo)
    return c


def kernel(**inputs):
    stage = int(inputs.pop("_stage", 3)) if "_stage" in inputs else 3
    if stage not in _CACHE:
        _CACHE[stage] = build(stage)
    nc = _CACHE[stage]
    f = lambda k: np.ascontiguousarray(np.asarray(inputs[k], dtype=np.float32)[0])
    x = np.asarray(inputs["x"], dtype=np.float32)
    pos = np.asarray(inputs["positions"]).astype(np.int32)
    gnames = ["ffn1_pre_g", "ffn1_post_g", "mix_pre_g", "q_a_norm_g", "kv_a_norm_g", "pool_scale",
              "mix_post_g", "ffn2_pre_g", "ffn2_post_g", "final_g"]
    gstack = np.ascontiguousarray(np.concatenate([f(k).reshape(-1, 128) for k in gnames], axis=0))
    shared = {
        "cst": _consts(), "gstack": gstack,
        "ffn1_w_gate": f("ffn1_w_gate"), "ffn1_w_up": f("ffn1_w_up"), "ffn1_w_down": f("ffn1_w_down"),
        "ffn2_w_gate": f("ffn2_w_gate"), "ffn2_w_up": f("ffn2_w_up"), "ffn2_w_down": f("ffn2_w_down"),
        "w_in": f("w_in"), "w_uq": f("w_uq"), "w_uk": f("w_uk"), "w_uv": f("w_uv"), "w_o_attn": f("w_o_attn"),
        "pool_w": f("pool_w"), "w_o_pool": f("w_o_pool"), "w_out": f("w_out"),
    }
    in_maps = []
    for b in range(8):
        m = dict(shared)
        m["x"] = np.ascontiguousarray(x[b])
        m["pos"] = np.ascontiguousarray(np.broadcast_to(pos[b][None, :], (128, T)))
        in_maps.append(m)
    res = run_bass_kernel_spmd(nc, in_maps, core_ids=list(range(8)))
    return np.stack([np.asarray(r["y"], dtype=np.float32) for r in res.results], axis=0)
```

```python
import numpy as np
import concourse.bass as bass
import concourse.mybir as mybir
from concourse.bass_utils import run_bass_kernel_spmd

F32 = mybir.dt.float32
BF16 = mybir.dt.bfloat16
I32 = mybir.dt.int32
U8 = mybir.dt.uint8
AF = mybir.ActivationFunctionType
ALU = mybir.AluOpType

ENGS = ("pe", "act", "dve", "pool", "sp")
_ENG_ATTR = {"pe": "tensor", "act": "scalar", "dve": "vector", "pool": "gpsimd", "sp": "sync"}

T = 2048
D = 1024
FF = 2816
NFF = 22
H = 8
EPS = 1e-6
NB = 4
TB = 512
IN_W = 3264
G_ROWS = 65
GO = dict(f1pre=0, f1post=8, mpre=16, gq=24, gkv=27, psc=29, mpost=33, f2pre=41, f2post=49, fin=57)
C_ID = 0
C_FC = 128
C_FS = 129
C_HPI = 130
C_EDGE = 132
C_W = C_EDGE + 64
WINS = (2, 4, 8, 16)


class Buf:
    __slots__ = ("name", "w", "r", "excl")

    def __init__(self, name, excl=False):
        self.name = name
        self.w = None
        self.r = {}
        self.excl = excl


class Op:
    __slots__ = ("stream", "idx", "sig", "count")

    def __init__(self, stream, idx):
        self.stream = stream
        self.idx = idx
        self.sig = False
        self.count = None


class Prog:
    def __init__(self, nc, eng_sems, dma_sems):
        self.nc = nc
        self.semobj = dict(zip(ENGS, eng_sems))
        self.nidx = {e: 0 for e in ENGS}
        self.ops = {e: [] for e in ENGS}
        self.known = {e: {} for e in ENGS}
        self.free_dma_sems = list(dma_sems)
        self.chain_last = {}
        self.last = {}

    def _need(self, eng, deps, op, skip_same):
        if op is None:
            return
        if skip_same and op.stream == eng:
            return
        if self.known[eng].get(op.stream, 0) >= op.idx:
            return
        cur = deps.get(op.stream)
        if cur is None or cur.idx < op.idx:
            deps[op.stream] = op

    def _collect(self, eng, reads, writes):
        deps = {}
        writes = list(writes)
        for b in reads:
            if b.excl:
                writes.append(b)
                continue
            self._need(eng, deps, b.w, False)
        for b in writes:
            self._need(eng, deps, b.w, True)
            for o in b.r.values():
                self._need(eng, deps, o, True)
        for s, o in deps.items():
            self.known[eng][s] = o.idx
            o.sig = True
        return list(deps.values())

    def _mark(self, op, reads, writes):
        for b in reads:
            if b.excl:
                b.w = op
                b.r = {}
            else:
                b.r[op.stream] = op
        for b in writes:
            b.w = op
            b.r = {}
        self.last[op.stream] = op

    def op(self, eng, fn, reads=(), writes=()):
        deps = self._collect(eng, reads, writes)
        self.nidx[eng] += 1
        o = Op(eng, self.nidx[eng])
        self._mark(o, reads, writes)
        self.ops[eng].append((deps, fn, o))
        return o

    def dma(self, eng, chain, fn, reads=(), writes=()):
        if chain not in self.semobj:
            self.semobj[chain] = self.free_dma_sems.pop()
            self.nidx[chain] = 0
        deps = self._collect(eng, reads, writes)
        last = self.chain_last.get(chain)
        if last is not None and self.known[eng].get(chain, 0) < last.idx:
            deps = [d for d in deps if d.stream != chain] + [last]
            self.known[eng][chain] = last.idx
        self.nidx[chain] += 1
        o = Op(chain, self.nidx[chain])
        o.sig = True
        self.chain_last[chain] = o
        self._mark(o, reads, writes)
        self.ops[eng].append((deps, fn, o))
        return o

    def barrier(self, engines=ENGS):
        lasts = list(self.last.values())
        for e in engines:
            deps = {}
            for o in lasts:
                self._need(e, deps, o, True)
            for s, o in deps.items():
                self.known[e][s] = o.idx
                o.sig = True
            if deps:
                self.ops[e].append((list(deps.values()), None, None))

    def emit(self, block):
        for e in ENGS:
            c = 0
            for deps, fn, o in self.ops[e]:
                if o is None:
                    continue
                if o.stream == e:
                    if o.sig:
                        c += 1
                        o.count = c
                else:
                    o.count = 16 * o.idx
        for e in ENGS:
            ops = self.ops[e]
            if not ops:
                continue

            def body(engine, ops=ops, e=e):
                for deps, fn, o in ops:
                    for d in deps:
                        engine.wait_ge(self.semobj[d.stream], d.count)
                    if fn is None:
                        continue
                    ins = fn(engine)
                    if o.stream != e:
                        ins.then_inc(self.semobj[o.stream], 16)
                    elif o.sig:
                        ins.then_inc(self.semobj[e], 1)

            getattr(block, _ENG_ATTR[e])(body)


class Arena:
    def __init__(self, tensor, nbytes):
        self.t = tensor
        self.free = [(0, nbytes)]
        self.used = {}

    def alloc(self, name, shape, dt, top=False):
        esz = {F32: 4, BF16: 2, I32: 4}[dt]
        n = esz
        for s in shape[1:]:
            n *= s
        nb = (n + 63) // 64 * 64
        order = list(enumerate(self.free))
        if top:
            order = order[::-1]
        for i, (o, sz) in order:
            if sz >= nb:
                if sz == nb:
                    self.free.pop(i)
                elif top:
                    self.free[i] = (o, sz - nb)
                    o = o + sz - nb
                else:
                    self.free[i] = (o + nb, sz - nb)
                self.used[name] = (o, nb)
                v = self.t[0:shape[0], o:o + n].bitcast(dt)
                if len(shape) == 3:
                    v = v.rearrange("p (a b) -> p a b", a=shape[1])
                return v
        raise RuntimeError(f"arena full for {name} {shape} free={self.free}")

    def release(self, *names):
        for name in names:
            o, nb = self.used.pop(name)
            self.free.append((o, nb))
        self.free.sort()
        m = []
        for o, sz in self.free:
            if m and m[-1][0] + m[-1][1] == o:
                m[-1] = (m[-1][0], m[-1][1] + sz)
            else:
                m.append((o, sz))
        self.free = m


class Rot:
    def __init__(self, ids):
        self.ids = list(ids)
        self.i = 0

    def next(self):
        v = self.ids[self.i % len(self.ids)]
        self.i += 1
        return v


from contextlib import ExitStack
import math

SCALE = 1.0 / math.sqrt(192.0)
TWO_PI = 2.0 * math.pi


def build(stage=3):
    nc = bass.Bass("TRN2", target_bir_lowering=False)

    def din(name, shape, dt=F32):
        return nc.dram_tensor(name, list(shape), dt, kind="ExternalInput").ap()

    x_d = din("x", [T, D])
    pos_d = din("pos", [128, T], I32)
    cst_d = din("cst", [128, C_W])
    g_d = din("gstack", [G_ROWS, 128])
    W = {}
    for f in ("ffn1", "ffn2"):
        W[f + "_g"] = din(f + "_w_gate", [D, FF])
        W[f + "_u"] = din(f + "_w_up", [D, FF])
        W[f + "_d"] = din(f + "_w_down", [FF, D])
    w_in_d = din("w_in", [D, IN_W])
    w_uq_d = din("w_uq", [384, 1536])
    w_uk_d = din("w_uk", [256, 1024])
    w_uv_d = din("w_uv", [256, 1024])
    w_oa_d = din("w_o_attn", [1024, 1024])
    pw_d = din("pool_w", [4, 128, 128])
    w_op_d = din("w_o_pool", [512, 1024])
    w_out_d = din("w_out", [1024, 1024])
    gfb_d = din("gfb", [128, D])
    y_d = nc.dram_tensor("y", [T, D], F32, kind="ExternalOutput").ap()

    es = ExitStack()
    ARENA_BYTES = 206 * 1024
    arena_t = es.enter_context(nc.sbuf_tensor("arena", [128, ARENA_BYTES], U8))
    A = Arena(arena_t, ARENA_BYTES)
    banks = [es.enter_context(nc.psum_tensor(f"bk{i}", [128, 512], F32)) for i in range(8)]
    BK = [Buf(f"bk{i}", excl=True) for i in range(8)]
    eng_sems = [es.enter_context(nc.semaphore(f"s_{e}")) for e in ENGS]
    dma_sems = [es.enter_context(nc.semaphore(f"d{i}")) for i in range(48)]
    P = Prog(nc, eng_sems, dma_sems)
    rot = Rot(range(8))

    def kc_rows(ap2d):
        return ap2d.rearrange("(c p) n -> p c n", p=128)

    def tbs(tb):
        return slice(tb * TB, (tb + 1) * TB)

    def mm(bank, c0, n, pairs, reads):
        out = banks[bank][:, c0:c0 + n]

        def fn(e, pairs=pairs, out=out):
            k = len(pairs)
            for i, (l, r) in enumerate(pairs):
                ins = e.matmul(out, l, r, start=(i == 0), stop=(i == k - 1))
            return ins
        return P.op("pe", fn, reads=reads, writes=[BK[bank]])

    def act(out, in_, func, reads, writes, **kw):
        return P.op("act", lambda e: e.activation(out=out, in_=in_, func=func, **kw), reads=reads, writes=writes)

    def tt(out, in0, in1, op, reads, writes, eng="dve"):
        return P.op(eng, lambda e: e.tensor_tensor(out=out, in0=in0, in1=in1, op=op), reads=reads, writes=writes)

    def stt(out, in0, scalar, in1, op0, op1, reads, writes, eng="dve"):
        return P.op(eng, lambda e: e.scalar_tensor_tensor(out=out, in0=in0, scalar=scalar, in1=in1, op0=op0, op1=op1),
                    reads=reads, writes=writes)

    def ts(out, in0, s1, s2, op0, op1, reads, writes, eng="dve"):
        if s2 is None:
            return P.op(eng, lambda e: e.tensor_scalar(out=out, in0=in0, scalar1=s1, scalar2=None, op0=op0),
                        reads=reads, writes=writes)
        return P.op(eng, lambda e: e.tensor_scalar(out=out, in0=in0, scalar1=s1, scalar2=s2, op0=op0, op1=op1),
                    reads=reads, writes=writes)

    def cp(out, in_, reads, writes, eng="dve"):
        return P.op(eng, lambda e: e.tensor_copy(out=out, in_=in_), reads=reads, writes=writes)

    def dma(q, chain, out, in_, reads=(), writes=()):
        return P.dma(q, chain, lambda e: e.dma_start(out=out, in_=in_), reads=reads, writes=writes)

    xT = A.alloc("xT", [128, 8, T], F32)
    XB = [[Buf(f"x{c}_{tb}") for tb in range(NB)] for c in range(8)]
    cst = A.alloc("cst", [128, C_W], F32); cstB = Buf("cst")
    gT = A.alloc("gT", [128, G_ROWS], F32); gTB = Buf("gT")
    gh = A.alloc("gh", [128, G_ROWS], F32); ghB = Buf("gh")
    ones = A.alloc("ones", [128, 128], BF16); onesB = Buf("ones")
    epsb = A.alloc("epsb", [128, 1], F32); epsB = Buf("eps")
    sq = A.alloc("sq", [128, 8, TB], BF16); sqB = Buf("sq")
    rs = [A.alloc(f"rs{i}", [128, TB], F32) for i in range(2)]; rsB = [Buf(f"rs{i}") for i in range(2)]
    sg = [A.alloc(f"sg{i}", [128, TB], F32) for i in range(2)]; sgB = [Buf(f"sg{i}") for i in range(2)]
    ctr = {"rs": 0, "sg": 0}
    ident = cst[:, C_ID:C_ID + 128]

    def norm_rs(src3, nch, Dn, reads):
        act(sq[:, 0:nch, :], src3, AF.Square, reads, [sqB])
        b = rot.next()
        mm(b, 0, TB, [(ones, sq[:, c, :]) for c in range(nch)], [onesB, sqB])
        ri = ctr["rs"] % 2
        ctr["rs"] += 1
        act(rs[ri], banks[b][:, :], AF.Ln, [BK[b], epsB], [rsB[ri]], bias=epsb[:, 0:1], scale=1.0 / Dn)
        act(rs[ri], rs[ri], AF.Exp, [rsB[ri]], [rsB[ri]], scale=-0.5)
        return ri

    def next_sg():
        i = ctr["sg"] % 2
        ctr["sg"] += 1
        return i

    dma("sp", "c_cst", cst, cst_d[:, :], writes=[cstB])
    gsb = A.alloc("gsb", [128, 128], F32, top=True); gsbB = Buf("gsb")
    dma("sp", "c_g", gsb[0:G_ROWS, :], g_d[:, :], writes=[gsbB])
    P.op("dve", lambda e: e.memset(ones, 1.0), writes=[onesB])
    P.op("dve", lambda e: e.memset(epsb, EPS), writes=[epsB])
    P.op("pe", lambda e: e.transpose(banks[0][:, 0:G_ROWS], gsb[0:G_ROWS, :], cst[0:G_ROWS, 0:G_ROWS]),
         reads=[gsbB, cstB], writes=[BK[0]])
    act(gT, banks[0][:, 0:G_ROWS], AF.Copy, [BK[0]], [gTB])
    P.op("act", lambda e: e.mul(out=gh, in_=gT, mul=0.5), reads=[gTB], writes=[ghB])

    xin = [A.alloc(f"xin{i}", [128, D], F32, top=True) for i in range(6)]
    xinB = [Buf(f"xin{i}") for i in range(6)]
    for i in range(16):
        xi = i % 6
        dma("sp", f"c_xin{xi}", xin[xi], x_d[i * 128:(i + 1) * 128, :], writes=[xinB[xi]])
        for half in range(2):
            b = rot.next()

            def trf(e, xi=xi, half=half, b=b):
                for j in range(4):
                    c = half * 4 + j
                    ins = e.transpose(banks[b][:, j * 128:(j + 1) * 128], xin[xi][:, c * 128:(c + 1) * 128], ident)
                return ins
            P.op("pe", trf, reads=[xinB[xi], cstB], writes=[BK[b]])
            act(xT[:, half * 4:half * 4 + 4, i * 128:(i + 1) * 128],
                banks[b][:, :].rearrange("p (c t) -> p c t", c=4), AF.Copy,
                [BK[b]], [XB[half * 4 + j][i // 4] for j in range(4)])
    A.release("gsb", "xin0", "xin1", "xin2", "xin3", "xin4", "xin5")

    def ffn(f, go_pre, go_post):
        hT = A.alloc("hT", [128, 8, 1024], BF16)
        HB = [[Buf(f"h{c}_{t}") for t in range(2)] for c in range(8)]
        aT = A.alloc("aT", [128, NFF, 1024], BF16)
        AB = [[Buf(f"a{j}_{t}") for t in range(2)] for j in range(NFF)]
        f1T = A.alloc("f1T", [128, 8, 1024], F32, top=True)
        FB = [[Buf(f"f{c}_{t}") for t in range(2)] for c in range(8)]
        wg = [A.alloc(f"wg{i}", [128, 8, 256], BF16) for i in range(2)]; wgB = [Buf(f"wg{i}") for i in range(2)]
        wu = [A.alloc(f"wu{i}", [128, 8, 256], BF16) for i in range(2)]; wuB = [Buf(f"wu{i}") for i in range(2)]
        wdn = [A.alloc(f"wdn{i}", [128, NFF, 128], BF16) for i in range(2)]; wdB = [Buf(f"wdn{i}") for i in range(2)]
        Wg, Wu, Wd = W[f + "_g"], W[f + "_u"], W[f + "_d"]
        cnt = {"g": 0, "d": 0}
        bgq = []

        def Nn(hh):
            for tl in range(2):
                tb = 2 * hh + tl
                ri = norm_rs(xT[:, :, tbs(tb)], 8, D, [XB[c][tb] for c in range(8)])
                for c in range(8):
                    stt(hT[:, c, tbs(tl)], xT[:, c, tbs(tb)], gT[:, go_pre + c:go_pre + c + 1], rs[ri],
                        ALU.mult, ALU.mult, [XB[c][tb], gTB, rsB[ri]], [HB[c][tl]])

        def GU(hh):
            for grp in range(11):
                bi = cnt["g"] % 2
                cnt["g"] += 1
                dma("pool", f"c_wg{bi}", wg[bi], kc_rows(Wg[:, grp * 256:(grp + 1) * 256]), writes=[wgB[bi]])
                dma("pool", f"c_wu{bi}", wu[bi], kc_rows(Wu[:, grp * 256:(grp + 1) * 256]), writes=[wuB[bi]])
                for j2 in range(2):
                    j = grp * 2 + j2
                    for tl in range(2):
                        bg = rot.next()
                        bu = rot.next()
                        hr = [HB[kc][tl] for kc in range(8)]
                        mm(bg, 0, TB, [(wg[bi][:, kc, j2 * 128:(j2 + 1) * 128], hT[:, kc, tbs(tl)]) for kc in range(8)],
                           [wgB[bi]] + hr)
                        mm(bu, 0, TB, [(wu[bi][:, kc, j2 * 128:(j2 + 1) * 128], hT[:, kc, tbs(tl)]) for kc in range(8)],
                           [wuB[bi]] + hr)
                        si = next_sg()
                        act(sg[si], banks[bg][:, :], AF.Silu, [BK[bg]], [sgB[si]])
                        tt(aT[:, j, tbs(tl)], sg[si], banks[bu][:, :], ALU.mult, [sgB[si], BK[bu]], [AB[j][tl]])
                        if bgq:
                            bgq.pop(0)()
            while bgq:
                bgq.pop(0)()

        def Dd(hh):
            for dc in range(8):
                bi = cnt["d"] % 2
                cnt["d"] += 1
                dma("pool", f"c_wd{bi}", wdn[bi], kc_rows(Wd[:, dc * 128:(dc + 1) * 128]), writes=[wdB[bi]])
                for tl in range(2):
                    b = rot.next()
                    mm(b, 0, TB, [(wdn[bi][:, j, :], aT[:, j, tbs(tl)]) for j in range(NFF)],
                       [wdB[bi]] + [AB[j][tl] for j in range(NFF)])
                    act(f1T[:, dc, tbs(tl)], banks[b][:, :], AF.Copy, [BK[b]], [FB[dc][tl]])

        def Pp(hh):
            th = []
            for tl in range(2):
                tb = 2 * hh + tl
                st = {}

                def t_norm(tl=tl, st=st):
                    st["ri"] = norm_rs(f1T[:, :, tbs(tl)], 8, D, [FB[c][tl] for c in range(8)])
                th.append(t_norm)
                for c in range(8):
                    def t_c(tl=tl, tb=tb, c=c, st=st, hh=hh):
                        ri = st["ri"]
                        stt(f1T[:, c, tbs(tl)], f1T[:, c, tbs(tl)], gh[:, go_post + c:go_post + c + 1], rs[ri],
                            ALU.mult, ALU.mult, [FB[c][tl], ghB, rsB[ri]], [FB[c][tl]])
                        tt(xT[:, c, tbs(tb)], f1T[:, c, tbs(tl)], xT[:, c, tbs(tb)], ALU.add,
                           [FB[c][tl], XB[c][tb]], [XB[c][tb]], eng=("dve" if hh == 0 else "pool"))
                    th.append(t_c)
            return th

        Nn(0)
        GU(0)
        Nn(1)
        Dd(0)
        bgq.extend(Pp(0))
        GU(1)
        Dd(1)
        for t in Pp(1):
            t()
        P.barrier()
        A.release("hT", "aT", "f1T", "wg0", "wg1", "wu0", "wu1", "wdn0", "wdn1")

    if stage >= 1:
        ffn("ffn1", GO["f1pre"], GO["f1post"])

    def make_uT(uT, UB, tblist=range(NB), off=0):
        for tb in tblist:
            ri = norm_rs(xT[:, :, tbs(tb)], 8, D, [XB[c][tb] for c in range(8)])
            for c in range(8):
                stt(uT[:, c, tbs(tb - off)], xT[:, c, tbs(tb)], gT[:, GO["mpre"] + c:GO["mpre"] + c + 1], rs[ri],
                    ALU.mult, ALU.mult, [XB[c][tb], gTB, rsB[ri]], [UB[c][tb - off]])

    def mixer():
        cosT = A.alloc("cosT", [128, T], F32, top=True); cosB = Buf("cos")
        sinT = A.alloc("sinT", [128, T], F32, top=True); sinB = Buf("sin")
        cqn = A.alloc("cqn", [128, 3, T], BF16, top=True)
        CQB = [[Buf(f"cq{c}_{t}") for t in range(NB)] for c in range(3)]
        ckvn = A.alloc("ckvn", [128, 2, T], BF16, top=True)
        CKB = [[Buf(f"ck{c}_{t}") for t in range(NB)] for c in range(2)]
        krT = A.alloc("krT", [128, T], BF16, top=True)
        KRB = [Buf(f"kr{t}") for t in range(NB)]
        posi = A.alloc("posi", [128, T], I32); posiB = Buf("posi")
        kf = posi.bitcast(F32)
        posf = A.alloc("posf", [128, T], F32); posfB = Buf("posf")
        ang = A.alloc("ang", [128, T], F32); angB = Buf("ang")
        uT = A.alloc("uT", [128, 8, T], BF16)
        UB = [[Buf(f"u{c}_{t}") for t in range(NB)] for c in range(8)]
        win = A.alloc("win", [128, 8, 768], BF16); winB = Buf("win")
        wkA = A.alloc("wkA", [128, 8, 128], BF16); wkAB = Buf("wkA")
        wkB = A.alloc("wkB", [128, 8, 128], BF16); wkBB = Buf("wkB")
        cqr = A.alloc("cqr", [128, 3, TB], F32); cqrB = [Buf(f"cqr{c}") for c in range(3)]
        ckr = A.alloc("ckr", [128, 2, TB], F32); ckrB = [Buf(f"ckr{c}") for c in range(2)]
        dma("pool", "c_win", win, kc_rows(w_in_d[:, 0:768]), writes=[winB])
        dma("sp", "c_pos", posi, pos_d[:, :], writes=[posiB])
        tabq = []
        tabq.append(lambda: cp(posf, posi, [posiB], [posfB]))
        tabq.append(lambda: ts(ang, posf, cst[:, C_FC:C_FC + 1], math.pi / 2, ALU.mult, ALU.add, [posfB, cstB], [angB]))
        tabq.append(lambda: ts(posf, posf, cst[:, C_FS:C_FS + 1], None, ALU.mult, None, [posfB, cstB], [posfB]))
        angs = []
        for ti in range(2):
            a = ang if ti == 0 else posf
            aB = angB if ti == 0 else posfB
            tabq.append(lambda a=a, aB=aB: ts(posi, a, 1.0 / TWO_PI, None, ALU.mult, None, [aB], [posiB]))
            tabq.append(lambda a=a, aB=aB: cp(kf, posi, [posiB], [posiB]))
            tabq.append(lambda a=a, aB=aB: stt(a, kf, -TWO_PI, a, ALU.mult, ALU.add, [posiB, aB], [aB]))
            tabq.append(lambda a=a, aB=aB: ts(a, a, 3.14159, -3.14159, ALU.min, ALU.max, [aB], [aB]))
            angs.append((a, aB))
        make_uT(uT, UB)
        for d0 in (0, 64):
            cp(wkA[:, :, d0:d0 + 64], win[:, :, 640:704], [winB], [wkAB])
            cp(wkB[:, :, d0:d0 + 32], win[:, :, 672:704], [winB], [wkBB])
            cp(wkB[:, :, d0 + 32:d0 + 64], win[:, :, 640:672], [winB], [wkBB])
        for tb in range(NB):
            ur = [UB[kc][tb] for kc in range(8)]
            for c in range(3):
                b = rot.next()
                mm(b, 0, TB, [(win[:, kc, c * 128:(c + 1) * 128], uT[:, kc, tbs(tb)]) for kc in range(8)], [winB] + ur)
                act(cqr[:, c, :], banks[b][:, :], AF.Copy, [BK[b]], [cqrB[c]])
                if tabq:
                    tabq.pop(0)()
            for c in range(2):
                b = rot.next()
                mm(b, 0, TB, [(win[:, kc, 384 + c * 128:384 + (c + 1) * 128], uT[:, kc, tbs(tb)]) for kc in range(8)],
                   [winB] + ur)
                act(ckr[:, c, :], banks[b][:, :], AF.Copy, [BK[b]], [ckrB[c]])
                if tabq:
                    tabq.pop(0)()
            ri = norm_rs(cqr[:, :, :], 3, 384, cqrB)
            for c in range(3):
                stt(cqn[:, c, tbs(tb)], cqr[:, c, :], gT[:, GO["gq"] + c:GO["gq"] + c + 1], rs[ri],
                    ALU.mult, ALU.mult, [cqrB[c], gTB, rsB[ri]], [CQB[c][tb]])
            ri = norm_rs(ckr[:, :, :], 2, 256, ckrB)
            for c in range(2):
                stt(ckvn[:, c, tbs(tb)], ckr[:, c, :], gT[:, GO["gkv"] + c:GO["gkv"] + c + 1], rs[ri],
                    ALU.mult, ALU.mult, [ckrB[c], gTB, rsB[ri]], [CKB[c][tb]])
        while tabq:
            tabq.pop(0)()
        act(cosT, angs[0][0], AF.Sin, [angs[0][1]], [cosB])
        act(sinT, angs[1][0], AF.Sin, [angs[1][1]], [sinB])
        for tb in range(NB):
            ur = [UB[kc][tb] for kc in range(8)]
            ba = rot.next()
            mm(ba, 0, TB, [(wkA[:, kc, :], uT[:, kc, tbs(tb)]) for kc in range(8)], [wkAB] + ur)
            bb = rot.next()
            mm(bb, 0, TB, [(wkB[:, kc, :], uT[:, kc, tbs(tb)]) for kc in range(8)], [wkBB] + ur)
            s0 = next_sg()
            tt(sg[s0], banks[ba][:, :], cosT[:, tbs(tb)], ALU.mult, [BK[ba], cosB], [sgB[s0]])
            s1 = next_sg()
            tt(sg[s1], banks[bb][:, :], sinT[:, tbs(tb)], ALU.mult, [BK[bb], sinB], [sgB[s1]])
            tt(krT[:, tbs(tb)], sg[s0], sg[s1], ALU.add, [sgB[s0], sgB[s1]], [KRB[tb]])
        P.barrier()
        A.release("uT", "posi", "posf", "ang", "win", "wkA", "wkB", "cqr", "ckr")

        wq = A.alloc("wq", [128, 3, 1536], BF16); wqB_ = Buf("wq")
        dma("pool", "c_wq", wq, kc_rows(w_uq_d[:, :]), writes=[wqB_])
        wqa = A.alloc("wqa", [128, 3, 512], BF16); wqaB = Buf("wqa")
        wqb = A.alloc("wqb", [128, 3, 512], BF16); wqbB = Buf("wqb")
        qrT = A.alloc("qrT", [128, 4, T], BF16, top=True)
        QRB = [[Buf(f"qr{j}_{t}") for t in range(NB)] for j in range(4)]
        for kc in range(3):
            src = wq[:, kc, :].rearrange("p (h d) -> p h d", h=8)
            cp(wqa[:, kc, :].rearrange("p (h d) -> p h d", h=8), src[:, :, 128:192], [wqB_], [wqaB])
            dstb = wqb[:, kc, :].rearrange("p (h d) -> p h d", h=8)
            cp(dstb[:, :, 0:32], src[:, :, 160:192], [wqB_], [wqbB])
            cp(dstb[:, :, 32:64], src[:, :, 128:160], [wqB_], [wqbB])
        for j in range(4):
            for tb in range(NB):
                cr = [CQB[kc][tb] for kc in range(3)]
                ba = rot.next()
                mm(ba, 0, TB, [(wqa[:, kc, j * 128:(j + 1) * 128], cqn[:, kc, tbs(tb)]) for kc in range(3)], [wqaB] + cr)
                bb = rot.next()
                mm(bb, 0, TB, [(wqb[:, kc, j * 128:(j + 1) * 128], cqn[:, kc, tbs(tb)]) for kc in range(3)], [wqbB] + cr)
                s0 = next_sg()
                tt(sg[s0], banks[ba][:, :], cosT[:, tbs(tb)], ALU.mult, [BK[ba], cosB], [sgB[s0]])
                s1 = next_sg()
                tt(sg[s1], banks[bb][:, :], sinT[:, tbs(tb)], ALU.mult, [BK[bb], sinB], [sgB[s1]])
                tt(qrT[:, j, tbs(tb)], sg[s0], sg[s1], ALU.add, [sgB[s0], sgB[s1]], [QRB[j][tb]])
        P.barrier()
        A.release("cosT", "sinT", "wqa", "wqb")

        attnT = A.alloc("attnT", [128, 8, T], BF16, top=True)
        ATB = [[Buf(f"at{h}_{t}") for t in range(NB)] for h in range(H)]
        wk = A.alloc("wk", [128, 2, 1024], BF16); wkB_ = Buf("wk")
        wv = A.alloc("wv", [128, 2, 1024], BF16); wvB_ = Buf("wv")
        dma("pool", "c_wk", wk, kc_rows(w_uk_d[:, :]), writes=[wkB_])
        dma("pool", "c_wv", wv, kc_rows(w_uv_d[:, :]), writes=[wvB_])
        qn = [A.alloc(f"qn{i}", [128, T], BF16) for i in range(2)]
        QNB = [[Buf(f"qn{i}_{t}") for t in range(NB)] for i in range(2)]
        kn = [A.alloc(f"kn{i}", [128, T], BF16) for i in range(2)]
        KNB = [[Buf(f"kn{i}_{t}") for t in range(NB)] for i in range(2)]
        Vh = [A.alloc(f"vh{i}", [128, 16, 128], BF16) for i in range(2)]
        VB = [[Buf(f"vh{i}_{g}") for g in range(4)] for i in range(2)]
        NPT = 6
        PT = [A.alloc(f"pt{i}", [128, TB], BF16) for i in range(NPT)]
        PTB = [Buf(f"pt{i}") for i in range(NPT)]
        accs = [[sg[0], sg[1]], [rs[0], rs[1]]]
        accB = [[Buf("accA0"), Buf("accB0")], [Buf("accA1"), Buf("accB1")]]
        accS = [sq[:, 0, :], sq[:, 1, :]]
        accSB = [Buf("accS0"), Buf("accS1")]
        rcp = [A.alloc(f"rcp{i}", [128, TB], F32) for i in range(2)]
        rcpB = [Buf(f"rcp{i}") for i in range(2)]
        strot = Rot([0, 1, 2, 3])

        def gen_thunks(h):
            hb = h % 2
            th = []
            for tb in range(NB):
                def tq(tb=tb):
                    b = strot.next()
                    mm(b, 0, TB, [(wq[:, kc, h * 192:h * 192 + 128], cqn[:, kc, tbs(tb)]) for kc in range(3)],
                       [wqB_] + [CQB[kc][tb] for kc in range(3)])
                    act(qn[hb][:, tbs(tb)], banks[b][:, :], AF.Copy, [BK[b]], [QNB[hb][tb]])
                th.append(tq)

                def tk(tb=tb):
                    b = strot.next()
                    mm(b, 0, TB, [(wk[:, kc, h * 128:(h + 1) * 128], ckvn[:, kc, tbs(tb)]) for kc in range(2)],
                       [wkB_] + [CKB[kc][tb] for kc in range(2)])
                    cp(kn[hb][:, tbs(tb)], banks[b][:, :], [BK[b]], [KNB[hb][tb]])
                th.append(tk)

                def tv(g4=tb):
                    b = strot.next()

                    def vfn(e, g4=g4, b=b):
                        for j in range(4):
                            tile = g4 * 4 + j
                            for kc in range(2):
                                ins = e.matmul(banks[b][:, j * 128:(j + 1) * 128],
                                               ckvn[:, kc, tile * 128:(tile + 1) * 128],
                                               wv[:, kc, h * 128:(h + 1) * 128], start=(kc == 0), stop=(kc == 1))
                        return ins
                    P.op("pe", vfn, reads=[wvB_] + [CKB[kc][g4] for kc in range(2)], writes=[BK[b]])
                    cp(Vh[hb][:, g4 * 4:(g4 + 1) * 4, :], banks[b][:, :].rearrange("p (j d) -> p j d", j=4),
                       [BK[b]], [VB[hb][g4]])
                th.append(tv)
            return th

        steps = [(h, qb, kc) for h in range(H) for qb in range(NB) for kc in range(16)]
        stbank = {}
        ptc = 0

        def emit_st(s):
            h, qb, kc = steps[s]
            hb = h % 2
            po = 64 * (h % 2)
            b = strot.next()
            stbank[s] = b
            mm(b, 0, TB, [(kn[hb][:, kc * 128:(kc + 1) * 128], qn[hb][:, tbs(qb)]),
                          (krT[po:po + 64, kc * 128:(kc + 1) * 128], qrT[po:po + 64, h // 2, tbs(qb)])],
               [KNB[hb][kc // 4], QNB[hb][qb], KRB[kc // 4], QRB[h // 2][qb]])

        def finalize(grp):
            h, qb = grp // NB, grp % NB
            g2 = grp % 2
            ob = 4 + g2
            sb_ = 6 + g2
            tt(accS[g2], accs[g2][0], accs[g2][1], ALU.add, accB[g2], [accSB[g2]])
            mm(sb_, 0, TB, [(ones, accS[g2])], [onesB, accSB[g2]])
            act(rcp[g2], banks[sb_][:, :], AF.Ln, [BK[sb_]], [rcpB[g2]])
            act(rcp[g2], rcp[g2], AF.Exp, [rcpB[g2]], [rcpB[g2]], scale=-1.0)
            tt(attnT[:, h, tbs(qb)], banks[ob][:, :], rcp[g2], ALU.mult, [BK[ob], rcpB[g2]], [ATB[h][qb]])

        for t in gen_thunks(0):
            t()
        genq = []
        emit_st(0)
        emit_st(1)
        for s, (h, qb, kc) in enumerate(steps):
            hb = h % 2
            grp = h * NB + qb
            g2 = grp % 2
            ob = 4 + g2
            if s + 2 < len(steps):
                emit_st(s + 2)
            pi = s % NPT
            b = stbank.pop(s)
            act(PT[pi], banks[b][:, :], AF.Exp, [BK[b]], [PTB[pi]], scale=SCALE)
            P.op("pe", lambda e, hb=hb, kc=kc, pi=pi, ob=ob: e.matmul(
                banks[ob][:, :], Vh[hb][:, kc, :], PT[pi], start=(kc == 0), stop=(kc == 15)),
                reads=[VB[hb][kc // 4], PTB[pi]], writes=[BK[ob]])
            ai = 1 if kc % 3 == 2 else 0
            ae = "pool" if ai else "dve"
            acc = accs[g2][ai]
            aB = accB[g2][ai]
            if kc == 0 or kc == 2:
                cp(acc, PT[pi], [PTB[pi]], [aB], eng=ae)
            else:
                tt(acc, acc, PT[pi], ALU.add, [aB, PTB[pi]], [aB], eng=ae)
            if kc == 3 and grp > 0:
                finalize(grp - 1)
            if qb == 0 and kc == 4 and h + 1 < H:
                genq.extend(gen_thunks(h + 1))
            if s % 4 == 2 and genq:
                genq.pop(0)()
        finalize(H * NB - 1)
        P.barrier()
        A.release("cqn", "ckvn", "krT", "wq", "qrT", "wk", "wv", "qn0", "qn1", "kn0", "kn1", "vh0", "vh1",
                  "pt0", "pt1", "pt2", "pt3", "pt4", "pt5", "rcp0", "rcp1")

        uT = A.alloc("uT", [128, 8, T], BF16)
        UB = [[Buf(f"u{c}_{t}") for t in range(NB)] for c in range(8)]
        ypT = A.alloc("ypT", [128, 4, T], BF16, top=True)
        YPB = [[Buf(f"yp{g}_{t}") for t in range(NB)] for g in range(4)]
        wpl = A.alloc("wpl", [128, 8, 512], BF16); wplB = Buf("wpl")
        dma("pool", "c_wpl", wpl, kc_rows(w_in_d[:, 704:1216]), writes=[wplB])
        pwt = A.alloc("pwt", [128, 4, 128], BF16); pwtB = Buf("pwt")
        dma("pool", "c_pwt", pwt, pw_d.rearrange("g c d -> c g d"), writes=[pwtB])
        XP = A.alloc("XP", [128, T + 16], F32); XPB = Buf("XP")
        sA = A.alloc("sA", [128, T + 16], F32); sAB = Buf("sA")
        sBt = A.alloc("sBt", [128, T + 16], F32); sBB = Buf("sBt")
        dT = A.alloc("dT", [128, T], BF16); dTB = Buf("dT")
        te = A.alloc("te", [128, 8], F32); teB = Buf("te")
        make_uT(uT, UB)
        P.op("dve", lambda e: e.memset(XP, 0.0), writes=[XPB])
        L = T + 16
        for g, w in enumerate(WINS):
            for tb in range(NB):
                b = rot.next()
                mm(b, 0, TB, [(wpl[:, kc, g * 128:(g + 1) * 128], uT[:, kc, tbs(tb)]) for kc in range(8)],
                   [wplB] + [UB[kc][tb] for kc in range(8)])
                cp(XP[:, 8 + tb * TB:8 + (tb + 1) * TB], banks[b][:, :], [BK[b]], [XPB])
            tt(sA[:, 0:L - 1], XP[:, 0:L - 1], XP[:, 1:L], ALU.add, [XPB], [sAB])
            tot, totB, off = sA, sAB, 7
            if w >= 4:
                tt(sBt[:, 0:L - 3], sA[:, 0:L - 3], sA[:, 2:L - 1], ALU.add, [sAB], [sBB])
                tot, totB, off = sBt, sBB, 6
            if w >= 8:
                tt(sA[:, 0:L - 7], sBt[:, 0:L - 7], sBt[:, 4:L - 3], ALU.add, [sBB], [sAB])
                tot, totB, off = sA, sAB, 4
            if w >= 16:
                tt(sBt[:, 0:L - 15], sA[:, 0:L - 15], sA[:, 8:L - 7], ALU.add, [sAB], [sBB])
                tot, totB, off = sBt, sBB, 0
            stt(dT, tot[:, off:off + T], 1.0 / w, XP[:, 8:8 + T], ALU.mult, ALU.subtract, [totB, XPB], [dTB])
            for (t0, c0) in ((0, 0), (T - 8, 8)):
                ec = C_EDGE + g * 16 + c0
                tt(te, tot[:, off + t0:off + t0 + 8], cst[:, ec:ec + 8], ALU.mult, [totB, cstB], [teB])
                tt(dT[:, t0:t0 + 8], te, XP[:, 8 + t0:16 + t0], ALU.subtract, [teB, XPB], [dTB])
            for tb in range(NB):
                b = rot.next()
                mm(b, 0, TB, [(pwt[:, g, :], dT[:, tbs(tb)])], [pwtB, dTB])
                act(ypT[:, g, tbs(tb)], banks[b][:, :], AF.Copy, [BK[b]], [YPB[g][tb]],
                    scale=gT[:, GO["psc"] + g:GO["psc"] + g + 1])
        P.barrier()
        A.release("uT", "wpl", "pwt", "XP", "sA", "sBt", "dT", "te")

        mixin = A.alloc("mixin", [128, 8, T], BF16)
        MXB = [[Buf(f"mx{c}_{t}") for t in range(NB)] for c in range(8)]
        woa = [A.alloc(f"woa{i}", [128, 8, 128], BF16) for i in range(2)]; woaB = [Buf(f"woa{i}") for i in range(2)]
        wop = [A.alloc(f"wop{i}", [128, 4, 128], BF16) for i in range(2)]; wopB = [Buf(f"wop{i}") for i in range(2)]
        wga = [A.alloc(f"wga{i}", [128, 8, 128], BF16) for i in range(2)]; wgaB = [Buf(f"wga{i}") for i in range(2)]
        wgp = [A.alloc(f"wgp{i}", [128, 8, 128], BF16) for i in range(2)]; wgpB = [Buf(f"wgp{i}") for i in range(2)]
        sgm = [A.alloc(f"sgm{i}", [128, TB], F32) for i in range(4)]; sgmB = [Buf(f"sgm{i}") for i in range(4)]
        uTh = A.alloc("uTh", [128, 8, 1024], BF16)
        UBh = [[Buf(f"uh{c}_{t}") for t in range(2)] for c in range(8)]
        sc = 0
        wc = 0
        for hh in range(2):
            make_uT(uTh, UBh, [2 * hh, 2 * hh + 1], 2 * hh)
            for dc in range(8):
                bi = wc % 2
                wc += 1
                cs = slice(dc * 128, (dc + 1) * 128)
                dma("pool", f"c_woa{bi}", woa[bi], kc_rows(w_oa_d[:, cs]), writes=[woaB[bi]])
                dma("pool", f"c_wop{bi}", wop[bi], kc_rows(w_op_d[:, cs]), writes=[wopB[bi]])
                dma("pool", f"c_wga{bi}", wga[bi], kc_rows(w_in_d[:, 1216 + dc * 128:1216 + (dc + 1) * 128]), writes=[wgaB[bi]])
                dma("pool", f"c_wgp{bi}", wgp[bi], kc_rows(w_in_d[:, 2240 + dc * 128:2240 + (dc + 1) * 128]), writes=[wgpB[bi]])
                for tl in range(2):
                    tb = 2 * hh + tl
                    ur = [UBh[kc][tl] for kc in range(8)]
                    bya = rot.next()
                    mm(bya, 0, TB, [(woa[bi][:, h, :], attnT[:, h, tbs(tb)]) for h in range(H)],
                       [woaB[bi]] + [ATB[h][tb] for h in range(H)])
                    bga = rot.next()
                    mm(bga, 0, TB, [(wga[bi][:, kc, :], uTh[:, kc, tbs(tl)]) for kc in range(8)], [wgaB[bi]] + ur)
                    byp = rot.next()
                    mm(byp, 0, TB, [(wop[bi][:, g, :], ypT[:, g, tbs(tb)]) for g in range(4)],
                       [wopB[bi]] + [YPB[g][tb] for g in range(4)])
                    bgp = rot.next()
                    mm(bgp, 0, TB, [(wgp[bi][:, kc, :], uTh[:, kc, tbs(tl)]) for kc in range(8)], [wgpB[bi]] + ur)
                    i0 = sc % 4
                    i1 = (sc + 1) % 4
                    sc += 2
                    act(sgm[i0], banks[bga][:, :], AF.Sigmoid, [BK[bga]], [sgmB[i0]])
                    act(sgm[i1], banks[bgp][:, :], AF.Sigmoid, [BK[bgp]], [sgmB[i1]])
                    tt(sgm[i0], sgm[i0], banks[bya][:, :], ALU.mult, [sgmB[i0], BK[bya]], [sgmB[i0]])
                    tt(sgm[i1], sgm[i1], banks[byp][:, :], ALU.mult, [sgmB[i1], BK[byp]], [sgmB[i1]])
                    tt(mixin[:, dc, tbs(tb)], sgm[i0], sgm[i1], ALU.add, [sgmB[i0], sgmB[i1]], [MXB[dc][tb]])
        P.barrier()
        A.release("uTh", "attnT", "ypT", "woa0", "woa1", "wop0", "wop1", "wga0", "wga1", "wgp0", "wgp1",
                  "sgm0", "sgm1", "sgm2", "sgm3")

        wo = A.alloc("wo", [128, 8, 1024], BF16); woB = Buf("wo")
        dma("pool", "c_wo", wo, kc_rows(w_out_d[:, :]), writes=[woB])
        mTs = [A.alloc(f"mT{i}", [128, 8, TB], F32) for i in range(2)]
        mTBs = [[Buf(f"mT{i}_{c}") for c in range(8)] for i in range(2)]
        for tb in range(NB):
            mT = mTs[tb % 2]
            mTB = mTBs[tb % 2]
            for dc in range(8):
                b = rot.next()
                mm(b, 0, TB, [(wo[:, kc, dc * 128:(dc + 1) * 128], mixin[:, kc, tbs(tb)]) for kc in range(8)],
                   [woB] + [MXB[kc][tb] for kc in range(8)])
                act(mT[:, dc, :], banks[b][:, :], AF.Copy, [BK[b]], [mTB[dc]])
            ri = norm_rs(mT[:, :, :], 8, D, mTB)
            for c in range(8):
                stt(mT[:, c, :], mT[:, c, :], gT[:, GO["mpost"] + c:GO["mpost"] + c + 1], rs[ri],
                    ALU.mult, ALU.mult, [mTB[c], gTB, rsB[ri]], [mTB[c]])
                tt(xT[:, c, tbs(tb)], mT[:, c, :], xT[:, c, tbs(tb)], ALU.add, [mTB[c], XB[c][tb]], [XB[c][tb]],
                   eng="pool")
        P.barrier()
        A.release("mixin", "wo", "mT0", "mT1")

    if stage >= 2:
        mixer()
    if stage >= 3:
        ffn("ffn2", GO["f2pre"], GO["f2post"])

    NYO = 6
    yo = [A.alloc(f"yo{i}", [128, D], F32) for i in range(NYO)]
    yoB = [[Buf(f"yo{i}_{hf}") for hf in range(2)] for i in range(NYO)]
    gfb = A.alloc("gfb", [128, D], F32); gfbB = Buf("gfb")
    dma("sp", "c_gfb", gfb, gfb_d[:, :], writes=[gfbB])
    rcol = [A.alloc(f"rcol{i}", [128, 1], F32) for i in range(4)]
    rcolB = [Buf(f"rcol{i}") for i in range(4)]
    sqsB = [Buf(f"sqs{i}") for i in range(4)]
    for i in range(16):
        tb = i // 4
        sl = i % 4
        yi = i % NYO
        tsl = slice(i * 128, (i + 1) * 128)
        xr = [XB[c][tb] for c in range(8)]
        sqs = sq[:, :, sl * 128:(sl + 1) * 128]
        act(sqs, xT[:, :, tsl], AF.Square, xr, [sqsB[sl]])
        bs = rot.next()

        def stf(e, sqs=sqs, bs=bs):
            for c in range(8):
                ins = e.matmul(banks[bs][:, 0:1], sqs[:, c, :], ones[:, 0:1], start=(c == 0), stop=(c == 7))
            return ins
        P.op("pe", stf, reads=[sqsB[sl], onesB], writes=[BK[bs]])
        act(rcol[sl], banks[bs][:, 0:1], AF.Ln, [BK[bs], epsB], [rcolB[sl]], bias=epsb[:, 0:1], scale=1.0 / D)
        act(rcol[sl], rcol[sl], AF.Exp, [rcolB[sl]], [rcolB[sl]], scale=-0.5)
        for half in range(2):
            b = rot.next()

            def trf(e, half=half, b=b, tsl=tsl):
                for j in range(4):
                    c = half * 4 + j
                    ins = e.transpose(banks[b][:, j * 128:(j + 1) * 128], xT[:, c, tsl], ident)
                return ins
            P.op("pe", trf, reads=[XB[half * 4 + j][tb] for j in range(4)] + [cstB], writes=[BK[b]])
            hs = slice(half * 512, (half + 1) * 512)
            stt(yo[yi][:, hs], banks[b][:, :], rcol[sl][:, 0:1], gfb[:, hs], ALU.mult, ALU.mult,
                [BK[b], rcolB[sl], gfbB], [yoB[yi][half]])
        dma("sp", f"c_yo{yi}", y_d[tsl, :], yo[yi], reads=yoB[yi])
    P.barrier(engines=("sp",))
    with nc.Block() as block:
        P.emit(block)
    es.close()
    return nc


_CACHE = {}


def _consts():
    c = np.zeros((128, C_W), np.float32)
    c[:, C_ID:C_ID + 128] = np.eye(128, dtype=np.float32)
    inv = (np.float32(10000.0) ** (-np.arange(0, 64, 2, dtype=np.float32) / np.float32(64))).astype(np.float32)
    for p in range(128):
        j = p % 32
        c[p, C_FC] = inv[j]
        c[p, C_FS] = -inv[j] if (p % 64) < 32 else inv[j]
    c[:, C_HPI] = np.pi / 2
    for g, w in enumerate(WINS):
        left = w // 2
        right = w - 1 - left
        for j in range(8):
            for (t, col) in ((j, j), (T - 8 + j, 8 + j)):
                lo = max(t - left, 0)
                hi = min(t + right + 1, T)
                c[:, C_EDGE + g * 16 + col] = 1.0 / (hi - lo)
    return c


def kernel(**inputs):
    stage = int(inputs.pop("_stage", 3)) if "_stage" in inputs else 3
    if stage not in _CACHE:
        _CACHE[stage] = build(stage)
    nc = _CACHE[stage]
    f = lambda k: np.ascontiguousarray(np.asarray(inputs[k], dtype=np.float32)[0])
    x = np.asarray(inputs["x"], dtype=np.float32)
    pos = np.asarray(inputs["positions"]).astype(np.int32)
    gnames = ["ffn1_pre_g", "ffn1_post_g", "mix_pre_g", "q_a_norm_g", "kv_a_norm_g", "pool_scale",
              "mix_post_g", "ffn2_pre_g", "ffn2_post_g", "final_g"]
    gstack = np.ascontiguousarray(np.concatenate([f(k).reshape(-1, 128) for k in gnames], axis=0))
    shared = {
        "cst": _consts(), "gstack": gstack,
        "gfb": np.ascontiguousarray(np.broadcast_to(f("final_g")[None, :], (128, D))),
        "ffn1_w_gate": f("ffn1_w_gate"), "ffn1_w_up": f("ffn1_w_up"), "ffn1_w_down": f("ffn1_w_down"),
        "ffn2_w_gate": f("ffn2_w_gate"), "ffn2_w_up": f("ffn2_w_up"), "ffn2_w_down": f("ffn2_w_down"),
        "w_in": f("w_in"), "w_uq": f("w_uq"), "w_uk": f("w_uk"), "w_uv": f("w_uv"), "w_o_attn": f("w_o_attn"),
        "pool_w": f("pool_w"), "w_o_pool": f("w_o_pool"), "w_out": f("w_out"),
    }
    in_maps = []
    for b in range(8):
        m = dict(shared)
        m["x"] = np.ascontiguousarray(x[b])
        m["pos"] = np.ascontiguousarray(np.broadcast_to(pos[b][None, :], (128, T)))
        in_maps.append(m)
    res = run_bass_kernel_spmd(nc, in_maps, core_ids=list(range(8)))
    return np.stack([np.asarray(r["y"], dtype=np.float32) for r in res.results], axis=0)
```

```python
import numpy as np
import concourse.bass as bass
import concourse.mybir as mybir
from concourse.bass_utils import run_bass_kernel_spmd

F32 = mybir.dt.float32
BF16 = mybir.dt.bfloat16
I32 = mybir.dt.int32
U8 = mybir.dt.uint8
AF = mybir.ActivationFunctionType
ALU = mybir.AluOpType

ENGS = ("pe", "act", "dve", "pool", "sp")
_ENG_ATTR = {"pe": "tensor", "act": "scalar", "dve": "vector", "pool": "gpsimd", "sp": "sync"}

T = 2048
D = 1024
FF = 2816
NFF = 22
H = 8
EPS = 1e-6
NB = 4
TB = 512
IN_W = 3264
G_ROWS = 65
GO = dict(f1pre=0, f1post=8, mpre=16, gq=24, gkv=27, psc=29, mpost=33, f2pre=41, f2post=49, fin=57)
C_ID = 0
C_FC = 128
C_FS = 129
C_HPI = 130
C_EDGE = 132
C_W = C_EDGE + 64
WINS = (2, 4, 8, 16)


class Buf:
    __slots__ = ("name", "w", "r", "excl")

    def __init__(self, name, excl=False):
        self.name = name
        self.w = None
        self.r = {}
        self.excl = excl


class Op:
    __slots__ = ("stream", "idx", "sig", "count")

    def __init__(self, stream, idx):
        self.stream = stream
        self.idx = idx
        self.sig = False
        self.count = None


class Prog:
    def __init__(self, nc, eng_sems, dma_sems):
        self.nc = nc
        self.semobj = dict(zip(ENGS, eng_sems))
        self.nidx = {e: 0 for e in ENGS}
        self.ops = {e: [] for e in ENGS}
        self.known = {e: {} for e in ENGS}
        self.free_dma_sems = list(dma_sems)
        self.chain_last = {}
        self.last = {}

    def _need(self, eng, deps, op, skip_same):
        if op is None:
            return
        if skip_same and op.stream == eng:
            return
        if self.known[eng].get(op.stream, 0) >= op.idx:
            return
        cur = deps.get(op.stream)
        if cur is None or cur.idx < op.idx:
            deps[op.stream] = op

    def _collect(self, eng, reads, writes):
        deps = {}
        writes = list(writes)
        for b in reads:
            if b.excl:
                writes.append(b)
                continue
            self._need(eng, deps, b.w, False)
        for b in writes:
            self._need(eng, deps, b.w, True)
            for o in b.r.values():
                self._need(eng, deps, o, True)
        for s, o in deps.items():
            self.known[eng][s] = o.idx
            o.sig = True
        return list(deps.values())

    def _mark(self, op, reads, writes):
        for b in reads:
            if b.excl:
                b.w = op
                b.r = {}
            else:
                b.r[op.stream] = op
        for b in writes:
            b.w = op
            b.r = {}
        self.last[op.stream] = op

    def op(self, eng, fn, reads=(), writes=()):
        deps = self._collect(eng, reads, writes)
        self.nidx[eng] += 1
        o = Op(eng, self.nidx[eng])
        self._mark(o, reads, writes)
        self.ops[eng].append((deps, fn, o))
        return o

    def dma(self, eng, chain, fn, reads=(), writes=()):
        if chain not in self.semobj:
            self.semobj[chain] = self.free_dma_sems.pop()
            self.nidx[chain] = 0
        deps = self._collect(eng, reads, writes)
        last = self.chain_last.get(chain)
        if last is not None and self.known[eng].get(chain, 0) < last.idx:
            deps = [d for d in deps if d.stream != chain] + [last]
            self.known[eng][chain] = last.idx
        self.nidx[chain] += 1
        o = Op(chain, self.nidx[chain])
        o.sig = True
        self.chain_last[chain] = o
        self._mark(o, reads, writes)
        self.ops[eng].append((deps, fn, o))
        return o

    def barrier(self, engines=ENGS):
        lasts = list(self.last.values())
        for e in engines:
            deps = {}
            for o in lasts:
                self._need(e, deps, o, True)
            for s, o in deps.items():
                self.known[e][s] = o.idx
                o.sig = True
            if deps:
                self.ops[e].append((list(deps.values()), None, None))

    def emit(self, block):
        for e in ENGS:
            c = 0
            for deps, fn, o in self.ops[e]:
                if o is None:
                    continue
                if o.stream == e:
                    if o.sig:
                        c += 1
                        o.count = c
                else:
                    o.count = 16 * o.idx
        for e in ENGS:
            ops = self.ops[e]
            if not ops:
                continue

            def body(engine, ops=ops, e=e):
                for deps, fn, o in ops:
                    for d in deps:
                        engine.wait_ge(self.semobj[d.stream], d.count)
                    if fn is None:
                        continue
                    ins = fn(engine)
                    if o.stream != e:
                        ins.then_inc(self.semobj[o.stream], 16)
                    elif o.sig:
                        ins.then_inc(self.semobj[e], 1)

            getattr(block, _ENG_ATTR[e])(body)


class Arena:
    def __init__(self, tensor, nbytes):
        self.t = tensor
        self.free = [(0, nbytes)]
        self.used = {}

    def alloc(self, name, shape, dt, top=False):
        esz = {F32: 4, BF16: 2, I32: 4}[dt]
        n = esz
        for s in shape[1:]:
            n *= s
        nb = (n + 63) // 64 * 64
        order = list(enumerate(self.free))
        if top:
            order = order[::-1]
        for i, (o, sz) in order:
            if sz >= nb:
                if sz == nb:
                    self.free.pop(i)
                elif top:
                    self.free[i] = (o, sz - nb)
                    o = o + sz - nb
                else:
                    self.free[i] = (o + nb, sz - nb)
                self.used[name] = (o, nb)
                v = self.t[0:shape[0], o:o + n].bitcast(dt)
                if len(shape) == 3:
                    v = v.rearrange("p (a b) -> p a b", a=shape[1])
                return v
        raise RuntimeError(f"arena full for {name} {shape} free={self.free}")

    def release(self, *names):
        for name in names:
            o, nb = self.used.pop(name)
            self.free.append((o, nb))
        self.free.sort()
        m = []
        for o, sz in self.free:
            if m and m[-1][0] + m[-1][1] == o:
                m[-1] = (m[-1][0], m[-1][1] + sz)
            else:
                m.append((o, sz))
        self.free = m


class Rot:
    def __init__(self, ids):
        self.ids = list(ids)
        self.i = 0

    def next(self):
        v = self.ids[self.i % len(self.ids)]
        self.i += 1
        return v


from contextlib import ExitStack
import math

SCALE = 1.0 / math.sqrt(192.0)
TWO_PI = 2.0 * math.pi


def build(stage=3):
    nc = bass.Bass("TRN2", target_bir_lowering=False)

    def din(name, shape, dt=F32):
        return nc.dram_tensor(name, list(shape), dt, kind="ExternalInput").ap()

    x_d = din("x", [T, D])
    pos_d = din("pos", [128, T], I32)
    cst_d = din("cst", [128, C_W])
    g_d = din("gstack", [G_ROWS, 128])
    W = {}
    for f in ("ffn1", "ffn2"):
        W[f + "_g"] = din(f + "_w_gate", [D, FF])
        W[f + "_u"] = din(f + "_w_up", [D, FF])
        W[f + "_d"] = din(f + "_w_down", [FF, D])
    w_in_d = din("w_in", [D, IN_W])
    w_uq_d = din("w_uq", [384, 1536])
    w_uk_d = din("w_uk", [256, 1024])
    w_uv_d = din("w_uv", [256, 1024])
    w_oa_d = din("w_o_attn", [1024, 1024])
    pw_d = din("pool_w", [4, 128, 128])
    w_op_d = din("w_o_pool", [512, 1024])
    w_out_d = din("w_out", [1024, 1024])
    gfb_d = din("gfb", [128, D])
    y_d = nc.dram_tensor("y", [T, D], F32, kind="ExternalOutput").ap()

    es = ExitStack()
    ARENA_BYTES = 206 * 1024
    arena_t = es.enter_context(nc.sbuf_tensor("arena", [128, ARENA_BYTES], U8))
    A = Arena(arena_t, ARENA_BYTES)
    banks = [es.enter_context(nc.psum_tensor(f"bk{i}", [128, 512], F32)) for i in range(8)]
    BK = [Buf(f"bk{i}", excl=True) for i in range(8)]
    eng_sems = [es.enter_context(nc.semaphore(f"s_{e}")) for e in ENGS]
    dma_sems = [es.enter_context(nc.semaphore(f"d{i}")) for i in range(48)]
    P = Prog(nc, eng_sems, dma_sems)
    rot = Rot(range(8))

    def kc_rows(ap2d):
        return ap2d.rearrange("(c p) n -> p c n", p=128)

    def tbs(tb):
        return slice(tb * TB, (tb + 1) * TB)

    def mm(bank, c0, n, pairs, reads):
        out = banks[bank][:, c0:c0 + n]

        def fn(e, pairs=pairs, out=out):
            k = len(pairs)
            for i, (l, r) in enumerate(pairs):
                ins = e.matmul(out, l, r, start=(i == 0), stop=(i == k - 1))
            return ins
        return P.op("pe", fn, reads=reads, writes=[BK[bank]])

    def act(out, in_, func, reads, writes, **kw):
        return P.op("act", lambda e: e.activation(out=out, in_=in_, func=func, **kw), reads=reads, writes=writes)

    def tt(out, in0, in1, op, reads, writes, eng="dve"):
        return P.op(eng, lambda e: e.tensor_tensor(out=out, in0=in0, in1=in1, op=op), reads=reads, writes=writes)

    def stt(out, in0, scalar, in1, op0, op1, reads, writes, eng="dve"):
        return P.op(eng, lambda e: e.scalar_tensor_tensor(out=out, in0=in0, scalar=scalar, in1=in1, op0=op0, op1=op1),
                    reads=reads, writes=writes)

    def ts(out, in0, s1, s2, op0, op1, reads, writes, eng="dve"):
        if s2 is None:
            return P.op(eng, lambda e: e.tensor_scalar(out=out, in0=in0, scalar1=s1, scalar2=None, op0=op0),
                        reads=reads, writes=writes)
        return P.op(eng, lambda e: e.tensor_scalar(out=out, in0=in0, scalar1=s1, scalar2=s2, op0=op0, op1=op1),
                    reads=reads, writes=writes)

    def cp(out, in_, reads, writes, eng="dve"):
        return P.op(eng, lambda e: e.tensor_copy(out=out, in_=in_), reads=reads, writes=writes)

    def dma(q, chain, out, in_, reads=(), writes=()):
        return P.dma(q, chain, lambda e: e.dma_start(out=out, in_=in_), reads=reads, writes=writes)

    xT = A.alloc("xT", [128, 8, T], F32)
    XB = [[Buf(f"x{c}_{tb}") for tb in range(NB)] for c in range(8)]
    cst = A.alloc("cst", [128, C_W], F32); cstB = Buf("cst")
    gT = A.alloc("gT", [128, G_ROWS], F32); gTB = Buf("gT")
    gh = A.alloc("gh", [128, G_ROWS], F32); ghB = Buf("gh")
    ones = A.alloc("ones", [128, 128], BF16); onesB = Buf("ones")
    epsb = A.alloc("epsb", [128, 1], F32); epsB = Buf("eps")
    sq = A.alloc("sq", [128, 8, TB], BF16); sqB = Buf("sq")
    rs = [A.alloc(f"rs{i}", [128, TB], F32) for i in range(2)]; rsB = [Buf(f"rs{i}") for i in range(2)]
    sg = [A.alloc(f"sg{i}", [128, TB], F32) for i in range(2)]; sgB = [Buf(f"sg{i}") for i in range(2)]
    ctr = {"rs": 0, "sg": 0}
    ident = cst[:, C_ID:C_ID + 128]

    def norm_rs(src3, nch, Dn, reads):
        act(sq[:, 0:nch, :], src3, AF.Square, reads, [sqB])
        b = rot.next()
        mm(b, 0, TB, [(ones, sq[:, c, :]) for c in range(nch)], [onesB, sqB])
        ri = ctr["rs"] % 2
        ctr["rs"] += 1
        act(rs[ri], banks[b][:, :], AF.Ln, [BK[b], epsB], [rsB[ri]], bias=epsb[:, 0:1], scale=1.0 / Dn)
        act(rs[ri], rs[ri], AF.Exp, [rsB[ri]], [rsB[ri]], scale=-0.5)
        return ri

    def next_sg():
        i = ctr["sg"] % 2
        ctr["sg"] += 1
        return i

    dma("sp", "c_cst", cst, cst_d[:, :], writes=[cstB])
    gsb = A.alloc("gsb", [128, 128], F32, top=True); gsbB = Buf("gsb")
    dma("sp", "c_g", gsb[0:G_ROWS, :], g_d[:, :], writes=[gsbB])
    P.op("dve", lambda e: e.memset(ones, 1.0), writes=[onesB])
    P.op("dve", lambda e: e.memset(epsb, EPS), writes=[epsB])
    P.op("pe", lambda e: e.transpose(banks[0][:, 0:G_ROWS], gsb[0:G_ROWS, :], cst[0:G_ROWS, 0:G_ROWS]),
         reads=[gsbB, cstB], writes=[BK[0]])
    act(gT, banks[0][:, 0:G_ROWS], AF.Copy, [BK[0]], [gTB])
    P.op("act", lambda e: e.mul(out=gh, in_=gT, mul=0.5), reads=[gTB], writes=[ghB])

    xin = [A.alloc(f"xin{i}", [128, D], F32, top=True) for i in range(6)]
    xinB = [Buf(f"xin{i}") for i in range(6)]
    for i in range(16):
        xi = i % 6
        dma("sp", f"c_xin{xi}", xin[xi], x_d[i * 128:(i + 1) * 128, :], writes=[xinB[xi]])
        for half in range(2):
            b = rot.next()

            def trf(e, xi=xi, half=half, b=b):
                for j in range(4):
                    c = half * 4 + j
                    ins = e.transpose(banks[b][:, j * 128:(j + 1) * 128], xin[xi][:, c * 128:(c + 1) * 128], ident)
                return ins
            P.op("pe", trf, reads=[xinB[xi], cstB], writes=[BK[b]])
            act(xT[:, half * 4:half * 4 + 4, i * 128:(i + 1) * 128],
                banks[b][:, :].rearrange("p (c t) -> p c t", c=4), AF.Copy,
                [BK[b]], [XB[half * 4 + j][i // 4] for j in range(4)])
    A.release("gsb", "xin0", "xin1", "xin2", "xin3", "xin4", "xin5")

    def ffn(f, go_pre, go_post):
        hT = A.alloc("hT", [128, 8, 1024], BF16)
        HB = [[Buf(f"h{c}_{t}") for t in range(2)] for c in range(8)]
        aT = A.alloc("aT", [128, NFF, 1024], BF16)
        AB = [[Buf(f"a{j}_{t}") for t in range(2)] for j in range(NFF)]
        f1T = A.alloc("f1T", [128, 8, 1024], F32, top=True)
        FB = [[Buf(f"f{c}_{t}") for t in range(2)] for c in range(8)]
        wg = [A.alloc(f"wg{i}", [128, 8, 256], BF16) for i in range(2)]; wgB = [Buf(f"wg{i}") for i in range(2)]
        wu = [A.alloc(f"wu{i}", [128, 8, 256], BF16) for i in range(2)]; wuB = [Buf(f"wu{i}") for i in range(2)]
        wdn = [A.alloc(f"wdn{i}", [128, NFF, 128], BF16) for i in range(2)]; wdB = [Buf(f"wdn{i}") for i in range(2)]
        Wg, Wu, Wd = W[f + "_g"], W[f + "_u"], W[f + "_d"]
        cnt = {"g": 0, "d": 0}
        bgq = []

        nst = {}

        def Nn_stats(hh, tl):
            tb = 2 * hh + tl
            nst[(hh, tl)] = norm_rs(xT[:, :, tbs(tb)], 8, D, [XB[c][tb] for c in range(8)])

        def Nn_apply(hh):
            for tl in range(2):
                tb = 2 * hh + tl
                ri = nst[(hh, tl)]
                for c in range(8):
                    stt(hT[:, c, tbs(tl)], xT[:, c, tbs(tb)], gT[:, go_pre + c:go_pre + c + 1], rs[ri],
                        ALU.mult, ALU.mult, [XB[c][tb], gTB, rsB[ri]], [HB[c][tl]])

        def Nn(hh):
            for tl in range(2):
                Nn_stats(hh, tl)
            Nn_apply(hh)

        def GU(hh):
            for grp in range(11):
                bi = cnt["g"] % 2
                cnt["g"] += 1
                dma("pool", f"c_wg{bi}", wg[bi], kc_rows(Wg[:, grp * 256:(grp + 1) * 256]), writes=[wgB[bi]])
                dma("pool", f"c_wu{bi}", wu[bi], kc_rows(Wu[:, grp * 256:(grp + 1) * 256]), writes=[wuB[bi]])
                for j2 in range(2):
                    j = grp * 2 + j2
                    for tl in range(2):
                        bg = rot.next()
                        bu = rot.next()
                        hr = [HB[kc][tl] for kc in range(8)]
                        mm(bg, 0, TB, [(wg[bi][:, kc, j2 * 128:(j2 + 1) * 128], hT[:, kc, tbs(tl)]) for kc in range(8)],
                           [wgB[bi]] + hr)
                        mm(bu, 0, TB, [(wu[bi][:, kc, j2 * 128:(j2 + 1) * 128], hT[:, kc, tbs(tl)]) for kc in range(8)],
                           [wuB[bi]] + hr)
                        si = next_sg()
                        act(sg[si], banks[bg][:, :], AF.Silu, [BK[bg]], [sgB[si]])
                        tt(aT[:, j, tbs(tl)], sg[si], banks[bu][:, :], ALU.mult, [sgB[si], BK[bu]], [AB[j][tl]])
                        if bgq:
                            bgq.pop(0)()
            while bgq:
                bgq.pop(0)()

        def Dd(hh):
            for dc in range(8):
                bi = cnt["d"] % 2
                cnt["d"] += 1
                dma("pool", f"c_wd{bi}", wdn[bi], kc_rows(Wd[:, dc * 128:(dc + 1) * 128]), writes=[wdB[bi]])
                for tl in range(2):
                    b = rot.next()
                    mm(b, 0, TB, [(wdn[bi][:, j, :], aT[:, j, tbs(tl)]) for j in range(NFF)],
                       [wdB[bi]] + [AB[j][tl] for j in range(NFF)])
                    act(f1T[:, dc, tbs(tl)], banks[b][:, :], AF.Copy, [BK[b]], [FB[dc][tl]])

        def Pp(hh):
            th = []
            for tl in range(2):
                tb = 2 * hh + tl
                st = {}

                def t_norm(tl=tl, st=st):
                    st["ri"] = norm_rs(f1T[:, :, tbs(tl)], 8, D, [FB[c][tl] for c in range(8)])
                th.append(t_norm)
                for c in range(8):
                    def t_c(tl=tl, tb=tb, c=c, st=st, hh=hh):
                        ri = st["ri"]
                        stt(f1T[:, c, tbs(tl)], f1T[:, c, tbs(tl)], gh[:, go_post + c:go_post + c + 1], rs[ri],
                            ALU.mult, ALU.mult, [FB[c][tl], ghB, rsB[ri]], [FB[c][tl]])
                        tt(xT[:, c, tbs(tb)], f1T[:, c, tbs(tl)], xT[:, c, tbs(tb)], ALU.add,
                           [FB[c][tl], XB[c][tb]], [XB[c][tb]], eng=("dve" if hh == 0 else "pool"))
                    th.append(t_c)
            return th

        Nn(0)
        bgq.extend([lambda: Nn_stats(1, 0), lambda: Nn_stats(1, 1)])
        GU(0)
        Nn_apply(1)
        Dd(0)
        bgq.extend(Pp(0))
        GU(1)
        Dd(1)
        for t in Pp(1):
            t()
        P.barrier()
        A.release("hT", "aT", "f1T", "wg0", "wg1", "wu0", "wu1", "wdn0", "wdn1")

    if stage >= 1:
        ffn("ffn1", GO["f1pre"], GO["f1post"])

    def make_uT(uT, UB, tblist=range(NB), off=0):
        for tb in tblist:
            ri = norm_rs(xT[:, :, tbs(tb)], 8, D, [XB[c][tb] for c in range(8)])
            for c in range(8):
                stt(uT[:, c, tbs(tb - off)], xT[:, c, tbs(tb)], gT[:, GO["mpre"] + c:GO["mpre"] + c + 1], rs[ri],
                    ALU.mult, ALU.mult, [XB[c][tb], gTB, rsB[ri]], [UB[c][tb - off]])

    def mixer():
        cosT = A.alloc("cosT", [128, T], F32, top=True); cosB = Buf("cos")
        sinT = A.alloc("sinT", [128, T], F32, top=True); sinB = Buf("sin")
        cqn = A.alloc("cqn", [128, 3, T], BF16, top=True)
        CQB = [[Buf(f"cq{c}_{t}") for t in range(NB)] for c in range(3)]
        ckvn = A.alloc("ckvn", [128, 2, T], BF16, top=True)
        CKB = [[Buf(f"ck{c}_{t}") for t in range(NB)] for c in range(2)]
        krT = A.alloc("krT", [128, T], BF16, top=True)
        KRB = [Buf(f"kr{t}") for t in range(NB)]
        posi = A.alloc("posi", [128, T], I32); posiB = Buf("posi")
        kf = posi.bitcast(F32)
        posf = A.alloc("posf", [128, T], F32); posfB = Buf("posf")
        ang = A.alloc("ang", [128, T], F32); angB = Buf("ang")
        uT = A.alloc("uT", [128, 8, T], BF16)
        UB = [[Buf(f"u{c}_{t}") for t in range(NB)] for c in range(8)]
        win = A.alloc("win", [128, 8, 768], BF16); winB = Buf("win")
        wkA = A.alloc("wkA", [128, 8, 128], BF16); wkAB = Buf("wkA")
        wkB = A.alloc("wkB", [128, 8, 128], BF16); wkBB = Buf("wkB")
        cqr = A.alloc("cqr", [128, 3, TB], F32); cqrB = [Buf(f"cqr{c}") for c in range(3)]
        ckr = A.alloc("ckr", [128, 2, TB], F32); ckrB = [Buf(f"ckr{c}") for c in range(2)]
        dma("pool", "c_win", win, kc_rows(w_in_d[:, 0:768]), writes=[winB])
        dma("sp", "c_pos", posi, pos_d[:, :], writes=[posiB])
        tabq = []
        tabq.append(lambda: cp(posf, posi, [posiB], [posfB]))
        tabq.append(lambda: ts(ang, posf, cst[:, C_FC:C_FC + 1], math.pi / 2, ALU.mult, ALU.add, [posfB, cstB], [angB]))
        tabq.append(lambda: ts(posf, posf, cst[:, C_FS:C_FS + 1], None, ALU.mult, None, [posfB, cstB], [posfB]))
        angs = []
        for ti in range(2):
            a = ang if ti == 0 else posf
            aB = angB if ti == 0 else posfB
            tabq.append(lambda a=a, aB=aB: ts(posi, a, 1.0 / TWO_PI, None, ALU.mult, None, [aB], [posiB]))
            tabq.append(lambda a=a, aB=aB: cp(kf, posi, [posiB], [posiB]))
            tabq.append(lambda a=a, aB=aB: stt(a, kf, -TWO_PI, a, ALU.mult, ALU.add, [posiB, aB], [aB]))
            tabq.append(lambda a=a, aB=aB: ts(a, a, 3.14159, -3.14159, ALU.min, ALU.max, [aB], [aB]))
            angs.append((a, aB))
        make_uT(uT, UB)
        for d0 in (0, 64):
            cp(wkA[:, :, d0:d0 + 64], win[:, :, 640:704], [winB], [wkAB])
            cp(wkB[:, :, d0:d0 + 32], win[:, :, 672:704], [winB], [wkBB])
            cp(wkB[:, :, d0 + 32:d0 + 64], win[:, :, 640:672], [winB], [wkBB])
        for tb in range(NB):
            ur = [UB[kc][tb] for kc in range(8)]
            for c in range(3):
                b = rot.next()
                mm(b, 0, TB, [(win[:, kc, c * 128:(c + 1) * 128], uT[:, kc, tbs(tb)]) for kc in range(8)], [winB] + ur)
                act(cqr[:, c, :], banks[b][:, :], AF.Copy, [BK[b]], [cqrB[c]])
                if tabq:
                    tabq.pop(0)()
            for c in range(2):
                b = rot.next()
                mm(b, 0, TB, [(win[:, kc, 384 + c * 128:384 + (c + 1) * 128], uT[:, kc, tbs(tb)]) for kc in range(8)],
                   [winB] + ur)
                act(ckr[:, c, :], banks[b][:, :], AF.Copy, [BK[b]], [ckrB[c]])
                if tabq:
                    tabq.pop(0)()
            ri = norm_rs(cqr[:, :, :], 3, 384, cqrB)
            for c in range(3):
                stt(cqn[:, c, tbs(tb)], cqr[:, c, :], gT[:, GO["gq"] + c:GO["gq"] + c + 1], rs[ri],
                    ALU.mult, ALU.mult, [cqrB[c], gTB, rsB[ri]], [CQB[c][tb]])
            ri = norm_rs(ckr[:, :, :], 2, 256, ckrB)
            for c in range(2):
                stt(ckvn[:, c, tbs(tb)], ckr[:, c, :], gT[:, GO["gkv"] + c:GO["gkv"] + c + 1], rs[ri],
                    ALU.mult, ALU.mult, [ckrB[c], gTB, rsB[ri]], [CKB[c][tb]])
        while tabq:
            tabq.pop(0)()
        act(cosT, angs[0][0], AF.Sin, [angs[0][1]], [cosB])
        act(sinT, angs[1][0], AF.Sin, [angs[1][1]], [sinB])
        for tb in range(NB):
            ur = [UB[kc][tb] for kc in range(8)]
            ba = rot.next()
            mm(ba, 0, TB, [(wkA[:, kc, :], uT[:, kc, tbs(tb)]) for kc in range(8)], [wkAB] + ur)
            bb = rot.next()
            mm(bb, 0, TB, [(wkB[:, kc, :], uT[:, kc, tbs(tb)]) for kc in range(8)], [wkBB] + ur)
            s0 = next_sg()
            tt(sg[s0], banks[ba][:, :], cosT[:, tbs(tb)], ALU.mult, [BK[ba], cosB], [sgB[s0]])
            s1 = next_sg()
            tt(sg[s1], banks[bb][:, :], sinT[:, tbs(tb)], ALU.mult, [BK[bb], sinB], [sgB[s1]])
            tt(krT[:, tbs(tb)], sg[s0], sg[s1], ALU.add, [sgB[s0], sgB[s1]], [KRB[tb]])
        P.barrier()
        A.release("uT", "posi", "posf", "ang", "win", "wkA", "wkB", "cqr", "ckr")

        wq = A.alloc("wq", [128, 3, 1536], BF16); wqB_ = Buf("wq")
        dma("pool", "c_wq", wq, kc_rows(w_uq_d[:, :]), writes=[wqB_])
        wqa = A.alloc("wqa", [128, 3, 512], BF16); wqaB = Buf("wqa")
        wqb = A.alloc("wqb", [128, 3, 512], BF16); wqbB = Buf("wqb")
        qrT = A.alloc("qrT", [128, 4, T], BF16, top=True)
        QRB = [[Buf(f"qr{j}_{t}") for t in range(NB)] for j in range(4)]
        for kc in range(3):
            src = wq[:, kc, :].rearrange("p (h d) -> p h d", h=8)
            cp(wqa[:, kc, :].rearrange("p (h d) -> p h d", h=8), src[:, :, 128:192], [wqB_], [wqaB])
            dstb = wqb[:, kc, :].rearrange("p (h d) -> p h d", h=8)
            cp(dstb[:, :, 0:32], src[:, :, 160:192], [wqB_], [wqbB])
            cp(dstb[:, :, 32:64], src[:, :, 128:160], [wqB_], [wqbB])
        for j in range(4):
            for tb in range(NB):
                cr = [CQB[kc][tb] for kc in range(3)]
                ba = rot.next()
                mm(ba, 0, TB, [(wqa[:, kc, j * 128:(j + 1) * 128], cqn[:, kc, tbs(tb)]) for kc in range(3)], [wqaB] + cr)
                bb = rot.next()
                mm(bb, 0, TB, [(wqb[:, kc, j * 128:(j + 1) * 128], cqn[:, kc, tbs(tb)]) for kc in range(3)], [wqbB] + cr)
                s0 = next_sg()
                tt(sg[s0], banks[ba][:, :], cosT[:, tbs(tb)], ALU.mult, [BK[ba], cosB], [sgB[s0]])
                s1 = next_sg()
                tt(sg[s1], banks[bb][:, :], sinT[:, tbs(tb)], ALU.mult, [BK[bb], sinB], [sgB[s1]])
                tt(qrT[:, j, tbs(tb)], sg[s0], sg[s1], ALU.add, [sgB[s0], sgB[s1]], [QRB[j][tb]])
        P.barrier()
        A.release("cosT", "sinT", "wqa", "wqb")

        attnT = A.alloc("attnT", [128, 8, T], BF16, top=True)
        ATB = [[Buf(f"at{h}_{t}") for t in range(NB)] for h in range(H)]
        wk = A.alloc("wk", [128, 2, 1024], BF16); wkB_ = Buf("wk")
        wv = A.alloc("wv", [128, 2, 1024], BF16); wvB_ = Buf("wv")
        dma("pool", "c_wk", wk, kc_rows(w_uk_d[:, :]), writes=[wkB_])
        dma("pool", "c_wv", wv, kc_rows(w_uv_d[:, :]), writes=[wvB_])
        qn = [A.alloc(f"qn{i}", [128, T], BF16) for i in range(2)]
        QNB = [[Buf(f"qn{i}_{t}") for t in range(NB)] for i in range(2)]
        kn = [A.alloc(f"kn{i}", [128, T], BF16) for i in range(2)]
        KNB = [[Buf(f"kn{i}_{t}") for t in range(NB)] for i in range(2)]
        Vh = [A.alloc(f"vh{i}", [128, 16, 128], BF16) for i in range(2)]
        VB = [[Buf(f"vh{i}_{g}") for g in range(4)] for i in range(2)]
        NPT = 6
        PT = [A.alloc(f"pt{i}", [128, TB], BF16) for i in range(NPT)]
        PTB = [Buf(f"pt{i}") for i in range(NPT)]
        accs = [[sg[0], sg[1]], [rs[0], rs[1]]]
        accB = [[Buf("accA0"), Buf("accB0")], [Buf("accA1"), Buf("accB1")]]
        accS = [sq[:, 0, :], sq[:, 1, :]]
        accSB = [Buf("accS0"), Buf("accS1")]
        rcp = [A.alloc(f"rcp{i}", [128, TB], F32) for i in range(2)]
        rcpB = [Buf(f"rcp{i}") for i in range(2)]
        strot = Rot([0, 1, 2, 3])

        def gen_thunks(h):
            hb = h % 2
            th = []
            for tb in range(NB):
                stq = {}

                def tq_a(tb=tb, st=stq):
                    st["b"] = b = strot.next()
                    mm(b, 0, TB, [(wq[:, kc, h * 192:h * 192 + 128], cqn[:, kc, tbs(tb)]) for kc in range(3)],
                       [wqB_] + [CQB[kc][tb] for kc in range(3)])

                def tq_b(tb=tb, st=stq):
                    b = st["b"]
                    cp(qn[hb][:, tbs(tb)], banks[b][:, :], [BK[b]], [QNB[hb][tb]])
                th.append((tq_a, tq_b))
                stk = {}

                def tk_a(tb=tb, st=stk):
                    st["b"] = b = strot.next()
                    mm(b, 0, TB, [(wk[:, kc, h * 128:(h + 1) * 128], ckvn[:, kc, tbs(tb)]) for kc in range(2)],
                       [wkB_] + [CKB[kc][tb] for kc in range(2)])

                def tk_b(tb=tb, st=stk):
                    b = st["b"]
                    cp(kn[hb][:, tbs(tb)], banks[b][:, :], [BK[b]], [KNB[hb][tb]])
                th.append((tk_a, tk_b))
                stv = {}

                def tv_a(g4=tb, st=stv):
                    st["b"] = b = strot.next()

                    def vfn(e, g4=g4, b=b):
                        for j in range(4):
                            tile = g4 * 4 + j
                            for kc in range(2):
                                ins = e.matmul(banks[b][:, j * 128:(j + 1) * 128],
                                               ckvn[:, kc, tile * 128:(tile + 1) * 128],
                                               wv[:, kc, h * 128:(h + 1) * 128], start=(kc == 0), stop=(kc == 1))
                        return ins
                    P.op("pe", vfn, reads=[wvB_] + [CKB[kc][g4] for kc in range(2)], writes=[BK[b]])

                def tv_b(g4=tb, st=stv):
                    b = st["b"]
                    cp(Vh[hb][:, g4 * 4:(g4 + 1) * 4, :], banks[b][:, :].rearrange("p (j d) -> p j d", j=4),
                       [BK[b]], [VB[hb][g4]])
                th.append((tv_a, tv_b))
            return th

        steps = [(h, qb, kc) for h in range(H) for qb in range(NB) for kc in range(16)]
        stbank = {}
        ptc = 0

        def emit_st(s):
            h, qb, kc = steps[s]
            hb = h % 2
            po = 64 * (h % 2)
            b = strot.next()
            stbank[s] = b
            mm(b, 0, TB, [(kn[hb][:, kc * 128:(kc + 1) * 128], qn[hb][:, tbs(qb)]),
                          (krT[po:po + 64, kc * 128:(kc + 1) * 128], qrT[po:po + 64, h // 2, tbs(qb)])],
               [KNB[hb][kc // 4], QNB[hb][qb], KRB[kc // 4], QRB[h // 2][qb]])

        def fin_a(grp):
            g2 = grp % 2
            tt(accS[g2], accs[g2][0], accs[g2][1], ALU.add, accB[g2], [accSB[g2]])
            mm(6 + g2, 0, TB, [(ones, accS[g2])], [onesB, accSB[g2]])

        def fin_b(grp):
            g2 = grp % 2
            act(rcp[g2], banks[6 + g2][:, :], AF.Ln, [BK[6 + g2]], [rcpB[g2]])
            act(rcp[g2], rcp[g2], AF.Exp, [rcpB[g2]], [rcpB[g2]], scale=-1.0)

        def fin_c(grp):
            h, qb = grp // NB, grp % NB
            g2 = grp % 2
            tt(attnT[:, h, tbs(qb)], banks[4 + g2][:, :], rcp[g2], ALU.mult, [BK[4 + g2], rcpB[g2]], [ATB[h][qb]])

        for (ta, tb_) in gen_thunks(0):
            ta()
            tb_()
        genq = []
        pend = {}
        emit_st(0)
        emit_st(1)
        for s, (h, qb, kc) in enumerate(steps):
            hb = h % 2
            grp = h * NB + qb
            g2 = grp % 2
            ob = 4 + g2
            if s + 2 < len(steps):
                emit_st(s + 2)
            pi = s % NPT
            b = stbank.pop(s)
            act(PT[pi], banks[b][:, :], AF.Exp, [BK[b]], [PTB[pi]], scale=SCALE)
            P.op("pe", lambda e, hb=hb, kc=kc, pi=pi, ob=ob: e.matmul(
                banks[ob][:, :], Vh[hb][:, kc, :], PT[pi], start=(kc == 0), stop=(kc == 15)),
                reads=[VB[hb][kc // 4], PTB[pi]], writes=[BK[ob]])
            ai = 1 if kc % 3 == 2 else 0
            ae = "pool" if ai else "dve"
            acc = accs[g2][ai]
            aB = accB[g2][ai]
            if kc == 0 or kc == 2:
                cp(acc, PT[pi], [PTB[pi]], [aB], eng=ae)
            else:
                tt(acc, acc, PT[pi], ALU.add, [aB, PTB[pi]], [aB], eng=ae)
            if grp > 0:
                if kc == 3:
                    fin_a(grp - 1)
                elif kc == 7:
                    fin_b(grp - 1)
                elif kc == 11:
                    fin_c(grp - 1)
            if qb == 0 and kc == 4 and h + 1 < H:
                genq.extend(gen_thunks(h + 1))
            if s in pend:
                pend.pop(s)()
            if s % 4 == 2 and genq:
                ta, tb_ = genq.pop(0)
                ta()
                pend[s + 2] = tb_
        last = H * NB - 1
        fin_a(last)
        fin_b(last)
        fin_c(last)
        P.barrier()
        A.release("cqn", "ckvn", "krT", "wq", "qrT", "wk", "wv", "qn0", "qn1", "kn0", "kn1", "vh0", "vh1",
                  "pt0", "pt1", "pt2", "pt3", "pt4", "pt5", "rcp0", "rcp1")

        uT = A.alloc("uT", [128, 8, T], BF16)
        UB = [[Buf(f"u{c}_{t}") for t in range(NB)] for c in range(8)]
        ypT = A.alloc("ypT", [128, 4, T], BF16, top=True)
        YPB = [[Buf(f"yp{g}_{t}") for t in range(NB)] for g in range(4)]
        wpl = A.alloc("wpl", [128, 8, 512], BF16); wplB = Buf("wpl")
        dma("pool", "c_wpl", wpl, kc_rows(w_in_d[:, 704:1216]), writes=[wplB])
        pwt = A.alloc("pwt", [128, 4, 128], BF16); pwtB = Buf("pwt")
        dma("pool", "c_pwt", pwt, pw_d.rearrange("g c d -> c g d"), writes=[pwtB])
        XP = A.alloc("XP", [128, T + 16], F32); XPB = Buf("XP")
        sA = A.alloc("sA", [128, T + 16], F32); sAB = Buf("sA")
        sBt = A.alloc("sBt", [128, T + 16], F32); sBB = Buf("sBt")
        dT = A.alloc("dT", [128, T], BF16); dTB = Buf("dT")
        te = A.alloc("te", [128, 8], F32); teB = Buf("te")
        make_uT(uT, UB)
        P.op("dve", lambda e: e.memset(XP, 0.0), writes=[XPB])
        L = T + 16
        for g, w in enumerate(WINS):
            for tb in range(NB):
                b = rot.next()
                mm(b, 0, TB, [(wpl[:, kc, g * 128:(g + 1) * 128], uT[:, kc, tbs(tb)]) for kc in range(8)],
                   [wplB] + [UB[kc][tb] for kc in range(8)])
                cp(XP[:, 8 + tb * TB:8 + (tb + 1) * TB], banks[b][:, :], [BK[b]], [XPB])
            tt(sA[:, 0:L - 1], XP[:, 0:L - 1], XP[:, 1:L], ALU.add, [XPB], [sAB])
            tot, totB, off = sA, sAB, 7
            if w >= 4:
                tt(sBt[:, 0:L - 3], sA[:, 0:L - 3], sA[:, 2:L - 1], ALU.add, [sAB], [sBB])
                tot, totB, off = sBt, sBB, 6
            if w >= 8:
                tt(sA[:, 0:L - 7], sBt[:, 0:L - 7], sBt[:, 4:L - 3], ALU.add, [sBB], [sAB])
                tot, totB, off = sA, sAB, 4
            if w >= 16:
                tt(sBt[:, 0:L - 15], sA[:, 0:L - 15], sA[:, 8:L - 7], ALU.add, [sAB], [sBB])
                tot, totB, off = sBt, sBB, 0
            stt(dT, tot[:, off:off + T], 1.0 / w, XP[:, 8:8 + T], ALU.mult, ALU.subtract, [totB, XPB], [dTB])
            for (t0, c0) in ((0, 0), (T - 8, 8)):
                ec = C_EDGE + g * 16 + c0
                tt(te, tot[:, off + t0:off + t0 + 8], cst[:, ec:ec + 8], ALU.mult, [totB, cstB], [teB])
                tt(dT[:, t0:t0 + 8], te, XP[:, 8 + t0:16 + t0], ALU.subtract, [teB, XPB], [dTB])
            for tb in range(NB):
                b = rot.next()
                mm(b, 0, TB, [(pwt[:, g, :], dT[:, tbs(tb)])], [pwtB, dTB])
                act(ypT[:, g, tbs(tb)], banks[b][:, :], AF.Copy, [BK[b]], [YPB[g][tb]],
                    scale=gT[:, GO["psc"] + g:GO["psc"] + g + 1])
        P.barrier()
        A.release("uT", "wpl", "pwt", "XP", "sA", "sBt", "dT", "te")

        mixin = A.alloc("mixin", [128, 8, T], BF16)
        MXB = [[Buf(f"mx{c}_{t}") for t in range(NB)] for c in range(8)]
        woa = [A.alloc(f"woa{i}", [128, 8, 128], BF16) for i in range(2)]; woaB = [Buf(f"woa{i}") for i in range(2)]
        wop = [A.alloc(f"wop{i}", [128, 4, 128], BF16) for i in range(2)]; wopB = [Buf(f"wop{i}") for i in range(2)]
        wga = [A.alloc(f"wga{i}", [128, 8, 128], BF16) for i in range(2)]; wgaB = [Buf(f"wga{i}") for i in range(2)]
        wgp = [A.alloc(f"wgp{i}", [128, 8, 128], BF16) for i in range(2)]; wgpB = [Buf(f"wgp{i}") for i in range(2)]
        sgm = [A.alloc(f"sgm{i}", [128, TB], F32) for i in range(4)]; sgmB = [Buf(f"sgm{i}") for i in range(4)]
        uTh = A.alloc("uTh", [128, 8, 1024], BF16)
        UBh = [[Buf(f"uh{c}_{t}") for t in range(2)] for c in range(8)]
        sc = 0
        wc = 0
        for hh in range(2):
            make_uT(uTh, UBh, [2 * hh, 2 * hh + 1], 2 * hh)
            for dc in range(8):
                bi = wc % 2
                wc += 1
                cs = slice(dc * 128, (dc + 1) * 128)
                dma("pool", f"c_woa{bi}", woa[bi], kc_rows(w_oa_d[:, cs]), writes=[woaB[bi]])
                dma("pool", f"c_wop{bi}", wop[bi], kc_rows(w_op_d[:, cs]), writes=[wopB[bi]])
                dma("pool", f"c_wga{bi}", wga[bi], kc_rows(w_in_d[:, 1216 + dc * 128:1216 + (dc + 1) * 128]), writes=[wgaB[bi]])
                dma("pool", f"c_wgp{bi}", wgp[bi], kc_rows(w_in_d[:, 2240 + dc * 128:2240 + (dc + 1) * 128]), writes=[wgpB[bi]])
                for tl in range(2):
                    tb = 2 * hh + tl
                    ur = [UBh[kc][tl] for kc in range(8)]
                    bya = rot.next()
                    mm(bya, 0, TB, [(woa[bi][:, h, :], attnT[:, h, tbs(tb)]) for h in range(H)],
                       [woaB[bi]] + [ATB[h][tb] for h in range(H)])
                    bga = rot.next()
                    mm(bga, 0, TB, [(wga[bi][:, kc, :], uTh[:, kc, tbs(tl)]) for kc in range(8)], [wgaB[bi]] + ur)
                    byp = rot.next()
                    mm(byp, 0, TB, [(wop[bi][:, g, :], ypT[:, g, tbs(tb)]) for g in range(4)],
                       [wopB[bi]] + [YPB[g][tb] for g in range(4)])
                    bgp = rot.next()
                    mm(bgp, 0, TB, [(wgp[bi][:, kc, :], uTh[:, kc, tbs(tl)]) for kc in range(8)], [wgpB[bi]] + ur)
                    i0 = sc % 4
                    i1 = (sc + 1) % 4
                    sc += 2
                    act(sgm[i0], banks[bga][:, :], AF.Sigmoid, [BK[bga]], [sgmB[i0]])
                    act(sgm[i1], banks[bgp][:, :], AF.Sigmoid, [BK[bgp]], [sgmB[i1]])
                    tt(sgm[i0], sgm[i0], banks[bya][:, :], ALU.mult, [sgmB[i0], BK[bya]], [sgmB[i0]])
                    tt(sgm[i1], sgm[i1], banks[byp][:, :], ALU.mult, [sgmB[i1], BK[byp]], [sgmB[i1]])
                    tt(mixin[:, dc, tbs(tb)], sgm[i0], sgm[i1], ALU.add, [sgmB[i0], sgmB[i1]], [MXB[dc][tb]])
        P.barrier()
        A.release("uTh", "attnT", "ypT", "woa0", "woa1", "wop0", "wop1", "wga0", "wga1", "wgp0", "wgp1",
                  "sgm0", "sgm1", "sgm2", "sgm3")

        wo = A.alloc("wo", [128, 8, 1024], BF16); woB = Buf("wo")
        dma("pool", "c_wo", wo, kc_rows(w_out_d[:, :]), writes=[woB])
        mTs = [A.alloc(f"mT{i}", [128, 8, TB], F32) for i in range(2)]
        mTBs = [[Buf(f"mT{i}_{c}") for c in range(8)] for i in range(2)]
        for tb in range(NB):
            mT = mTs[tb % 2]
            mTB = mTBs[tb % 2]
            for dc in range(8):
                b = rot.next()
                mm(b, 0, TB, [(wo[:, kc, dc * 128:(dc + 1) * 128], mixin[:, kc, tbs(tb)]) for kc in range(8)],
                   [woB] + [MXB[kc][tb] for kc in range(8)])
                act(mT[:, dc, :], banks[b][:, :], AF.Copy, [BK[b]], [mTB[dc]])
            ri = norm_rs(mT[:, :, :], 8, D, mTB)
            for c in range(8):
                stt(mT[:, c, :], mT[:, c, :], gT[:, GO["mpost"] + c:GO["mpost"] + c + 1], rs[ri],
                    ALU.mult, ALU.mult, [mTB[c], gTB, rsB[ri]], [mTB[c]])
                tt(xT[:, c, tbs(tb)], mT[:, c, :], xT[:, c, tbs(tb)], ALU.add, [mTB[c], XB[c][tb]], [XB[c][tb]],
                   eng="pool")
        P.barrier()
        A.release("mixin", "wo", "mT0", "mT1")

    if stage >= 2:
        mixer()
    if stage >= 3:
        ffn("ffn2", GO["f2pre"], GO["f2post"])

    NYO = 6
    yo = [A.alloc(f"yo{i}", [128, D], F32) for i in range(NYO)]
    yoB = [[Buf(f"yo{i}_{hf}") for hf in range(2)] for i in range(NYO)]
    gfb = A.alloc("gfb", [128, D], F32); gfbB = Buf("gfb")
    dma("sp", "c_gfb", gfb, gfb_d[:, :], writes=[gfbB])
    rcol = [A.alloc(f"rcol{i}", [128, 1], F32) for i in range(4)]
    rcolB = [Buf(f"rcol{i}") for i in range(4)]
    sqsB = [Buf(f"sqs{i}") for i in range(4)]
    for i in range(16):
        tb = i // 4
        sl = i % 4
        yi = i % NYO
        tsl = slice(i * 128, (i + 1) * 128)
        xr = [XB[c][tb] for c in range(8)]
        sqs = sq[:, :, sl * 128:(sl + 1) * 128]
        act(sqs, xT[:, :, tsl], AF.Square, xr, [sqsB[sl]])
        bs = rot.next()

        def stf(e, sqs=sqs, bs=bs):
            for c in range(8):
                ins = e.matmul(banks[bs][:, 0:1], sqs[:, c, :], ones[:, 0:1], start=(c == 0), stop=(c == 7))
            return ins
        P.op("pe", stf, reads=[sqsB[sl], onesB], writes=[BK[bs]])
        act(rcol[sl], banks[bs][:, 0:1], AF.Ln, [BK[bs], epsB], [rcolB[sl]], bias=epsb[:, 0:1], scale=1.0 / D)
        act(rcol[sl], rcol[sl], AF.Exp, [rcolB[sl]], [rcolB[sl]], scale=-0.5)
        for half in range(2):
            b = rot.next()

            def trf(e, half=half, b=b, tsl=tsl):
                for j in range(4):
                    c = half * 4 + j
                    ins = e.transpose(banks[b][:, j * 128:(j + 1) * 128], xT[:, c, tsl], ident)
                return ins
            P.op("pe", trf, reads=[XB[half * 4 + j][tb] for j in range(4)] + [cstB], writes=[BK[b]])
            hs = slice(half * 512, (half + 1) * 512)
            stt(yo[yi][:, hs], banks[b][:, :], rcol[sl][:, 0:1], gfb[:, hs], ALU.mult, ALU.mult,
                [BK[b], rcolB[sl], gfbB], [yoB[yi][half]])
        dma("sp", f"c_yo{yi}", y_d[tsl, :], yo[yi], reads=yoB[yi])
    P.barrier(engines=("sp",))
    with nc.Block() as block:
        P.emit(block)
    es.close()
    return nc


_CACHE = {}


def _consts():
    c = np.zeros((128, C_W), np.float32)
    c[:, C_ID:C_ID + 128] = np.eye(128, dtype=np.float32)
    inv = (np.float32(10000.0) ** (-np.arange(0, 64, 2, dtype=np.float32) / np.float32(64))).astype(np.float32)
    for p in range(128):
        j = p % 32
        c[p, C_FC] = inv[j]
        c[p, C_FS] = -inv[j] if (p % 64) < 32 else inv[j]
    c[:, C_HPI] = np.pi / 2
    for g, w in enumerate(WINS):
        left = w // 2
        right = w - 1 - left
        for j in range(8):
            for (t, col) in ((j, j), (T - 8 + j, 8 + j)):
                lo = max(t - left, 0)
                hi = min(t + right + 1, T)
                c[:, C_EDGE + g * 16 + col] = 1.0 / (hi - lo)
    return c


def kernel(**inputs):
    stage = int(inputs.pop("_stage", 3)) if "_stage" in inputs else 3
    if stage not in _CACHE:
        _CACHE[stage] = build(stage)
    nc = _CACHE[stage]
    f = lambda k: np.ascontiguousarray(np.asarray(inputs[k], dtype=np.float32)[0])
    x = np.asarray(inputs["x"], dtype=np.float32)
    pos = np.asarray(inputs["positions"]).astype(np.int32)
    gnames = ["ffn1_pre_g", "ffn1_post_g", "mix_pre_g", "q_a_norm_g", "kv_a_norm_g", "pool_scale",
              "mix_post_g", "ffn2_pre_g", "ffn2_post_g", "final_g"]
    gstack = np.ascontiguousarray(np.concatenate([f(k).reshape(-1, 128) for k in gnames], axis=0))
    shared = {
        "cst": _consts(), "gstack": gstack,
        "gfb": np.ascontiguousarray(np.broadcast_to(f("final_g")[None, :], (128, D))),
        "ffn1_w_gate": f("ffn1_w_gate"), "ffn1_w_up": f("ffn1_w_up"), "ffn1_w_down": f("ffn1_w_down"),
        "ffn2_w_gate": f("ffn2_w_gate"), "ffn2_w_up": f("ffn2_w_up"), "ffn2_w_down": f("ffn2_w_down"),
        "w_in": f("w_in"), "w_uq": f("w_uq"), "w_uk": f("w_uk"), "w_uv": f("w_uv"), "w_o_attn": f("w_o_attn"),
        "pool_w": f("pool_w"), "w_o_pool": f("w_o_pool"), "w_out": f("w_out"),
    }
    in_maps = []
    for b in range(8):
        m = dict(shared)
        m["x"] = np.ascontiguousarray(x[b])
        m["pos"] = np.ascontiguousarray(np.broadcast_to(pos[b][None, :], (128, T)))
        in_maps.append(m)
    res = run_bass_kernel_spmd(nc, in_maps, core_ids=list(range(8)))
    return np.stack([np.asarray(r["y"], dtype=np.float32) for r in res.results], axis=0)
```
